# Optimizing a Trainium2 kernel written in Bass

```python
import jax, jax.numpy as jnp
from jax import lax
import numpy as np

D_MODEL = 1024
BATCH = 32
SEQ = 2048
DEPTH = 4

D_RG = D_MODEL
RG_BLOCK_W = 256
RG_BLOCKS = D_RG // RG_BLOCK_W
CONV_W = 4
RG_C = 8.0
HG_EXPAND = 128
HG_HEADS = D_MODEL // HG_EXPAND
HG_DK = HG_EXPAND
HG_DV = D_MODEL // HG_HEADS
HG_CHUNK = 32
F_MIN = 1e-30
D_FF = 4 * D_MODEL
NORM_EPS = 1e-6
SPLIT_SIZES = (D_RG, D_RG, HG_HEADS * HG_DK, HG_HEADS * HG_DK, HG_HEADS * HG_DV, HG_HEADS * HG_DV, D_MODEL, D_MODEL)
D_IN = sum(SPLIT_SIZES)
SPLIT_POINTS = tuple(np.cumsum(SPLIT_SIZES)[:-1].tolist())

kernel_name = 'hybrid_rglru_hgrn2_gated_trunk'


def rms_norm(x, gain):
    xf = x.astype(jnp.float32)
    y = xf * lax.rsqrt(jnp.mean(xf * xf, axis=-1, keepdims=True) + NORM_EPS)
    return (y * gain.astype(jnp.float32)).astype(x.dtype)


def causal_depthwise_conv(x, w, b):
    y = lax.conv_general_dilated(
        x, w[:, None, :].astype(x.dtype), window_strides=(1,),
        padding=((CONV_W - 1, 0),), dimension_numbers=('NWC', 'WIO', 'NWC'),
        feature_group_count=x.shape[-1])
    return y + b.astype(x.dtype)


def rg_lru(x, w_r, b_r, w_i, b_i, lam):
    B_, S_, _ = x.shape
    xf = x.astype(jnp.float32)
    xb = xf.reshape(B_, S_, RG_BLOCKS, RG_BLOCK_W)
    r = jax.nn.sigmoid(jnp.einsum('bsnj,njk->bsnk', xb, w_r.astype(jnp.float32)).reshape(B_, S_, D_RG) + b_r.astype(jnp.float32))
    i = jax.nn.sigmoid(jnp.einsum('bsnj,njk->bsnk', xb, w_i.astype(jnp.float32)).reshape(B_, S_, D_RG) + b_i.astype(jnp.float32))
    log_a = -RG_C * r * jax.nn.softplus(-lam.astype(jnp.float32))
    a = jnp.exp(log_a)
    u = jnp.sqrt(jnp.maximum(-jnp.expm1(2.0 * log_a), 0.0)) * (i * xf)

    def combine(left, right):
        a1, b1 = left
        a2, b2 = right
        return a1 * a2, a2 * b1 + b2

    _, h = lax.associative_scan(combine, (a, u), axis=1)
    return h.astype(x.dtype)


def hgrn2_chunkwise(q, k, log_f, v):
    B_, S_, H, DK = q.shape
    DV = v.shape[-1]
    n_chunks = S_ // HG_CHUNK

    def to_chunks(t):
        return t.reshape(B_, n_chunks, HG_CHUNK, H, t.shape[-1]).transpose(1, 0, 3, 2, 4)

    causal = jnp.tril(jnp.ones((HG_CHUNK, HG_CHUNK), dtype=bool))

    def step(state, chunk):
        qc, kc, lfc, vc = chunk
        cum = jnp.cumsum(lfc, axis=2)
        o_inter = jnp.einsum('bhtk,bhkv->bhtv', qc * jnp.exp(cum), state)
        rel = cum[:, :, :, None, :] - cum[:, :, None, :, :]
        decay = jnp.where(causal[:, :, None], jnp.exp(jnp.minimum(rel, 0.0)), 0.0)
        scores = jnp.einsum('bhtsk,bhsk->bhts', qc[:, :, :, None, :] * decay, kc)
        o_intra = jnp.einsum('bhts,bhsv->bhtv', scores, vc)
        last = cum[:, :, -1, :]
        k_to_end = kc * jnp.exp(jnp.minimum(last[:, :, None, :] - cum, 0.0))
        state = jnp.exp(last)[..., None] * state + jnp.einsum('bhsk,bhsv->bhkv', k_to_end, vc)
        return state, o_inter + o_intra

    state0 = jnp.zeros((B_, H, DK, DV), jnp.float32)
    _, o = lax.scan(step, state0, (to_chunks(q), to_chunks(k), to_chunks(log_f), to_chunks(v)))
    return o.transpose(1, 0, 3, 2, 4).reshape(B_, S_, H, DV)


def hybrid_mixer(h, lower_bound, w_in, conv_w, conv_b, w_r, b_r, w_i, b_i, lam, hg_norm, w_out):
    B_, S_, _ = h.shape
    proj = h @ w_in
    xa, ga, q, f, v, g, m_a, m_b = jnp.split(proj, SPLIT_POINTS, axis=-1)
    xa = causal_depthwise_conv(xa, conv_w, conv_b)
    y_a = rg_lru(xa, w_r, b_r, w_i, b_i, lam) * jax.nn.gelu(ga)
    qf = jax.nn.silu(q.astype(jnp.float32)).reshape(B_, S_, HG_HEADS, HG_DK)
    zf = f.astype(jnp.float32).reshape(B_, S_, HG_HEADS, HG_DK)
    lb = lower_bound.reshape(HG_HEADS, HG_DK)
    sig = jax.nn.sigmoid(zf)
    f_gate = lb + (1.0 - lb) * sig
    log_f = jnp.log(jnp.maximum(f_gate, F_MIN))
    kf = (1.0 - lb) * jax.nn.sigmoid(-zf)
    vf = v.astype(jnp.float32).reshape(B_, S_, HG_HEADS, HG_DV)
    o = rms_norm(hgrn2_chunkwise(qf, kf, log_f, vf), hg_norm)
    y_b = o.reshape(B_, S_, HG_HEADS * HG_DV).astype(h.dtype) * jax.nn.silu(g)
    y = jax.nn.sigmoid(m_a) * y_a + jax.nn.sigmoid(m_b) * y_b
    return y @ w_out


def setup_inputs(seed: int = 0) -> dict:
    key = jax.random.key(seed)
    ks = jax.random.split(key, 20)
    f32 = jnp.float32
    nrm = lambda k, shape, scale: jax.random.normal(k, shape, f32) * scale
    a_c = jax.random.uniform(ks[10], (DEPTH, D_RG), f32, minval=0.9, maxval=0.999)
    a0 = a_c ** (1.0 / RG_C)
    lam = jnp.log(a0) - jnp.log1p(-a0)
    return {
        'x': jax.random.normal(ks[0], (BATCH, SEQ, D_MODEL), f32),
        'lb_logits': nrm(ks[1], (DEPTH, HG_HEADS * HG_DK), 0.1),
        'norm_mix': 1.0 + nrm(ks[2], (DEPTH, D_MODEL), 0.02),
        'w_in': nrm(ks[3], (DEPTH, D_MODEL, D_IN), D_MODEL ** -0.5),
        'conv_w': nrm(ks[4], (DEPTH, CONV_W, D_RG), CONV_W ** -0.5),
        'conv_b': nrm(ks[5], (DEPTH, D_RG), 0.02),
        'w_r': nrm(ks[6], (DEPTH, RG_BLOCKS, RG_BLOCK_W, RG_BLOCK_W), RG_BLOCK_W ** -0.5),
        'b_r': nrm(ks[7], (DEPTH, D_RG), 0.02),
        'w_i': nrm(ks[8], (DEPTH, RG_BLOCKS, RG_BLOCK_W, RG_BLOCK_W), RG_BLOCK_W ** -0.5),
        'b_i': nrm(ks[9], (DEPTH, D_RG), 0.02),
        'lam': lam,
        'hg_norm': 1.0 + nrm(ks[11], (DEPTH, HG_DV), 0.02),
        'w_out': nrm(ks[12], (DEPTH, D_MODEL, D_MODEL), D_MODEL ** -0.5),
        'norm_mlp': 1.0 + nrm(ks[13], (DEPTH, D_MODEL), 0.02),
        'w_up': nrm(ks[14], (DEPTH, D_MODEL, D_FF), D_MODEL ** -0.5),
        'w_down': nrm(ks[15], (DEPTH, D_FF, D_MODEL), D_FF ** -0.5),
        'norm_final': 1.0 + nrm(ks[16], (D_MODEL,), 0.02),
    }


def reference(x, lb_logits, norm_mix, w_in, conv_w, conv_b, w_r, b_r, w_i, b_i, lam, hg_norm, w_out, norm_mlp, w_up, w_down, norm_final):
    sm = jax.nn.softmax(lb_logits.astype(jnp.float32), axis=0)
    lower_bounds = jnp.clip(jnp.cumsum(sm, axis=0) - sm[0], 0.0, 1.0)
    for l in range(DEPTH):
        h = rms_norm(x, norm_mix[l])
        x = x + hybrid_mixer(h, lower_bounds[l], w_in[l], conv_w[l], conv_b[l], w_r[l], b_r[l],
                             w_i[l], b_i[l], lam[l], hg_norm[l], w_out[l])
        h = rms_norm(x, norm_mlp[l])
        x = x + jnp.square(jax.nn.relu(h @ w_up[l])) @ w_down[l]
    return rms_norm(x, norm_final)
```

```python
import contextlib
import numpy as np
import concourse.bass as bass
import concourse.mybir as mybir
from concourse.bass_utils import run_bass_kernel_spmd

F32 = mybir.dt.float32
BF16 = mybir.dt.bfloat16
ACT = mybir.ActivationFunctionType
ALU = mybir.AluOpType

D = 1024
KC = 8
DIN = 8192
DFF = 4096
TT = 512
NCH = TT // 32
EPS = 1e-6
RG_C = 8.0
ENGS = ("pe", "act", "dve", "pool", "sp")


class _Op:
    __slots__ = ("eng", "emit", "deps", "needs_inc", "semval", "dsem", "dval")

    def __init__(self, eng, emit):
        self.eng = eng
        self.emit = emit
        self.deps = set()
        self.needs_inc = False
        self.semval = 0
        self.dsem = None
        self.dval = 0


class TK:
    def __init__(self):
        self.ops = []
        self.last_w = {}
        self.readers = {}
        self.dma_cnt = {}

    def add(self, eng, emit, reads=(), writes=(), dma=None, ndma=1):
        idx = len(self.ops)
        op = _Op(eng, emit)
        deps = op.deps
        lw = self.last_w
        rd = self.readers
        for t in reads:
            w = lw.get(t)
            if w is not None:
                deps.add((w, 0))
        for t in writes:
            w = lw.get(t)
            if w is not None:
                deps.add((w, 1))
            for r in rd.get(t, ()):
                deps.add((r, 2))
        for t in reads:
            rd.setdefault(t, []).append(idx)
        for t in writes:
            lw[t] = idx
            rd[t] = []
        if dma is not None:
            op.dsem = dma
            c = self.dma_cnt.get(dma.num, 0) + 16 * ndma
            self.dma_cnt[dma.num] = c
            op.dval = c
        self.ops.append(op)
        return idx

    def finish(self, block, sems):
        ops = self.ops
        real = []
        for i, op in enumerate(ops):
            rl = []
            for (d, kind) in op.deps:
                if d == i:
                    continue
                p = ops[d]
                if p.dsem is None and op.dsem is None and p.eng == op.eng:
                    if op.eng == "pe" or kind != 0:
                        continue
                rl.append(d)
                if p.dsem is None:
                    p.needs_inc = True
            real.append(rl)
        cnt = {e: 0 for e in ENGS}
        for op in ops:
            if op.dsem is None and op.needs_inc:
                cnt[op.eng] += 1
                op.semval = cnt[op.eng]
        per_eng = {e: [] for e in ENGS}
        for i, op in enumerate(ops):
            per_eng[op.eng].append(i)
        self.stats = {e: len(v) for e, v in per_eng.items()}
        self.stats["sem_max"] = dict(cnt)

        def run(eng_name, eh):
            waited = {}
            for i in per_eng[eng_name]:
                op = ops[i]
                waits = {}
                for d in real[i]:
                    p = ops[d]
                    if p.dsem is not None:
                        s, v = p.dsem, p.dval
                    else:
                        s, v = sems[p.eng], p.semval
                    if waited.get(s.num, 0) < v and waits.get(s.num, (None, 0))[1] < v:
                        waits[s.num] = (s, v)
                for sn, (s, v) in waits.items():
                    eh.wait_ge(s, v)
                    waited[sn] = v
                ins = op.emit(eh)
                if op.dsem is None and op.needs_inc:
                    ins.then_inc(sems[eng_name], 1)

        @block.tensor
        def _(e):
            run("pe", e)

        @block.scalar
        def _(e):
            run("act", e)

        @block.vector
        def _(e):
            run("dve", e)

        @block.gpsimd
        def _(e):
            run("pool", e)

        @block.sync
        def _(e):
            run("sp", e)


def _vrow(NL):
    rows = {}
    r = 0
    for nm in ("norm_mix", "norm_mlp", "conv_b", "b_r", "b_i", "lam", "lb_logits"):
        rows[nm] = r
        r += NL
    rows["conv_w"] = r
    r += 4 * NL
    rows["norm_final"] = r
    r += 1
    rows["hg_norm"] = r
    r += NL
    return rows, r


def build_program(NSEQ, T, NL):
    assert T % TT == 0
    NTILE = T // TT
    nc = bass.Bass("TRN2", target_bir_lowering=False)
    dram = {}

    def din(name, shape):
        dram[name] = nc.dram_tensor(name, list(shape), F32, kind="ExternalInput").ap()
        return dram[name]

    x_d = din("x", [NSEQ * T, D])
    lbl_d = din("lb_logits", [NL, D])
    nmix_d = din("norm_mix", [NL, D])
    win_d = din("w_in", [NL, D, DIN])
    cw_d = din("conv_w", [NL, 4, D])
    cb_d = din("conv_b", [NL, D])
    wr_d = din("w_r", [NL, 4, 256, 256])
    br_d = din("b_r", [NL, D])
    wi_d = din("w_i", [NL, 4, 256, 256])
    bi_d = din("b_i", [NL, D])
    lam_d = din("lam", [NL, D])
    hgn_d = din("hg_norm", [NL, 128])
    wo_d = din("w_out", [NL, D, D])
    nmlp_d = din("norm_mlp", [NL, D])
    wu_d = din("w_up", [NL, D, DFF])
    wd_d = din("w_down", [NL, DFF, D])
    nf_d = din("norm_final", [1, D])
    out_d = nc.dram_tensor("out", [NSEQ * T, D], F32, kind="ExternalOutput").ap()

    def dscr(name, shape):
        return nc.dram_tensor(name, list(shape), BF16, kind="Internal").ap()

    win_s = dscr("win_s", [NL, 128, KC, DIN])
    wg_s = dscr("wg_s", [NL, 128, 4, 2, 2, 256])
    dg_s = dscr("dg_s", [NL, 128, 8, 4, 128])
    wo_s = dscr("wo_s", [NL, 128, KC, D])
    wu_s = dscr("wu_s", [NL, 128, KC, DFF])
    wd_s = dscr("wd_s", [NL, 128, 32, D])

    VROW, NV = _vrow(NL)
    tk = TK()
    es = contextlib.ExitStack()
    with es:
        def sem(name):
            return es.enter_context(nc.semaphore(name))

        sems = {e: sem("s_" + e) for e in ENGS}
        block_holder = []

        def sb(name, shape, dt):
            return es.enter_context(nc.sbuf_tensor(name, list(shape), dt))

        ident32 = sb("ident32", [128, 128], F32)
        identbf = sb("identbf", [128, 128], BF16)
        onesD = sb("onesD", [128, 128], BF16)
        onesH = sb("onesH", [128, 128], BF16)
        mask32 = sb("mask32", [32, NCH, 32], F32)
        vecT = sb("vecT", [128, KC, NV], F32)
        chalf = sb("chalf", [128, KC, NL], F32)
        cfull = sb("cfull", [128, KC, NL], F32)
        nchalf = sb("nchalf", [128, KC, NL], F32)
        brh = sb("brh", [128, KC, NL], F32)
        bih = sb("bih", [128, KC, NL], F32)
        kbp = sb("kbp", [128, KC, NL], F32)
        kbn = sb("kbn", [128, KC, NL], F32)
        kb1 = sb("kb1", [128, KC, NL], F32)
        Sc32 = sb("Sc32", [128, NL * 8, 128], F32)
        Sc16 = sb("Sc16", [128, NL * 8, 128], BF16)
        hcar = sb("hcar", [128, NL * 8], F32)
        xkeep = sb("xkeep", [128, NL * 8, 3], BF16)

        d_small = sem("d_small")

        def V(row, c):
            return vecT[:, c, row:row + 1]

        pes = contextlib.ExitStack()
        with pes:
            def psb(name, shape, dt):
                return pes.enter_context(nc.sbuf_tensor(name, list(shape), dt))

            st32 = [psb(f"st32_{i}", [128, 8192], F32) for i in range(2)]
            st16 = [psb(f"st16_{i}", [128, 8192], BF16) for i in range(2)]
            vrow = psb("vrow", [128, D], F32)
            d_ld = [sem("d_pl0"), sem("d_pl1")]
            d_st = [sem("d_ps0"), sem("d_ps1")]
            pps = pes.enter_context(nc.psum_tensor("pps", [128, 512], F32))
            block = pes.enter_context(nc.Block())

            def mk_ident(e):
                e.memset(ident32[:], 0.0)
                return e.affine_select(out=ident32[:], in_=ident32[:], pattern=[[-1, 128]],
                                       compare_op=ALU.not_equal, fill=1.0, base=0, channel_multiplier=1)
            tk.add("pool", mk_ident, writes=["ident32"])
            tk.add("pool", lambda e: e.tensor_copy(out=identbf[:], in_=ident32[:]), reads=["ident32"], writes=["identbf"])
            tk.add("pool", lambda e: e.memset(onesD[:], 1.0 / 1024.0), writes=["onesD"])
            tk.add("pool", lambda e: e.memset(onesH[:], 1.0 / 128.0), writes=["onesH"])

            def mk_mask(e):
                e.memset(mask32[:], 1.0)
                return e.affine_select(out=mask32[:], in_=mask32[:], pattern=[[0, NCH], [1, 32]],
                                       compare_op=ALU.is_ge, fill=0.0, base=0, channel_multiplier=-1)
            tk.add("pool", mk_mask, writes=["mask32"])
            tk.add("pool", lambda e: e.memset(vrow[:], 0.0), writes=["vrow"])

            def ld_small(e):
                ins = None
                for nm, ap in (("norm_mix", nmix_d), ("norm_mlp", nmlp_d), ("conv_b", cb_d), ("b_r", br_d),
                               ("b_i", bi_d), ("lam", lam_d), ("lb_logits", lbl_d)):
                    r = VROW[nm]
                    ins = e.dma_start(out=vrow[r:r + NL, :], in_=ap[:, :]).then_inc(d_small, 16)
                r = VROW["conv_w"]
                ins = e.dma_start(out=vrow[r:r + 4 * NL, :], in_=cw_d.rearrange("l j d -> (l j) d")).then_inc(d_small, 16)
                r = VROW["norm_final"]
                ins = e.dma_start(out=vrow[r:r + 1, :], in_=nf_d[:, :]).then_inc(d_small, 16)
                r = VROW["hg_norm"]
                ins = e.dma_start(out=vrow[r:r + NL, 0:128], in_=hgn_d[:, :]).then_inc(d_small, 16)
                return ins
            tk.add("sp", ld_small, reads=["vrow"], writes=["vrow"], dma=d_small, ndma=10)
            for half in range(2):
                def tr_small(e, half=half):
                    ins = None
                    for cc in range(4):
                        c = half * 4 + cc
                        ins = e.transpose(pps[:, cc * 128:cc * 128 + NV], vrow[0:NV, c * 128:(c + 1) * 128], ident32[0:NV, 0:NV])
                    return ins
                tk.add("pe", tr_small, reads=["vrow", "ident32"], writes=["pps"])
                tk.add("dve", lambda e, half=half: e.tensor_copy(
                    out=vecT[:, half * 4:(half + 1) * 4, :],
                    in_=pps[:].rearrange("p (c n) -> p c n", c=4)[:, :, 0:NV]), reads=["pps"], writes=["vecT"])

            lam_v = vecT[:, :, VROW["lam"]:VROW["lam"] + NL]
            lbl_v = vecT[:, :, VROW["lb_logits"]:VROW["lb_logits"] + NL]
            tA = psb("tA", [128, KC, NL], F32)
            tB = psb("tB", [128, KC, NL], F32)
            tC = psb("tC", [128, KC, 1], F32)

            tk.add("act", lambda e: e.activation(out=tA[:], in_=lam_v, func=ACT.Exp, scale=-1.0), reads=["vecT"], writes=["tA"])
            tk.add("dve", lambda e: e.tensor_scalar(out=tB[:], in0=tA[:], scalar1=0.2, scalar2=-0.25, op0=ALU.mult, op1=ALU.add), reads=["tA"], writes=["tB"])
            tk.add("dve", lambda e: e.tensor_tensor(out=tB[:], in0=tB[:], in1=tA[:], op=ALU.mult), reads=["tA", "tB"], writes=["tB"])
            tk.add("dve", lambda e: e.tensor_scalar(out=tB[:], in0=tB[:], scalar1=1.0 / 3.0, scalar2=None, op0=ALU.add), reads=["tB"], writes=["tB"])
            tk.add("dve", lambda e: e.tensor_tensor(out=tB[:], in0=tB[:], in1=tA[:], op=ALU.mult), reads=["tA", "tB"], writes=["tB"])
            tk.add("dve", lambda e: e.tensor_scalar(out=tB[:], in0=tB[:], scalar1=-0.5, scalar2=None, op0=ALU.add), reads=["tB"], writes=["tB"])
            tk.add("dve", lambda e: e.tensor_tensor(out=tB[:], in0=tB[:], in1=tA[:], op=ALU.mult), reads=["tA", "tB"], writes=["tB"])
            tk.add("dve", lambda e: e.tensor_scalar(out=tB[:], in0=tB[:], scalar1=1.0, scalar2=None, op0=ALU.add), reads=["tB"], writes=["tB"])
            tk.add("dve", lambda e: e.tensor_tensor(out=tB[:], in0=tB[:], in1=tA[:], op=ALU.mult), reads=["tA", "tB"], writes=["tB"])
            tk.add("dve", lambda e: e.tensor_scalar(out=cfull[:], in0=tB[:], scalar1=-RG_C, scalar2=None, op0=ALU.mult), reads=["tB"], writes=["cfull"])
            tk.add("dve", lambda e: e.tensor_scalar(out=chalf[:], in0=tB[:], scalar1=-0.5 * RG_C, scalar2=None, op0=ALU.mult), reads=["tB"], writes=["chalf"])
            tk.add("dve", lambda e: e.tensor_scalar(out=nchalf[:], in0=tB[:], scalar1=0.5 * RG_C, scalar2=None, op0=ALU.mult), reads=["tB"], writes=["nchalf"])
            tk.add("dve", lambda e: e.tensor_scalar(out=brh[:], in0=vecT[:, :, VROW["b_r"]:VROW["b_r"] + NL], scalar1=0.5, scalar2=None, op0=ALU.mult), reads=["vecT"], writes=["brh"])
            tk.add("dve", lambda e: e.tensor_scalar(out=bih[:], in0=vecT[:, :, VROW["b_i"]:VROW["b_i"] + NL], scalar1=0.5, scalar2=None, op0=ALU.mult), reads=["vecT"], writes=["bih"])
            tk.add("dve", lambda e: e.tensor_reduce(out=tC[:], in_=lbl_v, axis=mybir.AxisListType.X, op=ALU.max), reads=["vecT"], writes=["tC"])
            tk.add("dve", lambda e: e.tensor_tensor(out=tA[:], in0=lbl_v, in1=tC[:].to_broadcast([128, KC, NL]), op=ALU.subtract), reads=["vecT", "tC", "tB"], writes=["tA"])
            tk.add("act", lambda e: e.activation(out=tA[:], in_=tA[:], func=ACT.Exp), reads=["tA"], writes=["tA"])
            tk.add("dve", lambda e: e.tensor_reduce(out=tC[:], in_=tA[:], axis=mybir.AxisListType.X, op=ALU.add), reads=["tA"], writes=["tC"])
            tk.add("dve", lambda e: e.reciprocal(out=tC[:], in_=tC[:]), reads=["tC"], writes=["tC"])
            tk.add("dve", lambda e: e.tensor_tensor(out=tA[:], in0=tA[:], in1=tC[:].to_broadcast([128, KC, NL]), op=ALU.mult), reads=["tA", "tC"], writes=["tA"])
            tk.add("dve", lambda e: e.memset(tB[:], 0.0), reads=["tB"], writes=["tB"])
            for l in range(1, NL):
                tk.add("dve", lambda e, l=l: e.tensor_tensor(out=tB[:, :, l:l + 1], in0=tB[:, :, l - 1:l], in1=tA[:, :, l:l + 1], op=ALU.add), reads=["tA", "tB"], writes=["tB"])
            tk.add("dve", lambda e: e.tensor_scalar(out=tB[:], in0=tB[:], scalar1=0.0, scalar2=1.0, op0=ALU.max, op1=ALU.min), reads=["tB"], writes=["tB"])
            tk.add("dve", lambda e: e.tensor_scalar(out=kbp[:], in0=tB[:], scalar1=-0.5, scalar2=0.5, op0=ALU.mult, op1=ALU.add), reads=["tB"], writes=["kbp"])
            tk.add("dve", lambda e: e.tensor_scalar(out=kbn[:], in0=tB[:], scalar1=0.5, scalar2=-0.5, op0=ALU.mult, op1=ALU.add), reads=["tB"], writes=["kbn"])
            tk.add("dve", lambda e: e.tensor_scalar(out=kb1[:], in0=tB[:], scalar1=0.5, scalar2=0.5, op0=ALU.mult, op1=ALU.add), reads=["tB"], writes=["kb1"])

            rnd = [0]
            scr_toks = []

            def conv_round(load_fn, cast_fns, store_fn, nld=1, nst=1, xreads=()):
                i = rnd[0] % 2
                rnd[0] += 1
                s32, s16 = st32[i], st16[i]
                tk.add("sp", lambda e: load_fn(e, s32, d_ld[i]), reads=[], writes=[f"st32_{i}"], dma=d_ld[i], ndma=nld)
                for k, (eng, fn) in enumerate(cast_fns):
                    tk.add(eng, lambda e, fn=fn: fn(e, s32, s16), reads=[f"st32_{i}"] + list(xreads), writes=[f"st16_{i}_{k}"])
                tok = f"scr{len(scr_toks)}"
                scr_toks.append(tok)
                tk.add("sp", lambda e: store_fn(e, s16, d_st[i]), reads=[f"st16_{i}_{k}" for k in range(8)] + [f"st32_{i}"],
                       writes=[tok], dma=d_st[i], ndma=nst)

            def cast_generic(eng, lo, hi):
                if eng == "act":
                    return ("act", lambda e, s32, s16: e.activation(out=s16[:, lo:hi], in_=s32[:, lo:hi], func=ACT.Copy))
                return (eng, lambda e, s32, s16: e.tensor_copy(out=s16[:, lo:hi], in_=s32[:, lo:hi]))

            def split3(n):
                a = (n * 3 // 8) // 128 * 128
                b = a + (n * 3 // 8) // 128 * 128
                return [cast_generic("dve", 0, a), cast_generic("act", a, b), cast_generic("pool", b, n)]

            RG_G = (0, 1, 6)
            HG_G = (2, 3, 4, 5, 7)
            for l in range(NL):
                for kc in range(KC):
                    def ld(e, s32, ds, l=l, kc=kc):
                        return e.dma_start(out=s32[:, :], in_=win_d[l, kc * 128:(kc + 1) * 128, :]).then_inc(ds, 16)
                    casts = []
                    engs = ["dve", "act", "dve", "pool", "act", "dve", "act", "dve"]
                    for g in range(8):
                        if g in RG_G:
                            gi = RG_G.index(g)

                            def cf(e, s32, s16, g=g, gi=gi, eng=engs[g]):
                                src = s32[:, g * 1024:(g + 1) * 1024].rearrange("p (b c j) -> p b c j", b=4, c=2)
                                dst = s16[:, 0:3072].rearrange("p (b x) -> p b x", b=4)[:, :, gi * 256:(gi + 1) * 256].rearrange("p b (c j) -> p b c j", c=2)
                                if eng == "act":
                                    return e.activation(out=dst, in_=src, func=ACT.Copy)
                                return e.tensor_copy(out=dst, in_=src)
                        else:
                            gi = HG_G.index(g)

                            def cf(e, s32, s16, g=g, gi=gi, eng=engs[g]):
                                src = s32[:, g * 1024:(g + 1) * 1024].rearrange("p (h j) -> p h j", h=8)
                                dst = s16[:, 3072:8192].rearrange("p (h x) -> p h x", h=8)[:, :, gi * 128:(gi + 1) * 128]
                                if eng == "act":
                                    return e.activation(out=dst, in_=src, func=ACT.Copy)
                                return e.tensor_copy(out=dst, in_=src)
                        casts.append((engs[g], cf))

                    def stf(e, s16, ds, l=l, kc=kc):
                        return e.dma_start(out=win_s[l, :, kc, :], in_=s16[:, :]).then_inc(ds, 16)
                    conv_round(ld, casts, stf)
                def ld(e, s32, ds, l=l):
                    return e.dma_start(out=s32[:, :].rearrange("p (k n) -> p k n", k=KC),
                                       in_=wo_d[l].rearrange("(k p) n -> p k n", p=128)).then_inc(ds, 16)

                def stf(e, s16, ds, l=l):
                    return e.dma_start(out=wo_s[l], in_=s16[:, :].rearrange("p (k n) -> p k n", k=KC)).then_inc(ds, 16)
                conv_round(ld, split3(8192), stf)
                for k2 in range(KC // 2):
                    def ld(e, s32, ds, l=l, k2=k2):
                        return e.dma_start(out=s32[:, :].rearrange("p (k n) -> p k n", k=2),
                                           in_=wu_d[l, k2 * 256:(k2 + 1) * 256, :].rearrange("(k p) n -> p k n", p=128)).then_inc(ds, 16)

                    def stf(e, s16, ds, l=l, k2=k2):
                        return e.dma_start(out=wu_s[l, :, 2 * k2:2 * k2 + 2, :], in_=s16[:, :].rearrange("p (k n) -> p k n", k=2)).then_inc(ds, 16)
                    conv_round(ld, split3(8192), stf)
                for f8 in range(4):
                    def ld(e, s32, ds, l=l, f8=f8):
                        return e.dma_start(out=s32[:, :].rearrange("p (k n) -> p k n", k=8),
                                           in_=wd_d[l, f8 * 1024:(f8 + 1) * 1024, :].rearrange("(k p) n -> p k n", p=128)).then_inc(ds, 16)

                    def stf(e, s16, ds, l=l, f8=f8):
                        return e.dma_start(out=wd_s[l, :, 8 * f8:8 * f8 + 8, :], in_=s16[:, :].rearrange("p (k n) -> p k n", k=8)).then_inc(ds, 16)
                    conv_round(ld, split3(8192), stf)
                def ld(e, s32, ds, l=l):
                    e.dma_start(out=s32[:, 0:2048].rearrange("p (k n) -> p k n", k=8),
                                in_=wr_d[l].rearrange("b (k p) n -> p (b k) n", p=128)).then_inc(ds, 16)
                    return e.dma_start(out=s32[:, 2048:4096].rearrange("p (k n) -> p k n", k=8),
                                       in_=wi_d[l].rearrange("b (k p) n -> p (b k) n", p=128)).then_inc(ds, 16)

                def dgf(e, s32, s16, l=l):
                    ins = None
                    for c in range(8):
                        for j in range(4):
                            o = 4096 + (c * 4 + j) * 128
                            ins = e.activation(out=s16[:, o:o + 128], in_=identbf[:], func=ACT.Copy,
                                               scale=V(VROW["conv_w"] + 4 * l + j, c))
                    return ins

                def stf(e, s16, ds, l=l):
                    for g in range(2):
                        e.dma_start(out=wg_s[l, :, :, g, :, :],
                                    in_=s16[:, g * 2048:(g + 1) * 2048].rearrange("p (b k n) -> p b k n", b=4, k=2)).then_inc(ds, 16)
                    return e.dma_start(out=dg_s[l], in_=s16[:, 4096:8192].rearrange("p (c j n) -> p c j n", c=8, j=4)).then_inc(ds, 16)
                conv_round(ld, [cast_generic("dve", 0, 4096), ("act", dgf)], stf, nld=2, nst=3, xreads=("vecT", "identbf"))
            tk.add("sp", lambda e: e.nop(), reads=scr_toks + ["vecT", "kbp", "kbn", "kb1", "chalf", "cfull", "nchalf", "brh", "bih", "mask32", "onesD", "onesH", "identbf"], writes=["BAR"])
            for eng in ("pe", "act", "dve", "pool"):
                tk.add(eng, lambda e: e.nop(), reads=["BAR"], writes=[f"BAR_{eng}"])
            tk.finish(block, sems)
            stats0 = tk.stats

        tk2 = TK()
        tk2.dma_cnt = dict(tk.dma_cnt)
        mes = contextlib.ExitStack()
        with mes:
            def msb(name, shape, dt):
                return mes.enter_context(nc.sbuf_tensor(name, list(shape), dt))

            xres = msb("xres", [128, KC, TT], F32)
            hbuf = msb("hbuf", [128, KC, TT], BF16)
            ysum = msb("ysum", [128, KC, TT], BF16)
            iob = msb("iob", [128, 4, D], F32)
            hidden = iob[:].rearrange("p b d -> p (b d)").bitcast(BF16)[:, 0:16 * TT].rearrange("p (f t) -> p f t", f=16)
            NSLOT = 3
            slabs = [msb(f"slab{i}", [128, 8192], BF16) for i in range(NSLOT)]
            d_slab = [sem(f"d_slab{i}") for i in range(NSLOT)]
            d_x = sem("d_x")
            d_o = sem("d_o")
            xa_bf = msb("xa_bf", [128, 2, TT + 3], BF16)
            xc32 = msb("xc32", [128, 2, TT], F32)
            xcbf = msb("xcbf", [128, 2, TT], BF16)
            rgA = msb("rgA", [128, 2, TT], F32)
            rgI = msb("rgI", [128, 2, TT], F32)
            rgS = msb("rgS", [128, 2, TT], F32)
            rgG = msb("rgG", [128, 2, TT], F32)
            qs = msb("qs", [128, TT], F32)
            fz = msb("fz", [128, TT], F32)
            kk = msb("kk", [128, TT], F32)
            fm = msb("fm", [128, TT], F32)
            Pp = msb("Pp", [128, TT], F32)
            Rr = msb("Rr", [128, TT], F32)
            qt = msb("qt", [128, TT], BF16)
            kt = msb("kt", [128, TT], BF16)
            vF = msb("vF", [128, TT], BF16)
            kT32 = msb("kT32", [32, NCH, 128], BF16)
            v32 = msb("v32", [32, NCH, 128], BF16)
            Sm32 = msb("Sm32", [32, NCH, 32], BF16)
            Uall = msb("Uall", [128, NCH, 128], F32)
            Sbf = msb("Sbf", [128, NCH, 128], BF16)
            o2 = msb("o2", [128, TT], BF16)
            msq, onb, sgb, tmb = qs, Rr, kk, fz
            sqb = msb("sqb", [128, TT], BF16)
            rstd = msb("rstd", [128, TT], F32)
            rl = [msb(f"rl{i}", [128, TT], BF16) for i in range(2)]
            banks = [mes.enter_context(nc.psum_tensor(f"bank{i}", [128, TT], F32)) for i in range(8)]
            block = mes.enter_context(nc.Block())
            T2 = tk2

            bank_rr = [0]

            def nb():
                i = bank_rr[0] % 8
                bank_rr[0] += 1
                return banks[i], f"bank{i}"

            slot_rr = [0]

            def nslot():
                i = slot_rr[0] % NSLOT
                slot_rr[0] += 1
                return i

            T2.add("pool", lambda e: e.memset(fm[:], 0.0), writes=["fm"])
            T2.add("pool", lambda e: e.memset(xa_bf[:], 0.0), writes=["xa_bf"])

            def rms_rstd(src_chunks, reads, eps, ones, out_rstd, wtok):
                bk, bt = nb()
                n = len(src_chunks)
                for i, (ap, tok) in enumerate(src_chunks):
                    T2.add("act", lambda e, ap=ap: e.activation(out=sqb[:], in_=ap, func=ACT.Square), reads=[tok], writes=["sqb"])
                    T2.add("pe", lambda e, i=i: e.matmul(bk[:], ones[:], sqb[:], start=(i == 0), stop=(i == n - 1)), reads=["sqb"], writes=[bt])
                T2.add("act", lambda e: e.activation(out=out_rstd, in_=bk[:], func=ACT.Sqrt, bias=eps, scale=1.0), reads=[bt], writes=[wtok])
                T2.add("dve", lambda e: e.reciprocal(out=out_rstd, in_=out_rstd), reads=[wtok], writes=[wtok])

            for sq_i in range(NSEQ):
                T2.add("pool", lambda e: e.memset(Sc32[:], 0.0), writes=[f"Sc32_{l}_{h}" for l in range(NL) for h in range(8)])
                T2.add("pool", lambda e: e.memset(Sc16[:], 0.0), writes=[f"Sc16_{l}_{h}" for l in range(NL) for h in range(8)])
                T2.add("pool", lambda e: e.memset(hcar[:], 0.0), writes=[f"hcar_{l}_{c}" for l in range(NL) for c in range(8)])
                T2.add("pool", lambda e: e.memset(xkeep[:], 0.0), writes=[f"xkeep_{l}_{c}" for l in range(NL) for c in range(8)])
                for ti in range(NTILE):
                    row0 = sq_i * T + ti * TT
                    T2.add("sp", lambda e, row0=row0: e.dma_start(out=iob[:], in_=x_d[row0:row0 + TT, :].rearrange("(b p) d -> p b d", p=128)).then_inc(d_x, 16),
                           writes=["iob"] + [f"hid{f}" for f in range(16)] + [f"io_{b}_{cg}" for b in range(4) for cg in range(2)], dma=d_x)
                    for c in range(KC):
                        bk, bt = nb()

                        def trx(e, c=c, bk=bk):
                            ins = None
                            for b in range(4):
                                ins = e.transpose(bk[:, b * 128:(b + 1) * 128], iob[:, b, c * 128:(c + 1) * 128], ident32[:])
                            return ins
                        T2.add("pe", trx, reads=["iob"], writes=[bt])
                        if c % 2 == 0:
                            T2.add("act", lambda e, c=c, bk=bk: e.activation(out=xres[:, c, :], in_=bk[:], func=ACT.Copy), reads=[bt], writes=[f"xres{c}"])
                        else:
                            T2.add("dve", lambda e, c=c, bk=bk: e.tensor_copy(out=xres[:, c, :], in_=bk[:]), reads=[bt], writes=[f"xres{c}"])

                    for l in range(NL):
                        rms_rstd([(xres[:, c, :], f"xres{c}") for c in range(KC)], None, EPS, onesD, rstd[:], "rstd")
                        for c in range(KC):
                            T2.add("dve", lambda e, c=c, l=l: e.scalar_tensor_tensor(
                                out=hbuf[:, c, :], in0=xres[:, c, :], scalar=V(VROW["norm_mix"] + l, c), in1=rstd[:],
                                op0=ALU.mult, op1=ALU.mult), reads=[f"xres{c}", "rstd"], writes=[f"h{c}"])
                        hreads = [f"h{c}" for c in range(KC)]

                        def win_mm(bk, slab, col0, W_):
                            def f(e):
                                ins = None
                                sv = slab[:, :]
                                for kc in range(KC):
                                    ins = e.matmul(bk[:], sv[:, kc * W_ + col0: kc * W_ + col0 + 128], hbuf[:, kc, :],
                                                   start=(kc == 0), stop=(kc == KC - 1))
                                return ins
                            return f

                        for b in range(4):
                            si = nslot()
                            slab = slabs[si]
                            stok = f"slab{si}"

                            def ld_rg(e, slab=slab, l=l, b=b, ds=d_slab[si]):
                                e.dma_start(out=slab[:, 0:6144].rearrange("p (k n) -> p k n", k=KC),
                                            in_=win_s[l, :, :, b * 768:(b + 1) * 768]).then_inc(ds, 16)
                                e.dma_start(out=slab[:, 6144:7168].rearrange("p (g k n) -> p g k n", g=2, k=2),
                                            in_=wg_s[l, :, b, :, :, :]).then_inc(ds, 16)
                                return e.dma_start(out=slab[:, 7168:8192].rearrange("p (c j n) -> p c j n", c=2, j=4),
                                                   in_=dg_s[l, :, 2 * b:2 * b + 2, :, :]).then_inc(ds, 16)
                            T2.add("sp", ld_rg, writes=[stok], dma=d_slab[si], ndma=3)
                            chs = (2 * b, 2 * b + 1)
                            for ci, c in enumerate(chs):
                                T2.add("pool", lambda e, ci=ci, c=c, l=l: e.tensor_copy(out=xa_bf[:, ci, 0:3], in_=xkeep[:, l * 8 + c, :]),
                                       reads=[f"xkeep_{l}_{c}"], writes=[f"xa_halo{ci}"])
                                bk, bt = nb()
                                T2.add("pe", win_mm(bk, slab, 0 * 256 + ci * 128, 768), reads=hreads + [stok], writes=[bt])
                                T2.add("act", lambda e, ci=ci, bk=bk: e.activation(out=xa_bf[:, ci, 3:3 + TT], in_=bk[:], func=ACT.Copy),
                                       reads=[bt], writes=[f"xa_bf{ci}"])
                                bk2, bt2 = nb()

                                def convmm(e, ci=ci, bk2=bk2, slab=slab):
                                    ins = None
                                    for j in range(4):
                                        o = 7168 + (ci * 4 + j) * 128
                                        ins = e.matmul(bk2[:], slab[:, o:o + 128], xa_bf[:, ci, j:j + TT], start=(j == 0), stop=(j == 3))
                                    return ins
                                T2.add("pe", convmm, reads=[f"xa_bf{ci}", f"xa_halo{ci}", stok], writes=[bt2])
                                T2.add("pool", lambda e, ci=ci, c=c, l=l: e.tensor_copy(out=xkeep[:, l * 8 + c, :], in_=xa_bf[:, ci, TT:TT + 3]),
                                       reads=[f"xa_bf{ci}"], writes=[f"xkeep_{l}_{c}"])
                                T2.add("act", lambda e, ci=ci, c=c, l=l, bk2=bk2: e.activation(out=xc32[:, ci, :], in_=bk2[:], func=ACT.Identity,
                                                                                            bias=V(VROW["conv_b"] + l, c), scale=1.0),
                                       reads=[bt2], writes=[f"xc32_{ci}"])
                                T2.add("pool", lambda e, ci=ci: e.tensor_copy(out=xcbf[:, ci, :], in_=xc32[:, ci, :]), reads=[f"xc32_{ci}"], writes=[f"xcbf{ci}"])
                            for ci, c in enumerate(chs):
                                for g in range(2):
                                    bk, bt = nb()

                                    def gmm(e, ci=ci, g=g, bk=bk, slab=slab):
                                        ins = None
                                        for k2 in range(2):
                                            o = 6144 + (g * 2 + k2) * 256 + ci * 128
                                            ins = e.matmul(bk[:], slab[:, o:o + 128], xcbf[:, k2, :], start=(k2 == 0), stop=(k2 == 1))
                                        return ins
                                    T2.add("pe", gmm, reads=["xcbf0", "xcbf1", stok], writes=[bt])
                                    dst = rgA if g == 0 else rgI
                                    bsrc = brh if g == 0 else bih
                                    T2.add("act", lambda e, ci=ci, c=c, l=l, bk=bk, dst=dst, bsrc=bsrc: e.activation(
                                        out=dst[:, ci, :], in_=bk[:], func=ACT.Tanh, bias=bsrc[:, c, l:l + 1], scale=0.5),
                                        reads=[bt], writes=[("rgA" if g == 0 else "rgI") + str(ci)])
                            for ci, c in enumerate(chs):
                                T2.add("act", lambda e, ci=ci, c=c, l=l: e.activation(out=rgS[:, ci, :], in_=rgA[:, ci, :], func=ACT.Tanh,
                                                                                     bias=nchalf[:, c, l:l + 1], scale=nchalf[:, c, l:l + 1]),
                                       reads=[f"rgA{ci}"], writes=[f"rgS{ci}"])
                                T2.add("act", lambda e, ci=ci, c=c, l=l: e.activation(out=rgA[:, ci, :], in_=rgA[:, ci, :], func=ACT.Exp,
                                                                                     bias=chalf[:, c, l:l + 1], scale=chalf[:, c, l:l + 1]),
                                       reads=[f"rgA{ci}"], writes=[f"rgA{ci}"])
                            for ci, c in enumerate(chs):
                                T2.add("dve", lambda e, ci=ci: e.tensor_scalar(out=rgG[:, ci, :], in0=rgS[:, ci, :], scalar1=1.0, scalar2=None, op0=ALU.add),
                                       reads=[f"rgS{ci}"], writes=[f"rgG{ci}"])
                                T2.add("dve", lambda e, ci=ci: e.reciprocal(out=rgG[:, ci, :], in_=rgG[:, ci, :]), reads=[f"rgG{ci}"], writes=[f"rgG{ci}"])
                                T2.add("dve", lambda e, ci=ci: e.tensor_tensor(out=rgS[:, ci, :], in0=rgS[:, ci, :], in1=rgG[:, ci, :], op=ALU.mult),
                                       reads=[f"rgS{ci}", f"rgG{ci}"], writes=[f"rgS{ci}"])
                                T2.add("act", lambda e, ci=ci: e.activation(out=rgS[:, ci, :], in_=rgS[:, ci, :], func=ACT.Sqrt, scale=0.5),
                                       reads=[f"rgS{ci}"], writes=[f"rgS{ci}"])
                                T2.add("dve", lambda e, ci=ci: e.scalar_tensor_tensor(out=rgI[:, ci, :], in0=rgI[:, ci, :], scalar=1.0, in1=xc32[:, ci, :],
                                                                                      op0=ALU.add, op1=ALU.mult),
                                       reads=[f"rgI{ci}", f"xc32_{ci}"], writes=[f"rgI{ci}"])
                                T2.add("dve", lambda e, ci=ci: e.tensor_tensor(out=rgI[:, ci, :], in0=rgI[:, ci, :], in1=rgS[:, ci, :], op=ALU.mult),
                                       reads=[f"rgI{ci}", f"rgS{ci}"], writes=[f"rgI{ci}"])
                                T2.add("dve", lambda e, ci=ci, c=c, l=l: e.tensor_tensor_scan(out=xc32[:, ci, :], data0=rgA[:, ci, :], data1=rgI[:, ci, :],
                                                                                             initial=hcar[:, l * 8 + c:l * 8 + c + 1],
                                                                                             op0=ALU.mult, op1=ALU.add),
                                       reads=[f"rgA{ci}", f"rgI{ci}", f"hcar_{l}_{c}"], writes=[f"xc32_{ci}"])
                                T2.add("dve", lambda e, ci=ci, c=c, l=l: e.tensor_copy(out=hcar[:, l * 8 + c:l * 8 + c + 1], in_=xc32[:, ci, TT - 1:TT]),
                                       reads=[f"xc32_{ci}"], writes=[f"hcar_{l}_{c}"])
                            for ci, c in enumerate(chs):
                                bk, bt = nb()
                                T2.add("pe", win_mm(bk, slab, 1 * 256 + ci * 128, 768), reads=hreads + [stok], writes=[bt])
                                T2.add("act", lambda e, ci=ci, bk=bk: e.activation(out=rgG[:, ci, :], in_=bk[:], func=ACT.Gelu_apprx_tanh),
                                       reads=[bt], writes=[f"rgG{ci}"])
                                bk2, bt2 = nb()
                                T2.add("pe", win_mm(bk2, slab, 2 * 256 + ci * 128, 768), reads=hreads + [stok], writes=[bt2])
                                T2.add("act", lambda e, ci=ci, bk2=bk2: e.activation(out=rgS[:, ci, :], in_=bk2[:], func=ACT.Tanh, scale=0.5),
                                       reads=[bt2], writes=[f"rgS{ci}"])
                                T2.add("dve", lambda e, ci=ci: e.tensor_tensor(out=rgG[:, ci, :], in0=rgG[:, ci, :], in1=xc32[:, ci, :], op=ALU.mult),
                                       reads=[f"rgG{ci}", f"xc32_{ci}"], writes=[f"rgG{ci}"])
                                T2.add("dve", lambda e, ci=ci, c=c: e.scalar_tensor_tensor(out=ysum[:, c, :], in0=rgS[:, ci, :], scalar=1.0, in1=rgG[:, ci, :],
                                                                                           op0=ALU.add, op1=ALU.mult),
                                       reads=[f"rgS{ci}", f"rgG{ci}"], writes=[f"ysum{c}"])

                        for hh in range(8):
                            si = nslot()
                            slab = slabs[si]
                            stok = f"slab{si}"

                            def ld_hg(e, slab=slab, l=l, hh=hh, ds=d_slab[si]):
                                return e.dma_start(out=slab[:, 0:5120].rearrange("p (k n) -> p k n", k=KC),
                                                   in_=win_s[l, :, :, 3072 + hh * 640:3072 + (hh + 1) * 640]).then_inc(ds, 16)
                            T2.add("sp", ld_hg, writes=[stok], dma=d_slab[si])
                            sk = f"Sc_{l}_{hh}"
                            bk, bt = nb()
                            T2.add("pe", win_mm(bk, slab, 0, 640), reads=hreads + [stok], writes=[bt])
                            T2.add("act", lambda e, bk=bk: e.activation(out=qs[:], in_=bk[:], func=ACT.Silu), reads=[bt], writes=["qs"])
                            bk, bt = nb()
                            T2.add("pe", win_mm(bk, slab, 128, 640), reads=hreads + [stok], writes=[bt])
                            T2.add("act", lambda e, bk=bk: e.activation(out=fz[:], in_=bk[:], func=ACT.Tanh, scale=0.5), reads=[bt], writes=["fz"])
                            bk, bt = nb()
                            T2.add("pe", win_mm(bk, slab, 256, 640), reads=hreads + [stok], writes=[bt])
                            T2.add("act", lambda e, bk=bk: e.activation(out=vF[:], in_=bk[:], func=ACT.Copy), reads=[bt], writes=["vF"])
                            T2.add("dve", lambda e, l=l, hh=hh: e.tensor_scalar(out=kk[:], in0=fz[:], scalar1=kbn[:, hh, l:l + 1], scalar2=kbp[:, hh, l:l + 1],
                                                                                op0=ALU.mult, op1=ALU.add), reads=["fz"], writes=["kk"])
                            T2.add("dve", lambda e, l=l, hh=hh: e.tensor_scalar(out=fz[:], in0=fz[:], scalar1=kbp[:, hh, l:l + 1], scalar2=kb1[:, hh, l:l + 1],
                                                                                op0=ALU.mult, op1=ALU.add), reads=["fz", "kk"], writes=["fz"])
                            fv = fz[:].rearrange("p (c i) -> p c i", i=32)
                            fmv = fm[:].rearrange("p (c i) -> p c i", i=32)
                            T2.add("pool", lambda e: e.tensor_copy(out=fmv[:, :, 0:1], in_=fv[:, :, 0:1]), reads=["fz"], writes=["fm"])
                            T2.add("dve", lambda e: e.tensor_tensor_scan(out=Pp[:], data0=fz[:], data1=fm[:], initial=1.0, op0=ALU.mult, op1=ALU.max),
                                   reads=["fz", "fm"], writes=["Pp"])
                            T2.add("dve", lambda e: e.reciprocal(out=Rr[:], in_=Pp[:]), reads=["Pp"], writes=["Rr"])
                            T2.add("dve", lambda e: e.tensor_tensor(out=qt[:], in0=qs[:], in1=Pp[:], op=ALU.mult), reads=["qs", "Pp"], writes=["qt"])
                            T2.add("dve", lambda e: e.tensor_tensor(out=kt[:], in0=kk[:], in1=Rr[:], op=ALU.mult), reads=["kk", "Rr"], writes=["kt"])
                            bkT, btT = nb()
                            kTp = bkT[:].bitcast(BF16)

                            def trk(e, kTp=kTp):
                                ins = None
                                for c in range(8):
                                    ins = e.transpose(kTp[0:32, c * 128:(c + 1) * 128], kt[:, c * 32:(c + 1) * 32], identbf[:])
                                return ins

                            def trk2(e, kTp=kTp):
                                ins = None
                                for c in range(8, 16):
                                    ins = e.transpose(kTp[0:32, (c - 8) * 128:(c - 7) * 128], kt[:, c * 32:(c + 1) * 32], identbf[:])
                                return ins
                            T2.add("pe", trk, reads=["kt", "identbf"], writes=[btT])
                            T2.add("act", lambda e, kTp=kTp: e.activation(out=kT32[:, 0:8, :], in_=kTp[0:32, :].rearrange("p (c n) -> p c n", c=8), func=ACT.Copy),
                                   reads=[btT], writes=["kT32a"])
                            bkT2, btT2 = nb()
                            kTp2 = bkT2[:].bitcast(BF16)
                            T2.add("pe", lambda e, kTp2=kTp2: trk2(e, kTp2), reads=["kt", "identbf"], writes=[btT2])
                            T2.add("act", lambda e, kTp2=kTp2: e.activation(out=kT32[:, 8:16, :], in_=kTp2[0:32, :].rearrange("p (c n) -> p c n", c=8), func=ACT.Copy),
                                   reads=[btT2], writes=["kT32b"])
                            bkV, btV = nb()
                            vTp = bkV[:].bitcast(BF16)

                            def trv(e, vTp=vTp, lo=0):
                                ins = None
                                for c in range(lo, lo + 8):
                                    ins = e.transpose(vTp[0:32, (c - lo) * 128:(c - lo + 1) * 128], vF[:, c * 32:(c + 1) * 32], identbf[:])
                                return ins
                            T2.add("pe", lambda e, vTp=vTp: trv(e, vTp, 0), reads=["vF", "identbf"], writes=[btV])
                            T2.add("dve", lambda e, vTp=vTp: e.tensor_copy(out=v32[:, 0:8, :], in_=vTp[0:32, :].rearrange("p (c n) -> p c n", c=8)),
                                   reads=[btV], writes=["v32a"])
                            bkV2, btV2 = nb()
                            vTp2 = bkV2[:].bitcast(BF16)
                            T2.add("pe", lambda e, vTp2=vTp2: trv(e, vTp2, 8), reads=["vF", "identbf"], writes=[btV2])
                            T2.add("dve", lambda e, vTp2=vTp2: e.tensor_copy(out=v32[:, 8:16, :], in_=vTp2[0:32, :].rearrange("p (c n) -> p c n", c=8)),
                                   reads=[btV2], writes=["v32b"])
                            bkS, btS = nb()

                            def scmm(e, bkS=bkS):
                                ins = None
                                for c in range(NCH):
                                    ins = e.matmul(bkS[0:32, c * 32:(c + 1) * 32], kt[:, c * 32:(c + 1) * 32], qt[:, c * 32:(c + 1) * 32], start=True, stop=True)
                                return ins
                            T2.add("pe", scmm, reads=["kt", "qt"], writes=[btS])
                            T2.add("dve", lambda e, bkS=bkS: e.tensor_tensor(out=Sm32[:], in0=bkS[0:32, :].rearrange("p (c t) -> p c t", c=NCH), in1=mask32[:], op=ALU.mult),
                                   reads=[btS], writes=["Sm32"])
                            abanks = []
                            for q4 in range(4):
                                bkA, btA = nb()
                                abanks.append((bkA, btA))

                                def amm(e, bkA=bkA, q4=q4):
                                    ins = None
                                    for cc in range(4):
                                        c = q4 * 4 + cc
                                        ins = e.matmul(bkA[:, cc * 128:(cc + 1) * 128], kT32[:, c, :], v32[:, c, :], start=True, stop=True)
                                    return ins
                                T2.add("pe", amm, reads=["kT32a", "kT32b", "v32a", "v32b"], writes=[btA])
                            ecol = Pp[:].rearrange("p (c i) -> p c i", i=32)
                            for c in range(NCH):
                                bkA, btA = abanks[c // 4]
                                a_ap = bkA[:, (c % 4) * 128:(c % 4 + 1) * 128]
                                if c == 0:
                                    T2.add("dve", lambda e, a_ap=a_ap, l=l, hh=hh: e.tensor_tensor(out=Uall[:, 0, :], in0=Sc32[:, l * 8 + hh, :], in1=a_ap, op=ALU.add),
                                           reads=[btA, f"Sc32_{l}_{hh}"], writes=["U0"])
                                else:
                                    T2.add("dve", lambda e, a_ap=a_ap, c=c: e.scalar_tensor_tensor(out=Uall[:, c, :], in0=Uall[:, c - 1, :], scalar=ecol[:, c - 1, 31:32],
                                                                                                  in1=a_ap, op0=ALU.mult, op1=ALU.add),
                                           reads=[btA, f"U{c - 1}", "Pp"], writes=[f"U{c}"])
                            ureads = [f"U{c}" for c in range(NCH)]
                            T2.add("dve", lambda e: e.tensor_tensor(out=Sbf[:], in0=Uall[:], in1=ecol[:, :, 31:32].to_broadcast([128, NCH, 128]), op=ALU.mult),
                                   reads=ureads + ["Pp"], writes=["Sbf"])
                            bkO, btO = nb()

                            def omm(e, bkO=bkO, l=l, hh=hh):
                                ins = None
                                for c in range(NCH):
                                    e.matmul(bkO[:, c * 32:(c + 1) * 32], v32[:, c, :], Sm32[:, c, :], start=True, stop=False)
                                    st_ap = Sc16[:, l * 8 + hh, :] if c == 0 else Sbf[:, c - 1, :]
                                    ins = e.matmul(bkO[:, c * 32:(c + 1) * 32], st_ap, qt[:, c * 32:(c + 1) * 32], start=False, stop=True)
                                return ins
                            T2.add("pe", omm, reads=["v32a", "v32b", "Sm32", "Sbf", "qt", f"Sc16_{l}_{hh}"], writes=[btO])
                            T2.add("dve", lambda e, l=l, hh=hh: e.tensor_scalar(out=Sc32[:, l * 8 + hh, :], in0=Uall[:, NCH - 1, :], scalar1=ecol[:, NCH - 1, 31:32], scalar2=None,
                                                                                op0=ALU.mult), reads=[f"U{NCH - 1}", "Pp"], writes=[f"Sc32_{l}_{hh}"])
                            T2.add("pool", lambda e, l=l, hh=hh: e.tensor_copy(out=Sc16[:, l * 8 + hh, :], in_=Sbf[:, NCH - 1, :]), reads=["Sbf"], writes=[f"Sc16_{l}_{hh}"])
                            T2.add("act", lambda e, bkO=bkO: e.activation(out=o2[:], in_=bkO[:], func=ACT.Square), reads=[btO], writes=["o2"])
                            bkM, btM = nb()
                            T2.add("pe", lambda e, bkM=bkM: e.matmul(bkM[:], onesH[:], o2[:], start=True, stop=True), reads=["o2"], writes=[btM])
                            T2.add("act", lambda e, bkM=bkM: e.activation(out=msq[:], in_=bkM[:], func=ACT.Sqrt, bias=EPS, scale=1.0), reads=[btM], writes=["qs"])
                            T2.add("dve", lambda e: e.reciprocal(out=msq[:], in_=msq[:]), reads=["qs"], writes=["qs"])
                            T2.add("dve", lambda e, bkO=bkO, l=l: e.scalar_tensor_tensor(out=onb[:], in0=bkO[:], scalar=vecT[:, 0, VROW["hg_norm"] + l:VROW["hg_norm"] + l + 1],
                                                                                         in1=msq[:], op0=ALU.mult, op1=ALU.mult),
                                   reads=[btO, "qs"], writes=["Rr"])
                            bk, bt = nb()
                            T2.add("pe", win_mm(bk, slab, 384, 640), reads=hreads + [stok], writes=[bt])
                            T2.add("act", lambda e, bk=bk: e.activation(out=sgb[:], in_=bk[:], func=ACT.Silu), reads=[bt], writes=["kk"])
                            bk, bt = nb()
                            T2.add("pe", win_mm(bk, slab, 512, 640), reads=hreads + [stok], writes=[bt])
                            T2.add("act", lambda e, bk=bk: e.activation(out=tmb[:], in_=bk[:], func=ACT.Tanh, scale=0.5), reads=[bt], writes=["fz"])
                            T2.add("dve", lambda e: e.tensor_tensor(out=onb[:], in0=onb[:], in1=sgb[:], op=ALU.mult), reads=["Rr", "kk"], writes=["Rr"])
                            T2.add("dve", lambda e: e.scalar_tensor_tensor(out=onb[:], in0=tmb[:], scalar=1.0, in1=onb[:], op0=ALU.add, op1=ALU.mult),
                                   reads=["Rr", "fz"], writes=["Rr"])
                            T2.add("dve", lambda e, hh=hh: e.tensor_tensor(out=ysum[:, hh, :], in0=ysum[:, hh, :], in1=onb[:], op=ALU.add),
                                   reads=["Rr", f"ysum{hh}"], writes=[f"ysum{hh}"])

                        si = nslot()
                        slab = slabs[si]
                        stok = f"slab{si}"
                        T2.add("sp", lambda e, slab=slab, l=l, ds=d_slab[si]: e.dma_start(
                            out=slab[:, :].rearrange("p (k n) -> p k n", k=KC), in_=wo_s[l]).then_inc(ds, 16), writes=[stok], dma=d_slab[si])
                        yreads = [f"ysum{c}" for c in range(KC)]
                        for m in range(KC):
                            bk, bt = nb()

                            def womm(e, bk=bk, slab=slab, m=m):
                                ins = None
                                for kc in range(KC):
                                    ins = e.matmul(bk[:], slab[:, kc * D + m * 128: kc * D + (m + 1) * 128], ysum[:, kc, :], start=(kc == 0), stop=(kc == KC - 1))
                                return ins
                            T2.add("pe", womm, reads=yreads + [stok], writes=[bt])
                            T2.add("dve", lambda e, bk=bk, m=m: e.scalar_tensor_tensor(out=xres[:, m, :], in0=bk[:], scalar=0.5, in1=xres[:, m, :], op0=ALU.mult, op1=ALU.add),
                                   reads=[bt, f"xres{m}"], writes=[f"xres{m}"])

                        rms_rstd([(xres[:, c, :], f"xres{c}") for c in range(KC)], None, EPS, onesD, rstd[:], "rstd")
                        for c in range(KC):
                            T2.add("dve", lambda e, c=c, l=l: e.scalar_tensor_tensor(
                                out=hbuf[:, c, :], in0=xres[:, c, :], scalar=V(VROW["norm_mlp"] + l, c), in1=rstd[:],
                                op0=ALU.mult, op1=ALU.mult), reads=[f"xres{c}", "rstd"], writes=[f"h{c}"])
                        for sg in range(2):
                            uslabs = []
                            for half in range(2):
                                fg = sg * 2 + half
                                si = nslot()
                                T2.add("sp", lambda e, slab=slabs[si], l=l, fg=fg, ds=d_slab[si]: e.dma_start(
                                    out=slab[:, :].rearrange("p (k n) -> p k n", k=KC), in_=wu_s[l, :, :, fg * 1024:(fg + 1) * 1024]).then_inc(ds, 16),
                                    writes=[f"slab{si}"], dma=d_slab[si])
                                uslabs.append((slabs[si], f"slab{si}"))
                                for f8 in range(8):
                                    f = half * 8 + f8
                                    bk, bt = nb()

                                    def upmm(e, bk=bk, slab=slabs[si], f8=f8):
                                        ins = None
                                        for kc in range(KC):
                                            ins = e.matmul(bk[:], slab[:, kc * 1024 + f8 * 128: kc * 1024 + (f8 + 1) * 128], hbuf[:, kc, :], start=(kc == 0), stop=(kc == KC - 1))
                                        return ins
                                    T2.add("pe", upmm, reads=hreads + [f"slab{si}"], writes=[bt])
                                    r_ = rl[f % 2]
                                    T2.add("act", lambda e, bk=bk, r_=r_: e.activation(out=r_[:], in_=bk[:], func=ACT.Relu), reads=[bt], writes=[f"rl{f % 2}"])
                                    T2.add("pool", lambda e, r_=r_, f=f: e.tensor_tensor(out=hidden[:, f, :], in0=r_[:], in1=r_[:], op=ALU.mult),
                                           reads=[f"rl{f % 2}"], writes=[f"hid{f}", "iob"])
                            dslabs = []
                            for half in range(2):
                                fg = sg * 2 + half
                                si = nslot()
                                T2.add("sp", lambda e, slab=slabs[si], l=l, fg=fg, ds=d_slab[si]: e.dma_start(
                                    out=slab[:, :].rearrange("p (k n) -> p k n", k=8), in_=wd_s[l, :, fg * 8:(fg + 1) * 8, :]).then_inc(ds, 16),
                                    writes=[f"slab{si}"], dma=d_slab[si])
                                dslabs.append((slabs[si], f"slab{si}"))
                            for m in range(KC):
                                bk, bt = nb()

                                def dnmm(e, bk=bk, m=m, dslabs=dslabs):
                                    ins = None
                                    for f in range(16):
                                        slab = dslabs[f // 8][0]
                                        f8 = f % 8
                                        ins = e.matmul(bk[:], slab[:, f8 * D + m * 128: f8 * D + (m + 1) * 128], hidden[:, f, :], start=(f == 0), stop=(f == 15))
                                    return ins
                                T2.add("pe", dnmm, reads=[f"hid{f}" for f in range(16)] + [dslabs[0][1], dslabs[1][1]], writes=[bt])
                                T2.add("dve", lambda e, bk=bk, m=m: e.tensor_tensor(out=xres[:, m, :], in0=xres[:, m, :], in1=bk[:], op=ALU.add),
                                       reads=[bt, f"xres{m}"], writes=[f"xres{m}"])

                    rms_rstd([(xres[:, c, :], f"xres{c}") for c in range(KC)], None, EPS, onesD, rstd[:], "rstd")
                    for c in range(KC):
                        T2.add("dve", lambda e, c=c: e.scalar_tensor_tensor(
                            out=xres[:, c, :], in0=xres[:, c, :], scalar=V(VROW["norm_final"], c), in1=rstd[:],
                            op0=ALU.mult, op1=ALU.mult), reads=[f"xres{c}", "rstd"], writes=[f"xres{c}"])
                    for b in range(4):
                        for cg in range(2):
                            bk, bt = nb()

                            def tro(e, bk=bk, b=b, cg=cg):
                                ins = None
                                for cc in range(4):
                                    c = cg * 4 + cc
                                    ins = e.transpose(bk[:, cc * 128:(cc + 1) * 128], xres[:, c, b * 128:(b + 1) * 128], ident32[:])
                                return ins
                            T2.add("pe", tro, reads=[f"xres{c}" for c in range(cg * 4, cg * 4 + 4)], writes=[bt])
                            if cg == 0:
                                T2.add("act", lambda e, bk=bk, b=b, cg=cg: e.activation(out=iob[:, b, cg * 512:(cg + 1) * 512], in_=bk[:], func=ACT.Copy),
                                       reads=[bt], writes=[f"io_{b}_{cg}"])
                            else:
                                T2.add("dve", lambda e, bk=bk, b=b, cg=cg: e.tensor_copy(out=iob[:, b, cg * 512:(cg + 1) * 512], in_=bk[:]),
                                       reads=[bt], writes=[f"io_{b}_{cg}"])
                    T2.add("sp", lambda e, row0=row0: e.dma_start(out=out_d[row0:row0 + TT, :].rearrange("(b p) d -> p b d", p=128), in_=iob[:]).then_inc(d_o, 16),
                           reads=["iob"] + [f"hid{f}" for f in range(16)] + [f"io_{b}_{cg}" for b in range(4) for cg in range(2)], writes=["OUT"], dma=d_o)
            T2.add("sp", lambda e: e.nop(), reads=["OUT"])
            T2.finish(block, sems_main(nc, es, sems))
            stats1 = T2.stats
    build_program.stats = (stats0, stats1)
    return nc


def sems_main(nc, es, sems):
    return {e: es.enter_context(nc.semaphore("m_" + e)) for e in ENGS}


_CACHE = {}


def kernel(x, lb_logits, norm_mix, w_in, conv_w, conv_b, w_r, b_r, w_i, b_i, lam, hg_norm, w_out, norm_mlp, w_up, w_down, norm_final):
    x = np.asarray(x)
    B, T, _ = x.shape
    NL = int(np.asarray(w_in).shape[0])
    ncores = 8
    assert B % ncores == 0
    NSEQ = B // ncores
    key = (NSEQ, T, NL)
    if key not in _CACHE:
        _CACHE[key] = build_program(NSEQ, T, NL)
    nc = _CACHE[key]
    f32 = lambda a: np.ascontiguousarray(np.asarray(a, dtype=np.float32))
    shared = {
        "lb_logits": f32(lb_logits), "norm_mix": f32(norm_mix), "w_in": f32(w_in), "conv_w": f32(conv_w),
        "conv_b": f32(conv_b), "w_r": f32(w_r), "b_r": f32(b_r), "w_i": f32(w_i), "b_i": f32(b_i),
        "lam": f32(lam), "hg_norm": f32(hg_norm), "w_out": f32(w_out), "norm_mlp": f32(norm_mlp),
        "w_up": f32(w_up), "w_down": f32(w_down), "norm_final": f32(norm_final).reshape(1, D),
    }
    xs = f32(x).reshape(ncores, NSEQ * T, D)
    in_maps = []
    for i in range(ncores):
        m = dict(shared)
        m["x"] = xs[i]
        in_maps.append(m)
    res = run_bass_kernel_spmd(nc, in_maps, core_ids=list(range(ncores)))
    out = np.stack([np.asarray(r["out"]) for r in res.results], axis=0)
    return out.reshape(B, T, D).astype(np.float32)
```

```python
import contextlib
import numpy as np
import concourse.bass as bass
import concourse.mybir as mybir
from concourse.bass_utils import run_bass_kernel_spmd

F32 = mybir.dt.float32
BF16 = mybir.dt.bfloat16
ACT = mybir.ActivationFunctionType
ALU = mybir.AluOpType

D = 1024
KC = 8
DIN = 8192
DFF = 4096
TT = 512
NCH = TT // 32
EPS = 1e-6
RG_C = 8.0
ENGS = ("pe", "act", "dve", "pool", "sp")


class _Op:
    __slots__ = ("eng", "emit", "deps", "needs_inc", "semval", "dsem", "dval")

    def __init__(self, eng, emit):
        self.eng = eng
        self.emit = emit
        self.deps = set()
        self.needs_inc = False
        self.semval = 0
        self.dsem = None
        self.dval = 0


class TK:
    def __init__(self):
        self.ops = []
        self.last_w = {}
        self.readers = {}
        self.dma_cnt = {}

    def add(self, eng, emit, reads=(), writes=(), dma=None, ndma=1):
        idx = len(self.ops)
        op = _Op(eng, emit)
        deps = op.deps
        lw = self.last_w
        rd = self.readers
        for t in reads:
            w = lw.get(t)
            if w is not None:
                deps.add((w, 0))
        for t in writes:
            w = lw.get(t)
            if w is not None:
                deps.add((w, 1))
            for r in rd.get(t, ()):
                deps.add((r, 2))
        for t in reads:
            rd.setdefault(t, []).append(idx)
        for t in writes:
            lw[t] = idx
            rd[t] = []
        if dma is not None:
            op.dsem = dma
            c = self.dma_cnt.get(dma.num, 0) + 16 * ndma
            self.dma_cnt[dma.num] = c
            op.dval = c
        self.ops.append(op)
        return idx

    def finish(self, block, sems):
        ops = self.ops
        real = []
        for i, op in enumerate(ops):
            rl = []
            for (d, kind) in op.deps:
                if d == i:
                    continue
                p = ops[d]
                if p.dsem is None and op.dsem is None and p.eng == op.eng:
                    if op.eng == "pe" or kind != 0:
                        continue
                rl.append(d)
                if p.dsem is None:
                    p.needs_inc = True
            real.append(rl)
        cnt = {e: 0 for e in ENGS}
        for op in ops:
            if op.dsem is None and op.needs_inc:
                cnt[op.eng] += 1
                op.semval = cnt[op.eng]
        per_eng = {e: [] for e in ENGS}
        for i, op in enumerate(ops):
            per_eng[op.eng].append(i)
        self.stats = {e: len(v) for e, v in per_eng.items()}
        self.stats["sem_max"] = dict(cnt)

        def run(eng_name, eh):
            waited = {}
            for i in per_eng[eng_name]:
                op = ops[i]
                waits = {}
                for d in real[i]:
                    p = ops[d]
                    if p.dsem is not None:
                        s, v = p.dsem, p.dval
                    else:
                        s, v = sems[p.eng], p.semval
                    if waited.get(s.num, 0) < v and waits.get(s.num, (None, 0))[1] < v:
                        waits[s.num] = (s, v)
                for sn, (s, v) in waits.items():
                    eh.wait_ge(s, v)
                    waited[sn] = v
                ins = op.emit(eh)
                if op.dsem is None and op.needs_inc:
                    ins.then_inc(sems[eng_name], 1)

        @block.tensor
        def _(e):
            run("pe", e)

        @block.scalar
        def _(e):
            run("act", e)

        @block.vector
        def _(e):
            run("dve", e)

        @block.gpsimd
        def _(e):
            run("pool", e)

        @block.sync
        def _(e):
            run("sp", e)


def _vrow(NL):
    rows = {}
    r = 0
    for nm in ("norm_mix", "norm_mlp", "conv_b", "b_r", "b_i", "lam", "lb_logits"):
        rows[nm] = r
        r += NL
    rows["conv_w"] = r
    r += 4 * NL
    rows["norm_final"] = r
    r += 1
    rows["hg_norm"] = r
    r += NL
    return rows, r


def build_program(NSEQ, T, NL):
    assert T % TT == 0
    NTILE = T // TT
    nc = bass.Bass("TRN2", target_bir_lowering=False)
    dram = {}

    def din(name, shape):
        dram[name] = nc.dram_tensor(name, list(shape), F32, kind="ExternalInput").ap()
        return dram[name]

    x_d = din("x", [NSEQ * T, D])
    lbl_d = din("lb_logits", [NL, D])
    nmix_d = din("norm_mix", [NL, D])
    win_d = din("w_in", [NL, D, DIN])
    cw_d = din("conv_w", [NL, 4, D])
    cb_d = din("conv_b", [NL, D])
    wr_d = din("w_r", [NL, 4, 256, 256])
    br_d = din("b_r", [NL, D])
    wi_d = din("w_i", [NL, 4, 256, 256])
    bi_d = din("b_i", [NL, D])
    lam_d = din("lam", [NL, D])
    hgn_d = din("hg_norm", [NL, 128])
    wo_d = din("w_out", [NL, D, D])
    nmlp_d = din("norm_mlp", [NL, D])
    wu_d = din("w_up", [NL, D, DFF])
    wd_d = din("w_down", [NL, DFF, D])
    nf_d = din("norm_final", [1, D])
    out_d = nc.dram_tensor("out", [NSEQ * T, D], F32, kind="ExternalOutput").ap()

    def dscr(name, shape):
        return nc.dram_tensor(name, list(shape), BF16, kind="Internal").ap()

    win_s = dscr("win_s", [NL, 128, KC, DIN])
    wg_s = dscr("wg_s", [NL, 128, 4, 2, 2, 256])
    dg_s = dscr("dg_s", [NL, 128, 8, 4, 128])
    wo_s = dscr("wo_s", [NL, 128, KC, D])
    wu_s = dscr("wu_s", [NL, 128, KC, DFF])
    wd_s = dscr("wd_s", [NL, 128, 32, D])

    VROW, NV = _vrow(NL)
    tk = TK()
    es = contextlib.ExitStack()
    with es:
        def sem(name):
            return es.enter_context(nc.semaphore(name))

        sems = {e: sem("s_" + e) for e in ENGS}
        block_holder = []

        def sb(name, shape, dt):
            return es.enter_context(nc.sbuf_tensor(name, list(shape), dt))

        ident32 = sb("ident32", [128, 128], F32)
        identbf = sb("identbf", [128, 128], BF16)
        onesD = sb("onesD", [128, 128], BF16)
        onesH = sb("onesH", [128, 128], BF16)
        mask32 = sb("mask32", [32, NCH, 32], F32)
        vecT = sb("vecT", [128, KC, NV], F32)
        chalf = sb("chalf", [128, KC, NL], F32)
        cfull = sb("cfull", [128, KC, NL], F32)
        nchalf = sb("nchalf", [128, KC, NL], F32)
        epsb = sb("epsb", [128, 1], F32)
        brh = sb("brh", [128, KC, NL], F32)
        bih = sb("bih", [128, KC, NL], F32)
        kbp = sb("kbp", [128, KC, NL], F32)
        kbn = sb("kbn", [128, KC, NL], F32)
        kb1 = sb("kb1", [128, KC, NL], F32)
        Sc32 = sb("Sc32", [128, NL * 8, 128], F32)
        Sc16 = sb("Sc16", [128, NL * 8, 128], BF16)
        hcar = sb("hcar", [128, NL * 8], F32)
        xkeep = sb("xkeep", [128, NL * 8, 3], BF16)

        d_small = sem("d_small")

        def V(row, c):
            return vecT[:, c, row:row + 1]

        pes = contextlib.ExitStack()
        with pes:
            def psb(name, shape, dt):
                return pes.enter_context(nc.sbuf_tensor(name, list(shape), dt))

            st32 = [psb(f"st32_{i}", [128, 8192], F32) for i in range(2)]
            st16 = [psb(f"st16_{i}", [128, 8192], BF16) for i in range(2)]
            vrow = psb("vrow", [128, D], F32)
            d_ld = [sem("d_pl0"), sem("d_pl1")]
            d_st = [sem("d_ps0"), sem("d_ps1")]
            pps = pes.enter_context(nc.psum_tensor("pps", [128, 512], F32))
            block = pes.enter_context(nc.Block())

            def mk_ident(e):
                e.memset(ident32[:], 0.0)
                return e.affine_select(out=ident32[:], in_=ident32[:], pattern=[[-1, 128]],
                                       compare_op=ALU.not_equal, fill=1.0, base=0, channel_multiplier=1)
            tk.add("pool", mk_ident, writes=["ident32"])
            tk.add("pool", lambda e: e.tensor_copy(out=identbf[:], in_=ident32[:]), reads=["ident32"], writes=["identbf"])
            tk.add("pool", lambda e: e.memset(onesD[:], 1.0 / 1024.0), writes=["onesD"])
            tk.add("pool", lambda e: e.memset(onesH[:], 1.0 / 128.0), writes=["onesH"])

            def mk_mask(e):
                e.memset(mask32[:], 1.0)
                return e.affine_select(out=mask32[:], in_=mask32[:], pattern=[[0, NCH], [1, 32]],
                                       compare_op=ALU.is_ge, fill=0.0, base=0, channel_multiplier=-1)
            tk.add("pool", mk_mask, writes=["mask32"])
            tk.add("pool", lambda e: e.memset(vrow[:], 0.0), writes=["vrow"])
            tk.add("pool", lambda e: e.memset(epsb[:], EPS), writes=["epsb"])

            def ld_small(e):
                ins = None
                for nm, ap in (("norm_mix", nmix_d), ("norm_mlp", nmlp_d), ("conv_b", cb_d), ("b_r", br_d),
                               ("b_i", bi_d), ("lam", lam_d), ("lb_logits", lbl_d)):
                    r = VROW[nm]
                    ins = e.dma_start(out=vrow[r:r + NL, :], in_=ap[:, :]).then_inc(d_small, 16)
                r = VROW["conv_w"]
                ins = e.dma_start(out=vrow[r:r + 4 * NL, :], in_=cw_d.rearrange("l j d -> (l j) d")).then_inc(d_small, 16)
                r = VROW["norm_final"]
                ins = e.dma_start(out=vrow[r:r + 1, :], in_=nf_d[:, :]).then_inc(d_small, 16)
                r = VROW["hg_norm"]
                ins = e.dma_start(out=vrow[r:r + NL, 0:128], in_=hgn_d[:, :]).then_inc(d_small, 16)
                return ins
            tk.add("sp", ld_small, reads=["vrow"], writes=["vrow"], dma=d_small, ndma=10)
            for half in range(2):
                def tr_small(e, half=half):
                    ins = None
                    for cc in range(4):
                        c = half * 4 + cc
                        ins = e.transpose(pps[:, cc * 128:cc * 128 + NV], vrow[0:NV, c * 128:(c + 1) * 128], ident32[0:NV, 0:NV])
                    return ins
                tk.add("pe", tr_small, reads=["vrow", "ident32"], writes=["pps"])
                tk.add("dve", lambda e, half=half: e.tensor_copy(
                    out=vecT[:, half * 4:(half + 1) * 4, :],
                    in_=pps[:].rearrange("p (c n) -> p c n", c=4)[:, :, 0:NV]), reads=["pps"], writes=["vecT"])

            lam_v = vecT[:, :, VROW["lam"]:VROW["lam"] + NL]
            lbl_v = vecT[:, :, VROW["lb_logits"]:VROW["lb_logits"] + NL]
            tA = psb("tA", [128, KC, NL], F32)
            tB = psb("tB", [128, KC, NL], F32)
            tC = psb("tC", [128, KC, 1], F32)

            tk.add("act", lambda e: e.activation(out=tA[:], in_=lam_v, func=ACT.Exp, scale=-1.0), reads=["vecT"], writes=["tA"])
            tk.add("dve", lambda e: e.tensor_scalar(out=tB[:], in0=tA[:], scalar1=0.2, scalar2=-0.25, op0=ALU.mult, op1=ALU.add), reads=["tA"], writes=["tB"])
            tk.add("dve", lambda e: e.tensor_tensor(out=tB[:], in0=tB[:], in1=tA[:], op=ALU.mult), reads=["tA", "tB"], writes=["tB"])
            tk.add("dve", lambda e: e.tensor_scalar(out=tB[:], in0=tB[:], scalar1=1.0 / 3.0, scalar2=None, op0=ALU.add), reads=["tB"], writes=["tB"])
            tk.add("dve", lambda e: e.tensor_tensor(out=tB[:], in0=tB[:], in1=tA[:], op=ALU.mult), reads=["tA", "tB"], writes=["tB"])
            tk.add("dve", lambda e: e.tensor_scalar(out=tB[:], in0=tB[:], scalar1=-0.5, scalar2=None, op0=ALU.add), reads=["tB"], writes=["tB"])
            tk.add("dve", lambda e: e.tensor_tensor(out=tB[:], in0=tB[:], in1=tA[:], op=ALU.mult), reads=["tA", "tB"], writes=["tB"])
            tk.add("dve", lambda e: e.tensor_scalar(out=tB[:], in0=tB[:], scalar1=1.0, scalar2=None, op0=ALU.add), reads=["tB"], writes=["tB"])
            tk.add("dve", lambda e: e.tensor_tensor(out=tB[:], in0=tB[:], in1=tA[:], op=ALU.mult), reads=["tA", "tB"], writes=["tB"])
            tk.add("dve", lambda e: e.tensor_scalar(out=cfull[:], in0=tB[:], scalar1=-RG_C, scalar2=None, op0=ALU.mult), reads=["tB"], writes=["cfull"])
            tk.add("dve", lambda e: e.tensor_scalar(out=chalf[:], in0=tB[:], scalar1=-0.5 * RG_C, scalar2=None, op0=ALU.mult), reads=["tB"], writes=["chalf"])
            tk.add("dve", lambda e: e.tensor_scalar(out=nchalf[:], in0=tB[:], scalar1=0.5 * RG_C, scalar2=None, op0=ALU.mult), reads=["tB"], writes=["nchalf"])
            tk.add("dve", lambda e: e.tensor_scalar(out=brh[:], in0=vecT[:, :, VROW["b_r"]:VROW["b_r"] + NL], scalar1=0.5, scalar2=None, op0=ALU.mult), reads=["vecT"], writes=["brh"])
            tk.add("dve", lambda e: e.tensor_scalar(out=bih[:], in0=vecT[:, :, VROW["b_i"]:VROW["b_i"] + NL], scalar1=0.5, scalar2=None, op0=ALU.mult), reads=["vecT"], writes=["bih"])
            tk.add("dve", lambda e: e.tensor_reduce(out=tC[:], in_=lbl_v, axis=mybir.AxisListType.X, op=ALU.max), reads=["vecT"], writes=["tC"])
            tk.add("dve", lambda e: e.tensor_tensor(out=tA[:], in0=lbl_v, in1=tC[:].to_broadcast([128, KC, NL]), op=ALU.subtract), reads=["vecT", "tC", "tB"], writes=["tA"])
            tk.add("act", lambda e: e.activation(out=tA[:], in_=tA[:], func=ACT.Exp), reads=["tA"], writes=["tA"])
            tk.add("dve", lambda e: e.tensor_reduce(out=tC[:], in_=tA[:], axis=mybir.AxisListType.X, op=ALU.add), reads=["tA"], writes=["tC"])
            tk.add("dve", lambda e: e.reciprocal(out=tC[:], in_=tC[:]), reads=["tC"], writes=["tC"])
            tk.add("dve", lambda e: e.tensor_tensor(out=tA[:], in0=tA[:], in1=tC[:].to_broadcast([128, KC, NL]), op=ALU.mult), reads=["tA", "tC"], writes=["tA"])
            tk.add("dve", lambda e: e.memset(tB[:], 0.0), reads=["tB"], writes=["tB"])
            for l in range(1, NL):
                tk.add("dve", lambda e, l=l: e.tensor_tensor(out=tB[:, :, l:l + 1], in0=tB[:, :, l - 1:l], in1=tA[:, :, l:l + 1], op=ALU.add), reads=["tA", "tB"], writes=["tB"])
            tk.add("dve", lambda e: e.tensor_scalar(out=tB[:], in0=tB[:], scalar1=0.0, scalar2=1.0, op0=ALU.max, op1=ALU.min), reads=["tB"], writes=["tB"])
            tk.add("dve", lambda e: e.tensor_scalar(out=kbp[:], in0=tB[:], scalar1=-0.5, scalar2=0.5, op0=ALU.mult, op1=ALU.add), reads=["tB"], writes=["kbp"])
            tk.add("dve", lambda e: e.tensor_scalar(out=kbn[:], in0=tB[:], scalar1=0.5, scalar2=-0.5, op0=ALU.mult, op1=ALU.add), reads=["tB"], writes=["kbn"])
            tk.add("dve", lambda e: e.tensor_scalar(out=kb1[:], in0=tB[:], scalar1=0.5, scalar2=0.5, op0=ALU.mult, op1=ALU.add), reads=["tB"], writes=["kb1"])

            rnd = [0]
            scr_toks = []

            def conv_round(load_fn, cast_fns, store_fn, nld=1, nst=1, xreads=()):
                i = rnd[0] % 2
                rnd[0] += 1
                s32, s16 = st32[i], st16[i]
                tk.add("sp", lambda e: load_fn(e, s32, d_ld[i]), reads=[], writes=[f"st32_{i}"], dma=d_ld[i], ndma=nld)
                for k, (eng, fn) in enumerate(cast_fns):
                    tk.add(eng, lambda e, fn=fn: fn(e, s32, s16), reads=[f"st32_{i}"] + list(xreads), writes=[f"st16_{i}_{k}"])
                tok = f"scr{len(scr_toks)}"
                scr_toks.append(tok)
                tk.add("sp", lambda e: store_fn(e, s16, d_st[i]), reads=[f"st16_{i}_{k}" for k in range(8)] + [f"st32_{i}"],
                       writes=[tok], dma=d_st[i], ndma=nst)

            def cast_generic(eng, lo, hi):
                if eng == "act":
                    return ("act", lambda e, s32, s16: e.activation(out=s16[:, lo:hi], in_=s32[:, lo:hi], func=ACT.Copy))
                return (eng, lambda e, s32, s16: e.tensor_copy(out=s16[:, lo:hi], in_=s32[:, lo:hi]))

            def split3(n):
                a = (n * 3 // 8) // 128 * 128
                b = a + (n * 3 // 8) // 128 * 128
                return [cast_generic("dve", 0, a), cast_generic("act", a, b), cast_generic("pool", b, n)]

            RG_G = (0, 1, 6)
            HG_G = (2, 3, 4, 5, 7)
            for l in range(NL):
                for kc in range(KC):
                    def ld(e, s32, ds, l=l, kc=kc):
                        return e.dma_start(out=s32[:, :], in_=win_d[l, kc * 128:(kc + 1) * 128, :]).then_inc(ds, 16)
                    casts = []
                    engs = ["dve", "act", "dve", "pool", "act", "dve", "act", "dve"]
                    for g in range(8):
                        if g in RG_G:
                            gi = RG_G.index(g)

                            def cf(e, s32, s16, g=g, gi=gi, eng=engs[g]):
                                src = s32[:, g * 1024:(g + 1) * 1024].rearrange("p (b c j) -> p b c j", b=4, c=2)
                                dst = s16[:, 0:3072].rearrange("p (b x) -> p b x", b=4)[:, :, gi * 256:(gi + 1) * 256].rearrange("p b (c j) -> p b c j", c=2)
                                if eng == "act":
                                    return e.activation(out=dst, in_=src, func=ACT.Copy)
                                return e.tensor_copy(out=dst, in_=src)
                        else:
                            gi = HG_G.index(g)

                            def cf(e, s32, s16, g=g, gi=gi, eng=engs[g]):
                                src = s32[:, g * 1024:(g + 1) * 1024].rearrange("p (h j) -> p h j", h=8)
                                dst = s16[:, 3072:8192].rearrange("p (h x) -> p h x", h=8)[:, :, gi * 128:(gi + 1) * 128]
                                if eng == "act":
                                    return e.activation(out=dst, in_=src, func=ACT.Copy)
                                return e.tensor_copy(out=dst, in_=src)
                        casts.append((engs[g], cf))

                    def stf(e, s16, ds, l=l, kc=kc):
                        return e.dma_start(out=win_s[l, :, kc, :], in_=s16[:, :]).then_inc(ds, 16)
                    conv_round(ld, casts, stf)
                def ld(e, s32, ds, l=l):
                    return e.dma_start(out=s32[:, :].rearrange("p (k n) -> p k n", k=KC),
                                       in_=wo_d[l].rearrange("(k p) n -> p k n", p=128)).then_inc(ds, 16)

                def stf(e, s16, ds, l=l):
                    return e.dma_start(out=wo_s[l], in_=s16[:, :].rearrange("p (k n) -> p k n", k=KC)).then_inc(ds, 16)
                conv_round(ld, split3(8192), stf)
                for k2 in range(KC // 2):
                    def ld(e, s32, ds, l=l, k2=k2):
                        return e.dma_start(out=s32[:, :].rearrange("p (k n) -> p k n", k=2),
                                           in_=wu_d[l, k2 * 256:(k2 + 1) * 256, :].rearrange("(k p) n -> p k n", p=128)).then_inc(ds, 16)

                    def stf(e, s16, ds, l=l, k2=k2):
                        return e.dma_start(out=wu_s[l, :, 2 * k2:2 * k2 + 2, :], in_=s16[:, :].rearrange("p (k n) -> p k n", k=2)).then_inc(ds, 16)
                    conv_round(ld, split3(8192), stf)
                for f8 in range(4):
                    def ld(e, s32, ds, l=l, f8=f8):
                        return e.dma_start(out=s32[:, :].rearrange("p (k n) -> p k n", k=8),
                                           in_=wd_d[l, f8 * 1024:(f8 + 1) * 1024, :].rearrange("(k p) n -> p k n", p=128)).then_inc(ds, 16)

                    def stf(e, s16, ds, l=l, f8=f8):
                        return e.dma_start(out=wd_s[l, :, 8 * f8:8 * f8 + 8, :], in_=s16[:, :].rearrange("p (k n) -> p k n", k=8)).then_inc(ds, 16)
                    conv_round(ld, split3(8192), stf)
                def ld(e, s32, ds, l=l):
                    e.dma_start(out=s32[:, 0:2048].rearrange("p (k n) -> p k n", k=8),
                                in_=wr_d[l].rearrange("b (k p) n -> p (b k) n", p=128)).then_inc(ds, 16)
                    return e.dma_start(out=s32[:, 2048:4096].rearrange("p (k n) -> p k n", k=8),
                                       in_=wi_d[l].rearrange("b (k p) n -> p (b k) n", p=128)).then_inc(ds, 16)

                def dgf(e, s32, s16, l=l):
                    ins = None
                    for c in range(8):
                        for j in range(4):
                            o = 4096 + (c * 4 + j) * 128
                            ins = e.activation(out=s16[:, o:o + 128], in_=identbf[:], func=ACT.Copy,
                                               scale=V(VROW["conv_w"] + 4 * l + j, c))
                    return ins

                def stf(e, s16, ds, l=l):
                    for g in range(2):
                        e.dma_start(out=wg_s[l, :, :, g, :, :],
                                    in_=s16[:, g * 2048:(g + 1) * 2048].rearrange("p (b k n) -> p b k n", b=4, k=2)).then_inc(ds, 16)
                    return e.dma_start(out=dg_s[l], in_=s16[:, 4096:8192].rearrange("p (c j n) -> p c j n", c=8, j=4)).then_inc(ds, 16)
                conv_round(ld, [cast_generic("dve", 0, 4096), ("act", dgf)], stf, nld=2, nst=3, xreads=("vecT", "identbf"))
            tk.add("sp", lambda e: e.nop(), reads=scr_toks + ["vecT", "kbp", "kbn", "kb1", "chalf", "cfull", "nchalf", "brh", "bih", "mask32", "onesD", "onesH", "identbf", "epsb"], writes=["BAR"])
            for eng in ("pe", "act", "dve", "pool"):
                tk.add(eng, lambda e: e.nop(), reads=["BAR"], writes=[f"BAR_{eng}"])
            tk.finish(block, sems)
            stats0 = tk.stats

        tk2 = TK()
        tk2.dma_cnt = dict(tk.dma_cnt)
        mes = contextlib.ExitStack()
        with mes:
            def msb(name, shape, dt):
                return mes.enter_context(nc.sbuf_tensor(name, list(shape), dt))

            xres = msb("xres", [128, KC, TT], F32)
            hbuf = msb("hbuf", [128, KC, TT], BF16)
            ysum = msb("ysum", [128, KC, TT], BF16)
            iob = msb("iob", [128, 4, D], F32)
            hidden = iob[:].rearrange("p b d -> p (b d)").bitcast(BF16)[:, 0:16 * TT].rearrange("p (f t) -> p f t", f=16)
            NSLOT = 4
            slabs = [msb(f"slab{i}", [128, 8192 if i < 2 else 5120], BF16) for i in range(NSLOT)]
            d_slab = [sem(f"d_slab{i}") for i in range(NSLOT)]
            d_x = sem("d_x")
            d_o = sem("d_o")
            xa_bf = msb("xa_bf", [128, 2, TT + 3], BF16)
            xc32 = msb("xc32", [128, 2, TT], F32)
            xcbf = msb("xcbf", [128, 2, TT], BF16)
            rgA = msb("rgA", [128, 2, TT], F32)
            rgI = msb("rgI", [128, 2, TT], F32)
            rgS = msb("rgS", [128, 2, TT], F32)
            rgG = msb("rgG", [128, 2, TT], F32)
            qs = msb("qs", [128, TT], F32)
            fz = msb("fz", [128, TT], F32)
            kk = msb("kk", [128, TT], F32)
            fm = msb("fm", [128, TT], F32)
            Pp = msb("Pp", [128, TT], F32)
            Rr = msb("Rr", [128, TT], F32)
            qt = msb("qt", [128, TT], BF16)
            kt = msb("kt", [128, TT], BF16)
            vF = msb("vF", [128, TT], BF16)
            kT32 = msb("kT32", [32, NCH, 128], BF16)
            v32 = msb("v32", [32, NCH, 128], BF16)
            Sm32 = msb("Sm32", [32, NCH, 32], BF16)
            Uall = msb("Uall", [128, NCH, 128], F32)
            Sbf = msb("Sbf", [128, NCH, 128], BF16)
            o2 = msb("o2", [128, TT], BF16)
            msq, onb = qs, Rr
            sgb = msb("sgb", [128, TT], F32)
            tmb = msb("tmb", [128, TT], F32)
            sqb = msb("sqb", [128, TT], BF16)
            rstd = msb("rstd", [128, TT], F32)
            rl = [msb(f"rl{i}", [128, TT], BF16) for i in range(2)]
            banks = [mes.enter_context(nc.psum_tensor(f"bank{i}", [128, TT], F32)) for i in range(8)]
            block = mes.enter_context(nc.Block())
            T2 = tk2

            cur = [None]

            DEFC = {"pe": 1.8, "act": 0.65, "dve": 0.6, "pool": 1.0, "sp": 5.0}

            def A(eng, emit, reads=(), writes=(), dma=None, ndma=1, tag=None, cost=None):
                if cur[0] is not None:
                    cur[0].append((eng, emit, reads, writes, dma, ndma, tag, DEFC[eng] if cost is None else cost))
                else:
                    T2.add(eng, emit, reads=reads, writes=writes, dma=dma, ndma=ndma)

            def flush_merged(L1, L2):
                n1, n2 = len(L1), len(L2)
                i = j = 0
                merged = []
                efree = {e: 0.0 for e in ENGS}
                wdone = {}
                rdone = {}

                def est(op):
                    eng, _, reads, writes, dma, _, _, cost = op
                    t = efree[eng]
                    for tk_ in reads:
                        t = max(t, wdone.get(tk_, 0.0) + 0.15)
                    for tk_ in writes:
                        t = max(t, wdone.get(tk_, 0.0), rdone.get(tk_, 0.0) + 0.1)
                    return t

                def commit(op):
                    eng, _, reads, writes, dma, _, _, cost = op
                    t0 = est(op)
                    if eng == "sp":
                        efree[eng] = t0 + 0.1
                    else:
                        efree[eng] = t0 + cost
                    t1 = t0 + cost
                    for tk_ in reads:
                        rdone[tk_] = max(rdone.get(tk_, 0.0), t1)
                    for tk_ in writes:
                        wdone[tk_] = t1
                    merged.append(op)

                while i < n1 or j < n2:
                    if j >= n2:
                        commit(L1[i]); i += 1
                    elif i >= n1:
                        commit(L2[j]); j += 1
                    else:
                        a, b = est(L1[i]), est(L2[j])
                        if a < b:
                            commit(L1[i]); i += 1
                        else:
                            commit(L2[j]); j += 1
                seen = {}
                for op in merged:
                    tag = op[6]
                    if tag is not None:
                        kind, key, flags = tag
                        if key not in seen:
                            seen[key] = kind
                            flags["first"] = kind
                for (eng, emit, reads, writes, dma, ndma, tag, cost) in merged:
                    T2.add(eng, emit, reads=reads, writes=writes, dma=dma, ndma=ndma)

            POOLS = {"all": list(range(8)), "rg": [0, 1, 2], "hg": [3, 4, 5, 6, 7]}
            bank_rr = {"all": 0, "rg": 0, "hg": 0}

            def nb(pool="all"):
                lst = POOLS[pool]
                i = lst[bank_rr[pool] % len(lst)]
                bank_rr[pool] += 1
                return banks[i], f"bank{i}"

            slot_rr = {"big": 0, "small": 0}

            def nslot(kind="big"):
                i = slot_rr[kind] % 2 + (0 if kind == "big" else 2)
                slot_rr[kind] += 1
                return i

            A("pool", lambda e: e.memset(fm[:], 0.0), writes=["fm"])
            A("pool", lambda e: e.memset(xa_bf[:], 0.0), writes=["xa_bf"])

            def rms_rstd(src_chunks, reads, eps, ones, out_rstd, wtok):
                bk, bt = nb()
                n = len(src_chunks)
                for i, (ap, tok) in enumerate(src_chunks):
                    A("act", lambda e, ap=ap: e.activation(out=sqb[:], in_=ap, func=ACT.Square), reads=[tok], writes=["sqb"])
                    A("pe", lambda e, i=i: e.matmul(bk[:], ones[:], sqb[:], start=(i == 0), stop=(i == n - 1)), reads=["sqb"], writes=[bt])
                A("act", lambda e: e.activation(out=out_rstd, in_=bk[:], func=ACT.Ln, bias=epsb[:, 0:1], scale=1.0), reads=[bt], writes=[wtok])
                A("act", lambda e: e.activation(out=out_rstd, in_=out_rstd, func=ACT.Exp, scale=-0.5), reads=[wtok], writes=[wtok])

            for sq_i in range(NSEQ):
                A("pool", lambda e: e.memset(Sc32[:], 0.0), writes=[f"Sc32_{l}_{h}" for l in range(NL) for h in range(8)])
                A("pool", lambda e: e.memset(Sc16[:], 0.0), writes=[f"Sc16_{l}_{h}" for l in range(NL) for h in range(8)])
                A("pool", lambda e: e.memset(hcar[:], 0.0), writes=[f"hcar_{l}_{c}" for l in range(NL) for c in range(8)])
                A("pool", lambda e: e.memset(xkeep[:], 0.0), writes=[f"xkeep_{l}_{c}" for l in range(NL) for c in range(8)])
                for ti in range(NTILE):
                    row0 = sq_i * T + ti * TT
                    A("sp", lambda e, row0=row0: e.dma_start(out=iob[:], in_=x_d[row0:row0 + TT, :].rearrange("(b p) d -> p b d", p=128)).then_inc(d_x, 16),
                           writes=["iob"] + [f"hid{f}" for f in range(16)] + [f"io_{b}_{cg}" for b in range(4) for cg in range(2)], dma=d_x)
                    for c in range(KC):
                        bk, bt = nb()

                        def trx(e, c=c, bk=bk):
                            ins = None
                            for b in range(4):
                                ins = e.transpose(bk[:, b * 128:(b + 1) * 128], iob[:, b, c * 128:(c + 1) * 128], ident32[:])
                            return ins
                        A("pe", trx, reads=["iob"], writes=[bt])
                        if c % 2 == 0:
                            A("act", lambda e, c=c, bk=bk: e.activation(out=xres[:, c, :], in_=bk[:], func=ACT.Copy), reads=[bt], writes=[f"xres{c}"])
                        else:
                            A("dve", lambda e, c=c, bk=bk: e.tensor_copy(out=xres[:, c, :], in_=bk[:]), reads=[bt], writes=[f"xres{c}"])

                    for l in range(NL):
                        rms_rstd([(xres[:, c, :], f"xres{c}") for c in range(KC)], None, EPS, onesD, rstd[:], "rstd")
                        for c in range(KC):
                            A("dve", lambda e, c=c, l=l: e.scalar_tensor_tensor(
                                out=hbuf[:, c, :], in0=xres[:, c, :], scalar=V(VROW["norm_mix"] + l, c), in1=rstd[:],
                                op0=ALU.mult, op1=ALU.mult), reads=[f"xres{c}", "rstd"], writes=[f"h{c}"])
                        hreads = [f"h{c}" for c in range(KC)]

                        def win_mm(bk, slab, col0, W_):
                            def f(e):
                                ins = None
                                sv = slab[:, :]
                                for kc in range(KC):
                                    ins = e.matmul(bk[:], sv[:, kc * W_ + col0: kc * W_ + col0 + 128], hbuf[:, kc, :],
                                                   start=(kc == 0), stop=(kc == KC - 1))
                                return ins
                            return f

                        rg_lists = []
                        yflags = {c: {"first": None} for c in range(KC)}
                        for b in range(4):
                            cur[0] = []
                            rg_lists.append(cur[0])
                            si = nslot()
                            slab = slabs[si]
                            stok = f"slab{si}"

                            def ld_rg(e, slab=slab, l=l, b=b, ds=d_slab[si]):
                                e.dma_start(out=slab[:, 0:6144].rearrange("p (k n) -> p k n", k=KC),
                                            in_=win_s[l, :, :, b * 768:(b + 1) * 768]).then_inc(ds, 16)
                                e.dma_start(out=slab[:, 6144:7168].rearrange("p (g k n) -> p g k n", g=2, k=2),
                                            in_=wg_s[l, :, b, :, :, :]).then_inc(ds, 16)
                                return e.dma_start(out=slab[:, 7168:8192].rearrange("p (c j n) -> p c j n", c=2, j=4),
                                                   in_=dg_s[l, :, 2 * b:2 * b + 2, :, :]).then_inc(ds, 16)
                            A("sp", ld_rg, writes=[stok], dma=d_slab[si], ndma=3)
                            chs = (2 * b, 2 * b + 1)
                            for ci, c in enumerate(chs):
                                A("pool", lambda e, ci=ci, c=c, l=l: e.tensor_copy(out=xa_bf[:, ci, 0:3], in_=xkeep[:, l * 8 + c, :]),
                                       reads=[f"xkeep_{l}_{c}"], writes=[f"xa_halo{ci}"])
                                bk, bt = nb("rg")
                                A("pe", win_mm(bk, slab, 0 * 256 + ci * 128, 768), reads=hreads + [stok], writes=[bt])
                                A("act", lambda e, ci=ci, bk=bk: e.activation(out=xa_bf[:, ci, 3:3 + TT], in_=bk[:], func=ACT.Copy),
                                       reads=[bt], writes=[f"xa_bf{ci}"])
                                bk2, bt2 = nb("rg")

                                def convmm(e, ci=ci, bk2=bk2, slab=slab):
                                    ins = None
                                    for j in range(4):
                                        o = 7168 + (ci * 4 + j) * 128
                                        ins = e.matmul(bk2[:], slab[:, o:o + 128], xa_bf[:, ci, j:j + TT], start=(j == 0), stop=(j == 3))
                                    return ins
                                A("pe", convmm, reads=[f"xa_bf{ci}", f"xa_halo{ci}", stok], writes=[bt2], cost=0.9)
                                A("pool", lambda e, ci=ci, c=c, l=l: e.tensor_copy(out=xkeep[:, l * 8 + c, :], in_=xa_bf[:, ci, TT:TT + 3]),
                                       reads=[f"xa_bf{ci}"], writes=[f"xkeep_{l}_{c}"])
                                A("act", lambda e, ci=ci, c=c, l=l, bk2=bk2: e.activation(out=xc32[:, ci, :], in_=bk2[:], func=ACT.Identity,
                                                                                            bias=V(VROW["conv_b"] + l, c), scale=1.0),
                                       reads=[bt2], writes=[f"xc32_{ci}"])
                                A("pool", lambda e, ci=ci: e.tensor_copy(out=xcbf[:, ci, :], in_=xc32[:, ci, :]), reads=[f"xc32_{ci}"], writes=[f"xcbf{ci}"])
                            for ci, c in enumerate(chs):
                                for g in range(2):
                                    bk, bt = nb("rg")

                                    def gmm(e, ci=ci, g=g, bk=bk, slab=slab):
                                        ins = None
                                        for k2 in range(2):
                                            o = 6144 + (g * 2 + k2) * 256 + ci * 128
                                            ins = e.matmul(bk[:], slab[:, o:o + 128], xcbf[:, k2, :], start=(k2 == 0), stop=(k2 == 1))
                                        return ins
                                    A("pe", gmm, reads=["xcbf0", "xcbf1", stok], writes=[bt], cost=0.45)
                                    dst = rgA if g == 0 else rgI
                                    bsrc = brh if g == 0 else bih
                                    A("act", lambda e, ci=ci, c=c, l=l, bk=bk, dst=dst, bsrc=bsrc: e.activation(
                                        out=dst[:, ci, :], in_=bk[:], func=ACT.Tanh, bias=bsrc[:, c, l:l + 1], scale=0.5),
                                        reads=[bt], writes=[("rgA" if g == 0 else "rgI") + str(ci)])
                            for ci, c in enumerate(chs):
                                A("act", lambda e, ci=ci, c=c, l=l: e.activation(out=rgS[:, ci, :], in_=rgA[:, ci, :], func=ACT.Tanh,
                                                                                     bias=nchalf[:, c, l:l + 1], scale=nchalf[:, c, l:l + 1]),
                                       reads=[f"rgA{ci}"], writes=[f"rgS{ci}"])
                                A("act", lambda e, ci=ci, c=c, l=l: e.activation(out=rgA[:, ci, :], in_=rgA[:, ci, :], func=ACT.Exp,
                                                                                     bias=chalf[:, c, l:l + 1], scale=chalf[:, c, l:l + 1]),
                                       reads=[f"rgA{ci}"], writes=[f"rgA{ci}"])
                            for ci, c in enumerate(chs):
                                A("act", lambda e, ci=ci: e.activation(out=rgG[:, ci, :], in_=rgA[:, ci, :], func=ACT.Square),
                                  reads=[f"rgA{ci}"], writes=[f"rgG{ci}"])
                                A("dve", lambda e, ci=ci: e.scalar_tensor_tensor(out=rgS[:, ci, :], in0=rgG[:, ci, :], scalar=1.0, in1=rgS[:, ci, :],
                                                                                 op0=ALU.add, op1=ALU.mult),
                                  reads=[f"rgS{ci}", f"rgG{ci}"], writes=[f"rgS{ci}"])
                                A("act", lambda e, ci=ci: e.activation(out=rgS[:, ci, :], in_=rgS[:, ci, :], func=ACT.Sqrt, scale=0.25),
                                  reads=[f"rgS{ci}"], writes=[f"rgS{ci}"])
                                A("dve", lambda e, ci=ci: e.scalar_tensor_tensor(out=rgI[:, ci, :], in0=rgI[:, ci, :], scalar=1.0, in1=xc32[:, ci, :],
                                                                                      op0=ALU.add, op1=ALU.mult),
                                       reads=[f"rgI{ci}", f"xc32_{ci}"], writes=[f"rgI{ci}"])
                                A("dve", lambda e, ci=ci: e.tensor_tensor(out=rgI[:, ci, :], in0=rgI[:, ci, :], in1=rgS[:, ci, :], op=ALU.mult),
                                       reads=[f"rgI{ci}", f"rgS{ci}"], writes=[f"rgI{ci}"])
                                A("dve", lambda e, ci=ci, c=c, l=l: e.tensor_tensor_scan(out=xc32[:, ci, :], data0=rgA[:, ci, :], data1=rgI[:, ci, :],
                                                                                             initial=hcar[:, l * 8 + c:l * 8 + c + 1],
                                                                                             op0=ALU.mult, op1=ALU.add),
                                       reads=[f"rgA{ci}", f"rgI{ci}", f"hcar_{l}_{c}"], writes=[f"xc32_{ci}"], cost=1.3)
                                A("dve", lambda e, ci=ci, c=c, l=l: e.tensor_copy(out=hcar[:, l * 8 + c:l * 8 + c + 1], in_=xc32[:, ci, TT - 1:TT]),
                                       reads=[f"xc32_{ci}"], writes=[f"hcar_{l}_{c}"])
                            for ci, c in enumerate(chs):
                                bk, bt = nb("rg")
                                A("pe", win_mm(bk, slab, 1 * 256 + ci * 128, 768), reads=hreads + [stok], writes=[bt])
                                A("act", lambda e, ci=ci, bk=bk: e.activation(out=rgG[:, ci, :], in_=bk[:], func=ACT.Gelu_apprx_tanh),
                                       reads=[bt], writes=[f"rgG{ci}"])
                                bk2, bt2 = nb("rg")
                                A("pe", win_mm(bk2, slab, 2 * 256 + ci * 128, 768), reads=hreads + [stok], writes=[bt2])
                                A("act", lambda e, ci=ci, bk2=bk2: e.activation(out=rgS[:, ci, :], in_=bk2[:], func=ACT.Tanh, scale=0.5),
                                       reads=[bt2], writes=[f"rgS{ci}"])
                                A("dve", lambda e, ci=ci: e.tensor_tensor(out=rgG[:, ci, :], in0=rgG[:, ci, :], in1=xc32[:, ci, :], op=ALU.mult),
                                       reads=[f"rgG{ci}", f"xc32_{ci}"], writes=[f"rgG{ci}"])
                                A("dve", lambda e, ci=ci: e.scalar_tensor_tensor(out=rgG[:, ci, :], in0=rgS[:, ci, :], scalar=1.0, in1=rgG[:, ci, :],
                                                                                 op0=ALU.add, op1=ALU.mult),
                                  reads=[f"rgS{ci}", f"rgG{ci}"], writes=[f"rgG{ci}"])

                                def yrg(e, ci=ci, c=c, fl=yflags[c]):
                                    if fl["first"] == "rg":
                                        return e.tensor_copy(out=ysum[:, c, :], in_=rgG[:, ci, :])
                                    return e.tensor_tensor(out=ysum[:, c, :], in0=ysum[:, c, :], in1=rgG[:, ci, :], op=ALU.add)
                                A("dve", yrg, reads=[f"rgG{ci}", f"ysum{c}"], writes=[f"ysum{c}"], tag=("rg", c, yflags[c]))
                            cur[0] = None

                        hg_lists = []
                        for hh in range(8):
                            cur[0] = []
                            hg_lists.append(cur[0])
                            si = nslot("small")
                            slab = slabs[si]
                            stok = f"slab{si}"

                            def ld_hg(e, slab=slab, l=l, hh=hh, ds=d_slab[si]):
                                return e.dma_start(out=slab[:, 0:5120].rearrange("p (k n) -> p k n", k=KC),
                                                   in_=win_s[l, :, :, 3072 + hh * 640:3072 + (hh + 1) * 640]).then_inc(ds, 16)
                            A("sp", ld_hg, writes=[stok], dma=d_slab[si])
                            sk = f"Sc_{l}_{hh}"
                            bk, bt = nb("hg")
                            A("pe", win_mm(bk, slab, 0, 640), reads=hreads + [stok], writes=[bt])
                            A("act", lambda e, bk=bk: e.activation(out=qs[:], in_=bk[:], func=ACT.Silu), reads=[bt], writes=["qs"])
                            bk, bt = nb("hg")
                            A("pe", win_mm(bk, slab, 128, 640), reads=hreads + [stok], writes=[bt])
                            A("act", lambda e, bk=bk: e.activation(out=fz[:], in_=bk[:], func=ACT.Tanh, scale=0.5), reads=[bt], writes=["fz"])
                            bk, bt = nb("hg")
                            A("pe", win_mm(bk, slab, 256, 640), reads=hreads + [stok], writes=[bt])
                            A("act", lambda e, bk=bk: e.activation(out=vF[:], in_=bk[:], func=ACT.Copy), reads=[bt], writes=["vF"])
                            bk, bt = nb("hg")
                            A("pe", win_mm(bk, slab, 384, 640), reads=hreads + [stok], writes=[bt])
                            A("act", lambda e, bk=bk: e.activation(out=sgb[:], in_=bk[:], func=ACT.Silu), reads=[bt], writes=["sgb"])
                            bk, bt = nb("hg")
                            A("pe", win_mm(bk, slab, 512, 640), reads=hreads + [stok], writes=[bt])
                            A("act", lambda e, bk=bk: e.activation(out=tmb[:], in_=bk[:], func=ACT.Tanh, scale=0.5), reads=[bt], writes=["tmb"])
                            A("dve", lambda e, l=l, hh=hh: e.tensor_scalar(out=kk[:], in0=fz[:], scalar1=kbn[:, hh, l:l + 1], scalar2=kbp[:, hh, l:l + 1],
                                                                                op0=ALU.mult, op1=ALU.add), reads=["fz"], writes=["kk"])
                            A("dve", lambda e, l=l, hh=hh: e.tensor_scalar(out=fz[:], in0=fz[:], scalar1=kbp[:, hh, l:l + 1], scalar2=kb1[:, hh, l:l + 1],
                                                                                op0=ALU.mult, op1=ALU.add), reads=["fz", "kk"], writes=["fz"])
                            fv = fz[:].rearrange("p (c i) -> p c i", i=32)
                            fmv = fm[:].rearrange("p (c i) -> p c i", i=32)
                            A("pool", lambda e: e.tensor_copy(out=fmv[:, :, 0:1], in_=fv[:, :, 0:1]), reads=["fz"], writes=["fm"])
                            A("dve", lambda e: e.tensor_tensor_scan(out=Pp[:], data0=fz[:], data1=fm[:], initial=1.0, op0=ALU.mult, op1=ALU.max),
                                   reads=["fz", "fm"], writes=["Pp"], cost=1.3)
                            A("dve", lambda e: e.reciprocal(out=Rr[:], in_=Pp[:]), reads=["Pp"], writes=["Rr"], cost=3.4)
                            A("dve", lambda e: e.tensor_tensor(out=qt[:], in0=qs[:], in1=Pp[:], op=ALU.mult), reads=["qs", "Pp"], writes=["qt"])
                            A("dve", lambda e: e.tensor_tensor(out=kt[:], in0=kk[:], in1=Rr[:], op=ALU.mult), reads=["kk", "Rr"], writes=["kt"])
                            bkT, btT = nb("hg")
                            kTp = bkT[:].bitcast(BF16)

                            def trk(e, kTp=kTp):
                                ins = None
                                for c in range(8):
                                    ins = e.transpose(kTp[0:32, c * 128:(c + 1) * 128], kt[:, c * 32:(c + 1) * 32], identbf[:])
                                return ins

                            def trk2(e, kTp=kTp):
                                ins = None
                                for c in range(8, 16):
                                    ins = e.transpose(kTp[0:32, (c - 8) * 128:(c - 7) * 128], kt[:, c * 32:(c + 1) * 32], identbf[:])
                                return ins
                            A("pe", trk, reads=["kt", "identbf"], writes=[btT], cost=0.6)
                            A("act", lambda e, kTp=kTp: e.activation(out=kT32[:, 0:8, :], in_=kTp[0:32, :].rearrange("p (c n) -> p c n", c=8), func=ACT.Copy),
                                   reads=[btT], writes=["kT32a"])
                            bkT2, btT2 = nb("hg")
                            kTp2 = bkT2[:].bitcast(BF16)
                            A("pe", lambda e, kTp2=kTp2: trk2(e, kTp2), reads=["kt", "identbf"], writes=[btT2], cost=0.6)
                            A("act", lambda e, kTp2=kTp2: e.activation(out=kT32[:, 8:16, :], in_=kTp2[0:32, :].rearrange("p (c n) -> p c n", c=8), func=ACT.Copy),
                                   reads=[btT2], writes=["kT32b"])
                            bkV, btV = nb("hg")
                            vTp = bkV[:].bitcast(BF16)

                            def trv(e, vTp=vTp, lo=0):
                                ins = None
                                for c in range(lo, lo + 8):
                                    ins = e.transpose(vTp[0:32, (c - lo) * 128:(c - lo + 1) * 128], vF[:, c * 32:(c + 1) * 32], identbf[:])
                                return ins
                            A("pe", lambda e, vTp=vTp: trv(e, vTp, 0), reads=["vF", "identbf"], writes=[btV], cost=0.6)
                            A("dve", lambda e, vTp=vTp: e.tensor_copy(out=v32[:, 0:8, :], in_=vTp[0:32, :].rearrange("p (c n) -> p c n", c=8)),
                                   reads=[btV], writes=["v32a"])
                            bkV2, btV2 = nb("hg")
                            vTp2 = bkV2[:].bitcast(BF16)
                            A("pe", lambda e, vTp2=vTp2: trv(e, vTp2, 8), reads=["vF", "identbf"], writes=[btV2], cost=0.6)
                            A("dve", lambda e, vTp2=vTp2: e.tensor_copy(out=v32[:, 8:16, :], in_=vTp2[0:32, :].rearrange("p (c n) -> p c n", c=8)),
                                   reads=[btV2], writes=["v32b"])
                            bkS, btS = nb("hg")

                            def scmm(e, bkS=bkS):
                                ins = None
                                for c in range(NCH):
                                    ins = e.matmul(bkS[0:32, c * 32:(c + 1) * 32], kt[:, c * 32:(c + 1) * 32], qt[:, c * 32:(c + 1) * 32], start=True, stop=True)
                                return ins
                            A("pe", scmm, reads=["kt", "qt"], writes=[btS], cost=1.1)
                            A("dve", lambda e, bkS=bkS: e.tensor_tensor(out=Sm32[:], in0=bkS[0:32, :].rearrange("p (c t) -> p c t", c=NCH), in1=mask32[:], op=ALU.mult),
                                   reads=[btS], writes=["Sm32"])
                            abanks = []
                            for q4 in range(4):
                                bkA, btA = nb("hg")
                                abanks.append((bkA, btA))

                                def amm(e, bkA=bkA, q4=q4):
                                    ins = None
                                    for cc in range(4):
                                        c = q4 * 4 + cc
                                        ins = e.matmul(bkA[:, cc * 128:(cc + 1) * 128], kT32[:, c, :], v32[:, c, :], start=True, stop=True)
                                    return ins
                                A("pe", amm, reads=["kT32a", "kT32b", "v32a", "v32b"], writes=[btA], cost=0.5)
                            ecol = Pp[:].rearrange("p (c i) -> p c i", i=32)
                            for c in range(NCH):
                                bkA, btA = abanks[c // 4]
                                a_ap = bkA[:, (c % 4) * 128:(c % 4 + 1) * 128]
                                if c == 0:
                                    A("dve", lambda e, a_ap=a_ap, l=l, hh=hh: e.tensor_tensor(out=Uall[:, 0, :], in0=Sc32[:, l * 8 + hh, :], in1=a_ap, op=ALU.add),
                                           reads=[btA, f"Sc32_{l}_{hh}"], writes=["U0"])
                                else:
                                    A("dve", lambda e, a_ap=a_ap, c=c: e.scalar_tensor_tensor(out=Uall[:, c, :], in0=Uall[:, c - 1, :], scalar=ecol[:, c - 1, 31:32],
                                                                                                  in1=a_ap, op0=ALU.mult, op1=ALU.add),
                                           reads=[btA, f"U{c - 1}", "Pp"], writes=[f"U{c}"], cost=0.48)
                            ureads = [f"U{c}" for c in range(NCH)]
                            A("pool", lambda e: e.tensor_tensor(out=Sbf[:], in0=Uall[:], in1=ecol[:, :, 31:32].to_broadcast([128, NCH, 128]), op=ALU.mult),
                                   reads=ureads + ["Pp"], writes=["Sbf"], cost=4.5)
                            bkO, btO = nb("hg")

                            def omm(e, bkO=bkO, l=l, hh=hh):
                                ins = None
                                for c in range(NCH):
                                    e.matmul(bkO[:, c * 32:(c + 1) * 32], v32[:, c, :], Sm32[:, c, :], start=True, stop=False)
                                    st_ap = Sc16[:, l * 8 + hh, :] if c == 0 else Sbf[:, c - 1, :]
                                    ins = e.matmul(bkO[:, c * 32:(c + 1) * 32], st_ap, qt[:, c * 32:(c + 1) * 32], start=False, stop=True)
                                return ins
                            A("pe", omm, reads=["v32a", "v32b", "Sm32", "Sbf", "qt", f"Sc16_{l}_{hh}"], writes=[btO], cost=4.3)
                            A("dve", lambda e, l=l, hh=hh: e.tensor_scalar(out=Sc32[:, l * 8 + hh, :], in0=Uall[:, NCH - 1, :], scalar1=ecol[:, NCH - 1, 31:32], scalar2=None,
                                                                                op0=ALU.mult), reads=[f"U{NCH - 1}", "Pp"], writes=[f"Sc32_{l}_{hh}"])
                            A("pool", lambda e, l=l, hh=hh: e.tensor_copy(out=Sc16[:, l * 8 + hh, :], in_=Sbf[:, NCH - 1, :]), reads=["Sbf"], writes=[f"Sc16_{l}_{hh}"])
                            A("act", lambda e, bkO=bkO: e.activation(out=o2[:], in_=bkO[:], func=ACT.Square), reads=[btO], writes=["o2"])
                            bkM, btM = nb("hg")
                            A("pe", lambda e, bkM=bkM: e.matmul(bkM[:], onesH[:], o2[:], start=True, stop=True), reads=["o2"], writes=[btM], cost=0.25)
                            A("act", lambda e, bkM=bkM: e.activation(out=msq[:], in_=bkM[:], func=ACT.Ln, bias=epsb[:, 0:1], scale=1.0), reads=[btM], writes=["qs"])
                            A("act", lambda e: e.activation(out=msq[:], in_=msq[:], func=ACT.Exp, scale=-0.5), reads=["qs"], writes=["qs"])
                            A("dve", lambda e, bkO=bkO, l=l: e.scalar_tensor_tensor(out=onb[:], in0=bkO[:], scalar=vecT[:, 0, VROW["hg_norm"] + l:VROW["hg_norm"] + l + 1],
                                                                                         in1=msq[:], op0=ALU.mult, op1=ALU.mult),
                                   reads=[btO, "qs"], writes=["Rr"])
                            A("dve", lambda e: e.tensor_tensor(out=onb[:], in0=onb[:], in1=sgb[:], op=ALU.mult), reads=["Rr", "sgb"], writes=["Rr"])
                            A("dve", lambda e: e.scalar_tensor_tensor(out=onb[:], in0=tmb[:], scalar=1.0, in1=onb[:], op0=ALU.add, op1=ALU.mult),
                                   reads=["Rr", "tmb"], writes=["Rr"])
                            def yhg(e, hh=hh, fl=yflags[hh]):
                                if fl["first"] == "hg":
                                    return e.tensor_copy(out=ysum[:, hh, :], in_=onb[:])
                                return e.tensor_tensor(out=ysum[:, hh, :], in0=ysum[:, hh, :], in1=onb[:], op=ALU.add)
                            A("dve", yhg, reads=["Rr", f"ysum{hh}"], writes=[f"ysum{hh}"], tag=("hg", hh, yflags[hh]))
                            cur[0] = None
                        for b in range(4):
                            flush_merged(rg_lists[b], hg_lists[2 * b] + hg_lists[2 * b + 1])

                        si = nslot()
                        slab = slabs[si]
                        stok = f"slab{si}"
                        A("sp", lambda e, slab=slab, l=l, ds=d_slab[si]: e.dma_start(
                            out=slab[:, :].rearrange("p (k n) -> p k n", k=KC), in_=wo_s[l]).then_inc(ds, 16), writes=[stok], dma=d_slab[si])
                        yreads = [f"ysum{c}" for c in range(KC)]
                        for m in range(KC):
                            bk, bt = nb()

                            def womm(e, bk=bk, slab=slab, m=m):
                                ins = None
                                for kc in range(KC):
                                    ins = e.matmul(bk[:], slab[:, kc * D + m * 128: kc * D + (m + 1) * 128], ysum[:, kc, :], start=(kc == 0), stop=(kc == KC - 1))
                                return ins
                            A("pe", womm, reads=yreads + [stok], writes=[bt])
                            A("dve", lambda e, bk=bk, m=m: e.scalar_tensor_tensor(out=xres[:, m, :], in0=bk[:], scalar=0.5, in1=xres[:, m, :], op0=ALU.mult, op1=ALU.add),
                                   reads=[bt, f"xres{m}"], writes=[f"xres{m}"])

                        rms_rstd([(xres[:, c, :], f"xres{c}") for c in range(KC)], None, EPS, onesD, rstd[:], "rstd")
                        for c in range(KC):
                            A("dve", lambda e, c=c, l=l: e.scalar_tensor_tensor(
                                out=hbuf[:, c, :], in0=xres[:, c, :], scalar=V(VROW["norm_mlp"] + l, c), in1=rstd[:],
                                op0=ALU.mult, op1=ALU.mult), reads=[f"xres{c}", "rstd"], writes=[f"h{c}"])
                        for sg in range(2):
                            uslabs = []
                            for half in range(2):
                                fg = sg * 2 + half
                                si = nslot()
                                A("sp", lambda e, slab=slabs[si], l=l, fg=fg, ds=d_slab[si]: e.dma_start(
                                    out=slab[:, :].rearrange("p (k n) -> p k n", k=KC), in_=wu_s[l, :, :, fg * 1024:(fg + 1) * 1024]).then_inc(ds, 16),
                                    writes=[f"slab{si}"], dma=d_slab[si])
                                uslabs.append((slabs[si], f"slab{si}"))
                                for f8 in range(8):
                                    f = half * 8 + f8
                                    bk, bt = nb()

                                    def upmm(e, bk=bk, slab=slabs[si], f8=f8):
                                        ins = None
                                        for kc in range(KC):
                                            ins = e.matmul(bk[:], slab[:, kc * 1024 + f8 * 128: kc * 1024 + (f8 + 1) * 128], hbuf[:, kc, :], start=(kc == 0), stop=(kc == KC - 1))
                                        return ins
                                    A("pe", upmm, reads=hreads + [f"slab{si}"], writes=[bt])
                                    r_ = rl[f % 2]
                                    A("act", lambda e, bk=bk, r_=r_: e.activation(out=r_[:], in_=bk[:], func=ACT.Relu), reads=[bt], writes=[f"rl{f % 2}"])
                                    A("pool", lambda e, r_=r_, f=f: e.tensor_tensor(out=hidden[:, f, :], in0=r_[:], in1=r_[:], op=ALU.mult),
                                           reads=[f"rl{f % 2}"], writes=[f"hid{f}", "iob"])
                            dslabs = []
                            for half in range(2):
                                fg = sg * 2 + half
                                si = nslot()
                                A("sp", lambda e, slab=slabs[si], l=l, fg=fg, ds=d_slab[si]: e.dma_start(
                                    out=slab[:, :].rearrange("p (k n) -> p k n", k=8), in_=wd_s[l, :, fg * 8:(fg + 1) * 8, :]).then_inc(ds, 16),
                                    writes=[f"slab{si}"], dma=d_slab[si])
                                dslabs.append((slabs[si], f"slab{si}"))
                            for m in range(KC):
                                bk, bt = nb()

                                def dnmm(e, bk=bk, m=m, dslabs=dslabs):
                                    ins = None
                                    for f in range(16):
                                        slab = dslabs[f // 8][0]
                                        f8 = f % 8
                                        ins = e.matmul(bk[:], slab[:, f8 * D + m * 128: f8 * D + (m + 1) * 128], hidden[:, f, :], start=(f == 0), stop=(f == 15))
                                    return ins
                                A("pe", dnmm, reads=[f"hid{f}" for f in range(16)] + [dslabs[0][1], dslabs[1][1]], writes=[bt])
                                A("dve", lambda e, bk=bk, m=m: e.tensor_tensor(out=xres[:, m, :], in0=xres[:, m, :], in1=bk[:], op=ALU.add),
                                       reads=[bt, f"xres{m}"], writes=[f"xres{m}"])

                    rms_rstd([(xres[:, c, :], f"xres{c}") for c in range(KC)], None, EPS, onesD, rstd[:], "rstd")
                    for c in range(KC):
                        A("dve", lambda e, c=c: e.scalar_tensor_tensor(
                            out=xres[:, c, :], in0=xres[:, c, :], scalar=V(VROW["norm_final"], c), in1=rstd[:],
                            op0=ALU.mult, op1=ALU.mult), reads=[f"xres{c}", "rstd"], writes=[f"xres{c}"])
                    for b in range(4):
                        for cg in range(2):
                            bk, bt = nb()

                            def tro(e, bk=bk, b=b, cg=cg):
                                ins = None
                                for cc in range(4):
                                    c = cg * 4 + cc
                                    ins = e.transpose(bk[:, cc * 128:(cc + 1) * 128], xres[:, c, b * 128:(b + 1) * 128], ident32[:])
                                return ins
                            A("pe", tro, reads=[f"xres{c}" for c in range(cg * 4, cg * 4 + 4)], writes=[bt])
                            if cg == 0:
                                A("act", lambda e, bk=bk, b=b, cg=cg: e.activation(out=iob[:, b, cg * 512:(cg + 1) * 512], in_=bk[:], func=ACT.Copy),
                                       reads=[bt], writes=[f"io_{b}_{cg}"])
                            else:
                                A("dve", lambda e, bk=bk, b=b, cg=cg: e.tensor_copy(out=iob[:, b, cg * 512:(cg + 1) * 512], in_=bk[:]),
                                       reads=[bt], writes=[f"io_{b}_{cg}"])
                    A("sp", lambda e, row0=row0: e.dma_start(out=out_d[row0:row0 + TT, :].rearrange("(b p) d -> p b d", p=128), in_=iob[:]).then_inc(d_o, 16),
                           reads=["iob"] + [f"hid{f}" for f in range(16)] + [f"io_{b}_{cg}" for b in range(4) for cg in range(2)], writes=["OUT"], dma=d_o)
            A("sp", lambda e: e.nop(), reads=["OUT"])
            T2.finish(block, sems_main(nc, es, sems))
            stats1 = T2.stats
    build_program.stats = (stats0, stats1)
    return nc


def sems_main(nc, es, sems):
    return {e: es.enter_context(nc.semaphore("m_" + e)) for e in ENGS}


_CACHE = {}


def kernel(x, lb_logits, norm_mix, w_in, conv_w, conv_b, w_r, b_r, w_i, b_i, lam, hg_norm, w_out, norm_mlp, w_up, w_down, norm_final):
    x = np.asarray(x)
    B, T, _ = x.shape
    NL = int(np.asarray(w_in).shape[0])
    ncores = 8
    assert B % ncores == 0
    NSEQ = B // ncores
    key = (NSEQ, T, NL)
    if key not in _CACHE:
        _CACHE[key] = build_program(NSEQ, T, NL)
    nc = _CACHE[key]
    f32 = lambda a: np.ascontiguousarray(np.asarray(a, dtype=np.float32))
    shared = {
        "lb_logits": f32(lb_logits), "norm_mix": f32(norm_mix), "w_in": f32(w_in), "conv_w": f32(conv_w),
        "conv_b": f32(conv_b), "w_r": f32(w_r), "b_r": f32(b_r), "w_i": f32(w_i), "b_i": f32(b_i),
        "lam": f32(lam), "hg_norm": f32(hg_norm), "w_out": f32(w_out), "norm_mlp": f32(norm_mlp),
        "w_up": f32(w_up), "w_down": f32(w_down), "norm_final": f32(norm_final).reshape(1, D),
    }
    xs = f32(x).reshape(ncores, NSEQ * T, D)
    in_maps = []
    for i in range(ncores):
        m = dict(shared)
        m["x"] = xs[i]
        in_maps.append(m)
    res = run_bass_kernel_spmd(nc, in_maps, core_ids=list(range(ncores)))
    out = np.stack([np.asarray(r["out"]) for r in res.results], axis=0)
    return out.reshape(B, T, D).astype(np.float32)
```

```python
import contextlib
import numpy as np
import concourse.bass as bass
import concourse.mybir as mybir
from concourse.bass_utils import run_bass_kernel_spmd

F32 = mybir.dt.float32
BF16 = mybir.dt.bfloat16
ACT = mybir.ActivationFunctionType
ALU = mybir.AluOpType

D = 1024
KC = 8
DIN = 8192
DFF = 4096
TT = 512
NCH = TT // 32
EPS = 1e-6
RG_C = 8.0
ENGS = ("pe", "act", "dve", "pool", "sp")


class _Op:
    __slots__ = ("eng", "emit", "deps", "needs_inc", "semval", "dsem", "dval")

    def __init__(self, eng, emit):
        self.eng = eng
        self.emit = emit
        self.deps = set()
        self.needs_inc = False
        self.semval = 0
        self.dsem = None
        self.dval = 0


class TK:
    def __init__(self):
        self.ops = []
        self.last_w = {}
        self.readers = {}
        self.dma_cnt = {}

    def add(self, eng, emit, reads=(), writes=(), dma=None, ndma=1):
        idx = len(self.ops)
        op = _Op(eng, emit)
        deps = op.deps
        lw = self.last_w
        rd = self.readers
        for t in reads:
            w = lw.get(t)
            if w is not None:
                deps.add((w, 0))
        for t in writes:
            w = lw.get(t)
            if w is not None:
                deps.add((w, 1))
            for r in rd.get(t, ()):
                deps.add((r, 2))
        for t in reads:
            rd.setdefault(t, []).append(idx)
        for t in writes:
            lw[t] = idx
            rd[t] = []
        if dma is not None:
            op.dsem = dma
            c = self.dma_cnt.get(dma.num, 0) + 16 * ndma
            self.dma_cnt[dma.num] = c
            op.dval = c
        self.ops.append(op)
        return idx

    def finish(self, block, sems):
        ops = self.ops
        real = []
        for i, op in enumerate(ops):
            rl = []
            for (d, kind) in op.deps:
                if d == i:
                    continue
                p = ops[d]
                if p.dsem is None and op.dsem is None and p.eng == op.eng:
                    if op.eng == "pe" or kind != 0:
                        continue
                rl.append(d)
                if p.dsem is None:
                    p.needs_inc = True
            real.append(rl)
        cnt = {e: 0 for e in ENGS}
        for op in ops:
            if op.dsem is None and op.needs_inc:
                cnt[op.eng] += 1
                op.semval = cnt[op.eng]
        per_eng = {e: [] for e in ENGS}
        for i, op in enumerate(ops):
            per_eng[op.eng].append(i)
        self.stats = {e: len(v) for e, v in per_eng.items()}
        self.stats["sem_max"] = dict(cnt)

        def run(eng_name, eh):
            waited = {}
            for i in per_eng[eng_name]:
                op = ops[i]
                waits = {}
                for d in real[i]:
                    p = ops[d]
                    if p.dsem is not None:
                        s, v = p.dsem, p.dval
                    else:
                        s, v = sems[p.eng], p.semval
                    if waited.get(s.num, 0) < v and waits.get(s.num, (None, 0))[1] < v:
                        waits[s.num] = (s, v)
                for sn, (s, v) in waits.items():
                    eh.wait_ge(s, v)
                    waited[sn] = v
                ins = op.emit(eh)
                if op.dsem is None and op.needs_inc:
                    ins.then_inc(sems[eng_name], 1)

        @block.tensor
        def _(e):
            run("pe", e)

        @block.scalar
        def _(e):
            run("act", e)

        @block.vector
        def _(e):
            run("dve", e)

        @block.gpsimd
        def _(e):
            run("pool", e)

        @block.sync
        def _(e):
            run("sp", e)


def _vrow(NL):
    rows = {}
    r = 0
    for nm in ("norm_mix", "norm_mlp", "conv_b", "b_r", "b_i", "lam", "lb_logits"):
        rows[nm] = r
        r += NL
    rows["conv_w"] = r
    r += 4 * NL
    rows["norm_final"] = r
    r += 1
    rows["hg_norm"] = r
    r += NL
    return rows, r


def build_program(NSEQ, T, NL):
    assert T % TT == 0
    NTILE = T // TT
    nc = bass.Bass("TRN2", target_bir_lowering=False)
    dram = {}

    def din(name, shape):
        dram[name] = nc.dram_tensor(name, list(shape), F32, kind="ExternalInput").ap()
        return dram[name]

    x_d = din("x", [NSEQ * T, D])
    lbl_d = din("lb_logits", [NL, D])
    nmix_d = din("norm_mix", [NL, D])
    win_d = din("w_in", [NL, D, DIN])
    cw_d = din("conv_w", [NL, 4, D])
    cb_d = din("conv_b", [NL, D])
    wr_d = din("w_r", [NL, 4, 256, 256])
    br_d = din("b_r", [NL, D])
    wi_d = din("w_i", [NL, 4, 256, 256])
    bi_d = din("b_i", [NL, D])
    lam_d = din("lam", [NL, D])
    hgn_d = din("hg_norm", [NL, 128])
    wo_d = din("w_out", [NL, D, D])
    nmlp_d = din("norm_mlp", [NL, D])
    wu_d = din("w_up", [NL, D, DFF])
    wd_d = din("w_down", [NL, DFF, D])
    nf_d = din("norm_final", [1, D])
    out_d = nc.dram_tensor("out", [NSEQ * T, D], F32, kind="ExternalOutput").ap()

    def dscr(name, shape):
        return nc.dram_tensor(name, list(shape), BF16, kind="Internal").ap()

    win_s = dscr("win_s", [NL, 128, KC, DIN])
    wg_s = dscr("wg_s", [NL, 128, 4, 2, 2, 256])
    dg_s = dscr("dg_s", [NL, 128, 8, 4, 128])
    wo_s = dscr("wo_s", [NL, 128, KC, D])
    wu_s = dscr("wu_s", [NL, 128, KC, DFF])
    wd_s = dscr("wd_s", [NL, 128, 32, D])

    VROW, NV = _vrow(NL)
    tk = TK()
    es = contextlib.ExitStack()
    with es:
        def sem(name):
            return es.enter_context(nc.semaphore(name))

        sems = {e: sem("s_" + e) for e in ENGS}
        block_holder = []

        def sb(name, shape, dt):
            return es.enter_context(nc.sbuf_tensor(name, list(shape), dt))

        ident32 = sb("ident32", [128, 128], F32)
        identbf = sb("identbf", [128, 128], BF16)
        onesD = sb("onesD", [128, 128], BF16)
        onesH = sb("onesH", [128, 128], BF16)
        mask32 = sb("mask32", [32, NCH, 32], F32)
        vecT = sb("vecT", [128, KC, NV], F32)
        chalf = sb("chalf", [128, KC, NL], F32)
        cfull = sb("cfull", [128, KC, NL], F32)
        nchalf = sb("nchalf", [128, KC, NL], F32)
        epsb = sb("epsb", [128, 1], F32)
        brh = sb("brh", [128, KC, NL], F32)
        bih = sb("bih", [128, KC, NL], F32)
        kbp = sb("kbp", [128, KC, NL], F32)
        kbn = sb("kbn", [128, KC, NL], F32)
        kb1 = sb("kb1", [128, KC, NL], F32)
        Sc32 = sb("Sc32", [128, NL * 8, 128], F32)
        Sc16 = sb("Sc16", [128, NL * 8, 128], BF16)
        hcar = sb("hcar", [128, NL * 8], F32)
        xkeep = sb("xkeep", [128, NL * 8, 3], BF16)

        d_small = sem("d_small")

        def V(row, c):
            return vecT[:, c, row:row + 1]

        pes = contextlib.ExitStack()
        with pes:
            def psb(name, shape, dt):
                return pes.enter_context(nc.sbuf_tensor(name, list(shape), dt))

            st32 = [psb(f"st32_{i}", [128, 8192], F32) for i in range(2)]
            st16 = [psb(f"st16_{i}", [128, 8192], BF16) for i in range(2)]
            vrow = psb("vrow", [128, D], F32)
            d_ld = [sem("d_pl0"), sem("d_pl1")]
            d_st = [sem("d_ps0"), sem("d_ps1")]
            pps = pes.enter_context(nc.psum_tensor("pps", [128, 512], F32))
            block = pes.enter_context(nc.Block())

            def mk_ident(e):
                e.memset(ident32[:], 0.0)
                return e.affine_select(out=ident32[:], in_=ident32[:], pattern=[[-1, 128]],
                                       compare_op=ALU.not_equal, fill=1.0, base=0, channel_multiplier=1)
            tk.add("pool", mk_ident, writes=["ident32"])
            tk.add("pool", lambda e: e.tensor_copy(out=identbf[:], in_=ident32[:]), reads=["ident32"], writes=["identbf"])
            tk.add("pool", lambda e: e.memset(onesD[:], 1.0 / 1024.0), writes=["onesD"])
            tk.add("pool", lambda e: e.memset(onesH[:], 1.0 / 128.0), writes=["onesH"])

            def mk_mask(e):
                e.memset(mask32[:], 1.0)
                return e.affine_select(out=mask32[:], in_=mask32[:], pattern=[[0, NCH], [1, 32]],
                                       compare_op=ALU.is_ge, fill=0.0, base=0, channel_multiplier=-1)
            tk.add("pool", mk_mask, writes=["mask32"])
            tk.add("pool", lambda e: e.memset(vrow[:], 0.0), writes=["vrow"])
            tk.add("pool", lambda e: e.memset(epsb[:], EPS), writes=["epsb"])

            def ld_small(e):
                ins = None
                for nm, ap in (("norm_mix", nmix_d), ("norm_mlp", nmlp_d), ("conv_b", cb_d), ("b_r", br_d),
                               ("b_i", bi_d), ("lam", lam_d), ("lb_logits", lbl_d)):
                    r = VROW[nm]
                    ins = e.dma_start(out=vrow[r:r + NL, :], in_=ap[:, :]).then_inc(d_small, 16)
                r = VROW["conv_w"]
                ins = e.dma_start(out=vrow[r:r + 4 * NL, :], in_=cw_d.rearrange("l j d -> (l j) d")).then_inc(d_small, 16)
                r = VROW["norm_final"]
                ins = e.dma_start(out=vrow[r:r + 1, :], in_=nf_d[:, :]).then_inc(d_small, 16)
                r = VROW["hg_norm"]
                ins = e.dma_start(out=vrow[r:r + NL, 0:128], in_=hgn_d[:, :]).then_inc(d_small, 16)
                return ins
            tk.add("sp", ld_small, reads=["vrow"], writes=["vrow"], dma=d_small, ndma=10)
            for half in range(2):
                def tr_small(e, half=half):
                    ins = None
                    for cc in range(4):
                        c = half * 4 + cc
                        ins = e.transpose(pps[:, cc * 128:cc * 128 + NV], vrow[0:NV, c * 128:(c + 1) * 128], ident32[0:NV, 0:NV])
                    return ins
                tk.add("pe", tr_small, reads=["vrow", "ident32"], writes=["pps"])
                tk.add("dve", lambda e, half=half: e.tensor_copy(
                    out=vecT[:, half * 4:(half + 1) * 4, :],
                    in_=pps[:].rearrange("p (c n) -> p c n", c=4)[:, :, 0:NV]), reads=["pps"], writes=["vecT"])

            lam_v = vecT[:, :, VROW["lam"]:VROW["lam"] + NL]
            lbl_v = vecT[:, :, VROW["lb_logits"]:VROW["lb_logits"] + NL]
            tA = psb("tA", [128, KC, NL], F32)
            tB = psb("tB", [128, KC, NL], F32)
            tC = psb("tC", [128, KC, 1], F32)

            tk.add("act", lambda e: e.activation(out=tA[:], in_=lam_v, func=ACT.Exp, scale=-1.0), reads=["vecT"], writes=["tA"])
            tk.add("dve", lambda e: e.tensor_scalar(out=tB[:], in0=tA[:], scalar1=0.2, scalar2=-0.25, op0=ALU.mult, op1=ALU.add), reads=["tA"], writes=["tB"])
            tk.add("dve", lambda e: e.tensor_tensor(out=tB[:], in0=tB[:], in1=tA[:], op=ALU.mult), reads=["tA", "tB"], writes=["tB"])
            tk.add("dve", lambda e: e.tensor_scalar(out=tB[:], in0=tB[:], scalar1=1.0 / 3.0, scalar2=None, op0=ALU.add), reads=["tB"], writes=["tB"])
            tk.add("dve", lambda e: e.tensor_tensor(out=tB[:], in0=tB[:], in1=tA[:], op=ALU.mult), reads=["tA", "tB"], writes=["tB"])
            tk.add("dve", lambda e: e.tensor_scalar(out=tB[:], in0=tB[:], scalar1=-0.5, scalar2=None, op0=ALU.add), reads=["tB"], writes=["tB"])
            tk.add("dve", lambda e: e.tensor_tensor(out=tB[:], in0=tB[:], in1=tA[:], op=ALU.mult), reads=["tA", "tB"], writes=["tB"])
            tk.add("dve", lambda e: e.tensor_scalar(out=tB[:], in0=tB[:], scalar1=1.0, scalar2=None, op0=ALU.add), reads=["tB"], writes=["tB"])
            tk.add("dve", lambda e: e.tensor_tensor(out=tB[:], in0=tB[:], in1=tA[:], op=ALU.mult), reads=["tA", "tB"], writes=["tB"])
            tk.add("dve", lambda e: e.tensor_scalar(out=cfull[:], in0=tB[:], scalar1=-RG_C, scalar2=None, op0=ALU.mult), reads=["tB"], writes=["cfull"])
            tk.add("dve", lambda e: e.tensor_scalar(out=chalf[:], in0=tB[:], scalar1=-0.5 * RG_C, scalar2=None, op0=ALU.mult), reads=["tB"], writes=["chalf"])
            tk.add("dve", lambda e: e.tensor_scalar(out=nchalf[:], in0=tB[:], scalar1=0.5 * RG_C, scalar2=None, op0=ALU.mult), reads=["tB"], writes=["nchalf"])
            tk.add("dve", lambda e: e.tensor_scalar(out=brh[:], in0=vecT[:, :, VROW["b_r"]:VROW["b_r"] + NL], scalar1=0.5, scalar2=None, op0=ALU.mult), reads=["vecT"], writes=["brh"])
            tk.add("dve", lambda e: e.tensor_scalar(out=bih[:], in0=vecT[:, :, VROW["b_i"]:VROW["b_i"] + NL], scalar1=0.5, scalar2=None, op0=ALU.mult), reads=["vecT"], writes=["bih"])
            tk.add("dve", lambda e: e.tensor_reduce(out=tC[:], in_=lbl_v, axis=mybir.AxisListType.X, op=ALU.max), reads=["vecT"], writes=["tC"])
            tk.add("dve", lambda e: e.tensor_tensor(out=tA[:], in0=lbl_v, in1=tC[:].to_broadcast([128, KC, NL]), op=ALU.subtract), reads=["vecT", "tC", "tB"], writes=["tA"])
            tk.add("act", lambda e: e.activation(out=tA[:], in_=tA[:], func=ACT.Exp), reads=["tA"], writes=["tA"])
            tk.add("dve", lambda e: e.tensor_reduce(out=tC[:], in_=tA[:], axis=mybir.AxisListType.X, op=ALU.add), reads=["tA"], writes=["tC"])
            tk.add("dve", lambda e: e.reciprocal(out=tC[:], in_=tC[:]), reads=["tC"], writes=["tC"])
            tk.add("dve", lambda e: e.tensor_tensor(out=tA[:], in0=tA[:], in1=tC[:].to_broadcast([128, KC, NL]), op=ALU.mult), reads=["tA", "tC"], writes=["tA"])
            tk.add("dve", lambda e: e.memset(tB[:], 0.0), reads=["tB"], writes=["tB"])
            for l in range(1, NL):
                tk.add("dve", lambda e, l=l: e.tensor_tensor(out=tB[:, :, l:l + 1], in0=tB[:, :, l - 1:l], in1=tA[:, :, l:l + 1], op=ALU.add), reads=["tA", "tB"], writes=["tB"])
            tk.add("dve", lambda e: e.tensor_scalar(out=tB[:], in0=tB[:], scalar1=0.0, scalar2=1.0, op0=ALU.max, op1=ALU.min), reads=["tB"], writes=["tB"])
            tk.add("dve", lambda e: e.tensor_scalar(out=kbp[:], in0=tB[:], scalar1=-0.5, scalar2=0.5, op0=ALU.mult, op1=ALU.add), reads=["tB"], writes=["kbp"])
            tk.add("dve", lambda e: e.tensor_scalar(out=kbn[:], in0=tB[:], scalar1=0.5, scalar2=-0.5, op0=ALU.mult, op1=ALU.add), reads=["tB"], writes=["kbn"])
            tk.add("dve", lambda e: e.tensor_scalar(out=kb1[:], in0=tB[:], scalar1=0.5, scalar2=0.5, op0=ALU.mult, op1=ALU.add), reads=["tB"], writes=["kb1"])

            rnd = [0]
            scr_toks = []

            def conv_round(load_fn, cast_fns, store_fn, nld=1, nst=1, xreads=()):
                i = rnd[0] % 2
                rnd[0] += 1
                s32, s16 = st32[i], st16[i]
                tk.add("sp", lambda e: load_fn(e, s32, d_ld[i]), reads=[], writes=[f"st32_{i}"], dma=d_ld[i], ndma=nld)
                for k, (eng, fn) in enumerate(cast_fns):
                    tk.add(eng, lambda e, fn=fn: fn(e, s32, s16), reads=[f"st32_{i}"] + list(xreads), writes=[f"st16_{i}_{k}"])
                tok = f"scr{len(scr_toks)}"
                scr_toks.append(tok)
                tk.add("sp", lambda e: store_fn(e, s16, d_st[i]), reads=[f"st16_{i}_{k}" for k in range(8)] + [f"st32_{i}"],
                       writes=[tok], dma=d_st[i], ndma=nst)

            def cast_generic(eng, lo, hi):
                if eng == "act":
                    return ("act", lambda e, s32, s16: e.activation(out=s16[:, lo:hi], in_=s32[:, lo:hi], func=ACT.Copy))
                return (eng, lambda e, s32, s16: e.tensor_copy(out=s16[:, lo:hi], in_=s32[:, lo:hi]))

            def split3(n):
                a = (n * 3 // 8) // 128 * 128
                b = a + (n * 3 // 8) // 128 * 128
                return [cast_generic("dve", 0, a), cast_generic("act", a, b), cast_generic("pool", b, n)]

            RG_G = (0, 1, 6)
            HG_G = (2, 3, 4, 5, 7)
            for l in range(NL):
                for kc in range(KC):
                    def ld(e, s32, ds, l=l, kc=kc):
                        return e.dma_start(out=s32[:, :], in_=win_d[l, kc * 128:(kc + 1) * 128, :]).then_inc(ds, 16)
                    casts = []
                    engs = ["dve", "act", "dve", "pool", "act", "dve", "act", "dve"]
                    for g in range(8):
                        if g in RG_G:
                            gi = RG_G.index(g)

                            def cf(e, s32, s16, g=g, gi=gi, eng=engs[g]):
                                src = s32[:, g * 1024:(g + 1) * 1024].rearrange("p (b c j) -> p b c j", b=4, c=2)
                                dst = s16[:, 0:3072].rearrange("p (b x) -> p b x", b=4)[:, :, gi * 256:(gi + 1) * 256].rearrange("p b (c j) -> p b c j", c=2)
                                if eng == "act":
                                    return e.activation(out=dst, in_=src, func=ACT.Copy)
                                return e.tensor_copy(out=dst, in_=src)
                        else:
                            gi = HG_G.index(g)

                            def cf(e, s32, s16, g=g, gi=gi, eng=engs[g]):
                                src = s32[:, g * 1024:(g + 1) * 1024].rearrange("p (h j) -> p h j", h=8)
                                dst = s16[:, 3072:8192].rearrange("p (h x) -> p h x", h=8)[:, :, gi * 128:(gi + 1) * 128]
                                if eng == "act":
                                    return e.activation(out=dst, in_=src, func=ACT.Copy)
                                return e.tensor_copy(out=dst, in_=src)
                        casts.append((engs[g], cf))

                    def stf(e, s16, ds, l=l, kc=kc):
                        return e.dma_start(out=win_s[l, :, kc, :], in_=s16[:, :]).then_inc(ds, 16)
                    conv_round(ld, casts, stf)
                def ld(e, s32, ds, l=l):
                    return e.dma_start(out=s32[:, :].rearrange("p (k n) -> p k n", k=KC),
                                       in_=wo_d[l].rearrange("(k p) n -> p k n", p=128)).then_inc(ds, 16)

                def stf(e, s16, ds, l=l):
                    return e.dma_start(out=wo_s[l], in_=s16[:, :].rearrange("p (k n) -> p k n", k=KC)).then_inc(ds, 16)
                conv_round(ld, split3(8192), stf)
                for k2 in range(KC // 2):
                    def ld(e, s32, ds, l=l, k2=k2):
                        return e.dma_start(out=s32[:, :].rearrange("p (k n) -> p k n", k=2),
                                           in_=wu_d[l, k2 * 256:(k2 + 1) * 256, :].rearrange("(k p) n -> p k n", p=128)).then_inc(ds, 16)

                    def stf(e, s16, ds, l=l, k2=k2):
                        return e.dma_start(out=wu_s[l, :, 2 * k2:2 * k2 + 2, :], in_=s16[:, :].rearrange("p (k n) -> p k n", k=2)).then_inc(ds, 16)
                    conv_round(ld, split3(8192), stf)
                for f8 in range(4):
                    def ld(e, s32, ds, l=l, f8=f8):
                        return e.dma_start(out=s32[:, :].rearrange("p (k n) -> p k n", k=8),
                                           in_=wd_d[l, f8 * 1024:(f8 + 1) * 1024, :].rearrange("(k p) n -> p k n", p=128)).then_inc(ds, 16)

                    def stf(e, s16, ds, l=l, f8=f8):
                        return e.dma_start(out=wd_s[l, :, 8 * f8:8 * f8 + 8, :], in_=s16[:, :].rearrange("p (k n) -> p k n", k=8)).then_inc(ds, 16)
                    conv_round(ld, split3(8192), stf)
                def ld(e, s32, ds, l=l):
                    e.dma_start(out=s32[:, 0:2048].rearrange("p (k n) -> p k n", k=8),
                                in_=wr_d[l].rearrange("b (k p) n -> p (b k) n", p=128)).then_inc(ds, 16)
                    return e.dma_start(out=s32[:, 2048:4096].rearrange("p (k n) -> p k n", k=8),
                                       in_=wi_d[l].rearrange("b (k p) n -> p (b k) n", p=128)).then_inc(ds, 16)

                def dgf(e, s32, s16, l=l):
                    ins = None
                    for c in range(8):
                        for j in range(4):
                            o = 4096 + (c * 4 + j) * 128
                            ins = e.activation(out=s16[:, o:o + 128], in_=identbf[:], func=ACT.Copy,
                                               scale=V(VROW["conv_w"] + 4 * l + j, c))
                    return ins

                def stf(e, s16, ds, l=l):
                    for g in range(2):
                        e.dma_start(out=wg_s[l, :, :, g, :, :],
                                    in_=s16[:, g * 2048:(g + 1) * 2048].rearrange("p (b k n) -> p b k n", b=4, k=2)).then_inc(ds, 16)
                    return e.dma_start(out=dg_s[l], in_=s16[:, 4096:8192].rearrange("p (c j n) -> p c j n", c=8, j=4)).then_inc(ds, 16)
                conv_round(ld, [cast_generic("dve", 0, 4096), ("act", dgf)], stf, nld=2, nst=3, xreads=("vecT", "identbf"))
            tk.add("sp", lambda e: e.nop(), reads=scr_toks + ["vecT", "kbp", "kbn", "kb1", "chalf", "cfull", "nchalf", "brh", "bih", "mask32", "onesD", "onesH", "identbf", "epsb"], writes=["BAR"])
            for eng in ("pe", "act", "dve", "pool"):
                tk.add(eng, lambda e: e.nop(), reads=["BAR"], writes=[f"BAR_{eng}"])
            tk.finish(block, sems)
            stats0 = tk.stats

        tk2 = TK()
        tk2.dma_cnt = dict(tk.dma_cnt)
        mes = contextlib.ExitStack()
        with mes:
            def msb(name, shape, dt):
                return mes.enter_context(nc.sbuf_tensor(name, list(shape), dt))

            xres = msb("xres", [128, KC, TT], F32)
            hbuf = msb("hbuf", [128, KC, TT], BF16)
            ysum = msb("ysum", [128, KC, TT], BF16)
            iob = msb("iob", [128, 4, D], F32)
            hidden = iob[:].rearrange("p b d -> p (b d)").bitcast(BF16)[:, 0:16 * TT].rearrange("p (f t) -> p f t", f=16)
            NSLOT = 4
            slabs = [msb(f"slab{i}", [128, 8192 if i < 2 else 5120], BF16) for i in range(NSLOT)]
            d_slab = [sem(f"d_slab{i}") for i in range(NSLOT)]
            d_x = sem("d_x")
            d_o = sem("d_o")
            xa_bf = msb("xa_bf", [128, 2, TT + 3], BF16)
            xc32 = msb("xc32", [128, 2, TT], F32)
            xcbf = msb("xcbf", [128, 2, TT], BF16)
            rgA = msb("rgA", [128, 2, TT], F32)
            rgI = msb("rgI", [128, 2, TT], F32)
            rgS = msb("rgS", [128, 2, TT], F32)
            rgG = msb("rgG", [128, 2, TT], F32)
            qs = msb("qs", [128, TT], F32)
            fz = msb("fz", [128, TT], F32)
            kk = msb("kk", [128, TT], F32)
            fm = msb("fm", [128, TT], F32)
            Pp = msb("Pp", [128, TT], F32)
            Rr = msb("Rr", [128, TT], F32)
            qt = msb("qt", [128, TT], BF16)
            kt = msb("kt", [128, TT], BF16)
            qt2 = msb("qt2", [128, TT], BF16)
            vF = msb("vF", [128, TT], BF16)
            kT32 = msb("kT32", [32, NCH, 128], BF16)
            v32 = msb("v32", [32, NCH, 128], BF16)
            Sm32 = msb("Sm32", [32, NCH, 32], BF16)
            Uall = msb("Uall", [128, NCH, 128], F32)
            Sbf = msb("Sbf", [128, NCH, 128], BF16)
            o2 = msb("o2", [128, TT], BF16)
            msq, onb = qs, Rr
            sgb = msb("sgb", [128, TT], F32)
            tmb = msb("tmb", [128, TT], F32)
            sqb = msb("sqb", [128, TT], BF16)
            rstd = msb("rstd", [128, TT], F32)
            rl = [msb(f"rl{i}", [128, TT], BF16) for i in range(2)]
            banks = [mes.enter_context(nc.psum_tensor(f"bank{i}", [128, TT], F32)) for i in range(8)]
            block = mes.enter_context(nc.Block())
            T2 = tk2

            cur = [None]

            DEFC = {"pe": 1.8, "act": 0.65, "dve": 0.6, "pool": 1.0, "sp": 5.0}

            def A(eng, emit, reads=(), writes=(), dma=None, ndma=1, tag=None, cost=None):
                if cur[0] is not None:
                    cur[0].append((eng, emit, reads, writes, dma, ndma, tag, DEFC[eng] if cost is None else cost))
                else:
                    T2.add(eng, emit, reads=reads, writes=writes, dma=dma, ndma=ndma)

            def flush_merged(L1, L2):
                n1, n2 = len(L1), len(L2)
                i = j = 0
                merged = []
                efree = {e: 0.0 for e in ENGS}
                wdone = {}
                rdone = {}

                def est(op):
                    eng, _, reads, writes, dma, _, _, cost = op
                    t = efree[eng]
                    for tk_ in reads:
                        t = max(t, wdone.get(tk_, 0.0) + 0.15)
                    for tk_ in writes:
                        t = max(t, wdone.get(tk_, 0.0), rdone.get(tk_, 0.0) + 0.1)
                    return t

                def commit(op):
                    eng, _, reads, writes, dma, _, _, cost = op
                    t0 = est(op)
                    if eng == "sp":
                        efree[eng] = t0 + 0.1
                    else:
                        efree[eng] = t0 + cost
                    t1 = t0 + cost
                    for tk_ in reads:
                        rdone[tk_] = max(rdone.get(tk_, 0.0), t1)
                    for tk_ in writes:
                        wdone[tk_] = t1
                    merged.append(op)

                while i < n1 or j < n2:
                    if j >= n2:
                        commit(L1[i]); i += 1
                    elif i >= n1:
                        commit(L2[j]); j += 1
                    else:
                        a, b = est(L1[i]), est(L2[j])
                        if a < b:
                            commit(L1[i]); i += 1
                        else:
                            commit(L2[j]); j += 1
                seen = {}
                for op in merged:
                    tag = op[6]
                    if tag is not None:
                        kind, key, flags = tag
                        if key not in seen:
                            seen[key] = kind
                            flags["first"] = kind
                for (eng, emit, reads, writes, dma, ndma, tag, cost) in merged:
                    T2.add(eng, emit, reads=reads, writes=writes, dma=dma, ndma=ndma)

            POOLS = {"all": list(range(8)), "rg": [0, 1, 2], "hg": [3, 4, 5, 6, 7]}
            bank_rr = {"all": 0, "rg": 0, "hg": 0}

            def nb(pool="all"):
                lst = POOLS[pool]
                i = lst[bank_rr[pool] % len(lst)]
                bank_rr[pool] += 1
                return banks[i], f"bank{i}"

            slot_rr = {"big": 0, "small": 0}

            def nslot(kind="big"):
                i = slot_rr[kind] % 2 + (0 if kind == "big" else 2)
                slot_rr[kind] += 1
                return i

            A("pool", lambda e: e.memset(fm[:], 0.0), writes=["fm"])
            A("pool", lambda e: e.memset(xa_bf[:], 0.0), writes=["xa_bf"])

            def rms_rstd(src_chunks, reads, eps, ones, out_rstd, wtok):
                bk, bt = nb()
                n = len(src_chunks)
                for i, (ap, tok) in enumerate(src_chunks):
                    A("act", lambda e, ap=ap: e.activation(out=sqb[:], in_=ap, func=ACT.Square), reads=[tok], writes=["sqb"])
                    A("pe", lambda e, i=i: e.matmul(bk[:], ones[:], sqb[:], start=(i == 0), stop=(i == n - 1)), reads=["sqb"], writes=[bt])
                A("act", lambda e: e.activation(out=out_rstd, in_=bk[:], func=ACT.Ln, bias=epsb[:, 0:1], scale=1.0), reads=[bt], writes=[wtok])
                A("act", lambda e: e.activation(out=out_rstd, in_=out_rstd, func=ACT.Exp, scale=-0.5), reads=[wtok], writes=[wtok])

            for sq_i in range(NSEQ):
                A("pool", lambda e: e.memset(Sc32[:], 0.0), writes=[f"Sc32_{l}_{h}" for l in range(NL) for h in range(8)])
                A("pool", lambda e: e.memset(Sc16[:], 0.0), writes=[f"Sc16_{l}_{h}" for l in range(NL) for h in range(8)])
                A("pool", lambda e: e.memset(hcar[:], 0.0), writes=[f"hcar_{l}_{c}" for l in range(NL) for c in range(8)])
                A("pool", lambda e: e.memset(xkeep[:], 0.0), writes=[f"xkeep_{l}_{c}" for l in range(NL) for c in range(8)])
                for ti in range(NTILE):
                    row0 = sq_i * T + ti * TT
                    A("sp", lambda e, row0=row0: e.dma_start(out=iob[:], in_=x_d[row0:row0 + TT, :].rearrange("(b p) d -> p b d", p=128)).then_inc(d_x, 16),
                           writes=["iob"] + [f"hid{f}" for f in range(16)] + [f"io_{b}_{cg}" for b in range(4) for cg in range(2)], dma=d_x)
                    for c in range(KC):
                        bk, bt = nb()

                        def trx(e, c=c, bk=bk):
                            ins = None
                            for b in range(4):
                                ins = e.transpose(bk[:, b * 128:(b + 1) * 128], iob[:, b, c * 128:(c + 1) * 128], ident32[:])
                            return ins
                        A("pe", trx, reads=["iob"], writes=[bt])
                        if c % 2 == 0:
                            A("act", lambda e, c=c, bk=bk: e.activation(out=xres[:, c, :], in_=bk[:], func=ACT.Copy), reads=[bt], writes=[f"xres{c}"])
                        else:
                            A("dve", lambda e, c=c, bk=bk: e.tensor_copy(out=xres[:, c, :], in_=bk[:]), reads=[bt], writes=[f"xres{c}"])

                    for l in range(NL):
                        rms_rstd([(xres[:, c, :], f"xres{c}") for c in range(KC)], None, EPS, onesD, rstd[:], "rstd")
                        for c in range(KC):
                            A("dve", lambda e, c=c, l=l: e.scalar_tensor_tensor(
                                out=hbuf[:, c, :], in0=xres[:, c, :], scalar=V(VROW["norm_mix"] + l, c), in1=rstd[:],
                                op0=ALU.mult, op1=ALU.mult), reads=[f"xres{c}", "rstd"], writes=[f"h{c}"])
                        hreads = [f"h{c}" for c in range(KC)]

                        def win_mm(bk, slab, col0, W_):
                            def f(e):
                                ins = None
                                sv = slab[:, :]
                                for kc in range(KC):
                                    ins = e.matmul(bk[:], sv[:, kc * W_ + col0: kc * W_ + col0 + 128], hbuf[:, kc, :],
                                                   start=(kc == 0), stop=(kc == KC - 1))
                                return ins
                            return f

                        rg_lists = []
                        yflags = {c: {"first": None} for c in range(KC)}
                        for b in range(4):
                            cur[0] = []
                            rg_lists.append(cur[0])
                            si = nslot()
                            slab = slabs[si]
                            stok = f"slab{si}"

                            def ld_rg(e, slab=slab, l=l, b=b, ds=d_slab[si]):
                                e.dma_start(out=slab[:, 0:6144].rearrange("p (k n) -> p k n", k=KC),
                                            in_=win_s[l, :, :, b * 768:(b + 1) * 768]).then_inc(ds, 16)
                                e.dma_start(out=slab[:, 6144:7168].rearrange("p (g k n) -> p g k n", g=2, k=2),
                                            in_=wg_s[l, :, b, :, :, :]).then_inc(ds, 16)
                                return e.dma_start(out=slab[:, 7168:8192].rearrange("p (c j n) -> p c j n", c=2, j=4),
                                                   in_=dg_s[l, :, 2 * b:2 * b + 2, :, :]).then_inc(ds, 16)
                            A("sp", ld_rg, writes=[stok], dma=d_slab[si], ndma=3)
                            chs = (2 * b, 2 * b + 1)
                            for ci, c in enumerate(chs):
                                A("pool", lambda e, ci=ci, c=c, l=l: e.tensor_copy(out=xa_bf[:, ci, 0:3], in_=xkeep[:, l * 8 + c, :]),
                                       reads=[f"xkeep_{l}_{c}"], writes=[f"xa_halo{ci}"])
                                bk, bt = nb("rg")
                                A("pe", win_mm(bk, slab, 0 * 256 + ci * 128, 768), reads=hreads + [stok], writes=[bt])
                                A("act", lambda e, ci=ci, bk=bk: e.activation(out=xa_bf[:, ci, 3:3 + TT], in_=bk[:], func=ACT.Copy),
                                       reads=[bt], writes=[f"xa_bf{ci}"])
                                bk2, bt2 = nb("rg")

                                def convmm(e, ci=ci, bk2=bk2, slab=slab):
                                    ins = None
                                    for j in range(4):
                                        o = 7168 + (ci * 4 + j) * 128
                                        ins = e.matmul(bk2[:], slab[:, o:o + 128], xa_bf[:, ci, j:j + TT], start=(j == 0), stop=(j == 3))
                                    return ins
                                A("pe", convmm, reads=[f"xa_bf{ci}", f"xa_halo{ci}", stok], writes=[bt2], cost=0.9)
                                A("pool", lambda e, ci=ci, c=c, l=l: e.tensor_copy(out=xkeep[:, l * 8 + c, :], in_=xa_bf[:, ci, TT:TT + 3]),
                                       reads=[f"xa_bf{ci}"], writes=[f"xkeep_{l}_{c}"])
                                A("act", lambda e, ci=ci, c=c, l=l, bk2=bk2: e.activation(out=xc32[:, ci, :], in_=bk2[:], func=ACT.Identity,
                                                                                            bias=V(VROW["conv_b"] + l, c), scale=1.0),
                                       reads=[bt2], writes=[f"xc32_{ci}"])
                                A("pool", lambda e, ci=ci: e.tensor_copy(out=xcbf[:, ci, :], in_=xc32[:, ci, :]), reads=[f"xc32_{ci}"], writes=[f"xcbf{ci}"])
                            for ci, c in enumerate(chs):
                                for g in range(2):
                                    bk, bt = nb("rg")

                                    def gmm(e, ci=ci, g=g, bk=bk, slab=slab):
                                        ins = None
                                        for k2 in range(2):
                                            o = 6144 + (g * 2 + k2) * 256 + ci * 128
                                            ins = e.matmul(bk[:], slab[:, o:o + 128], xcbf[:, k2, :], start=(k2 == 0), stop=(k2 == 1))
                                        return ins
                                    A("pe", gmm, reads=["xcbf0", "xcbf1", stok], writes=[bt], cost=0.45)
                                    dst = rgA if g == 0 else rgI
                                    bsrc = brh if g == 0 else bih
                                    A("act", lambda e, ci=ci, c=c, l=l, bk=bk, dst=dst, bsrc=bsrc: e.activation(
                                        out=dst[:, ci, :], in_=bk[:], func=ACT.Tanh, bias=bsrc[:, c, l:l + 1], scale=0.5),
                                        reads=[bt], writes=[("rgA" if g == 0 else "rgI") + str(ci)])
                            for ci, c in enumerate(chs):
                                A("act", lambda e, ci=ci, c=c, l=l: e.activation(out=rgS[:, ci, :], in_=rgA[:, ci, :], func=ACT.Tanh,
                                                                                     bias=nchalf[:, c, l:l + 1], scale=nchalf[:, c, l:l + 1]),
                                       reads=[f"rgA{ci}"], writes=[f"rgS{ci}"])
                                A("act", lambda e, ci=ci, c=c, l=l: e.activation(out=rgA[:, ci, :], in_=rgA[:, ci, :], func=ACT.Exp,
                                                                                     bias=chalf[:, c, l:l + 1], scale=chalf[:, c, l:l + 1]),
                                       reads=[f"rgA{ci}"], writes=[f"rgA{ci}"])
                            for ci, c in enumerate(chs):
                                A("act", lambda e, ci=ci: e.activation(out=rgG[:, ci, :], in_=rgA[:, ci, :], func=ACT.Square),
                                  reads=[f"rgA{ci}"], writes=[f"rgG{ci}"])
                                A("dve", lambda e, ci=ci: e.scalar_tensor_tensor(out=rgS[:, ci, :], in0=rgG[:, ci, :], scalar=1.0, in1=rgS[:, ci, :],
                                                                                 op0=ALU.add, op1=ALU.mult),
                                  reads=[f"rgS{ci}", f"rgG{ci}"], writes=[f"rgS{ci}"])
                                A("act", lambda e, ci=ci: e.activation(out=rgS[:, ci, :], in_=rgS[:, ci, :], func=ACT.Sqrt, scale=0.25),
                                  reads=[f"rgS{ci}"], writes=[f"rgS{ci}"])
                                A("dve", lambda e, ci=ci: e.scalar_tensor_tensor(out=rgI[:, ci, :], in0=rgI[:, ci, :], scalar=1.0, in1=xc32[:, ci, :],
                                                                                      op0=ALU.add, op1=ALU.mult),
                                       reads=[f"rgI{ci}", f"xc32_{ci}"], writes=[f"rgI{ci}"])
                                A("dve", lambda e, ci=ci: e.tensor_tensor(out=rgI[:, ci, :], in0=rgI[:, ci, :], in1=rgS[:, ci, :], op=ALU.mult),
                                       reads=[f"rgI{ci}", f"rgS{ci}"], writes=[f"rgI{ci}"])
                                A("dve", lambda e, ci=ci, c=c, l=l: e.tensor_tensor_scan(out=xc32[:, ci, :], data0=rgA[:, ci, :], data1=rgI[:, ci, :],
                                                                                             initial=hcar[:, l * 8 + c:l * 8 + c + 1],
                                                                                             op0=ALU.mult, op1=ALU.add),
                                       reads=[f"rgA{ci}", f"rgI{ci}", f"hcar_{l}_{c}"], writes=[f"xc32_{ci}"], cost=1.3)
                                A("dve", lambda e, ci=ci, c=c, l=l: e.tensor_copy(out=hcar[:, l * 8 + c:l * 8 + c + 1], in_=xc32[:, ci, TT - 1:TT]),
                                       reads=[f"xc32_{ci}"], writes=[f"hcar_{l}_{c}"])
                            for ci, c in enumerate(chs):
                                bk, bt = nb("rg")
                                A("pe", win_mm(bk, slab, 1 * 256 + ci * 128, 768), reads=hreads + [stok], writes=[bt])
                                A("act", lambda e, ci=ci, bk=bk: e.activation(out=rgG[:, ci, :], in_=bk[:], func=ACT.Gelu_apprx_tanh),
                                       reads=[bt], writes=[f"rgG{ci}"])
                                bk2, bt2 = nb("rg")
                                A("pe", win_mm(bk2, slab, 2 * 256 + ci * 128, 768), reads=hreads + [stok], writes=[bt2])
                                A("act", lambda e, ci=ci, bk2=bk2: e.activation(out=rgS[:, ci, :], in_=bk2[:], func=ACT.Tanh, scale=0.5),
                                       reads=[bt2], writes=[f"rgS{ci}"])
                                A("dve", lambda e, ci=ci: e.tensor_tensor(out=rgG[:, ci, :], in0=rgG[:, ci, :], in1=xc32[:, ci, :], op=ALU.mult),
                                       reads=[f"rgG{ci}", f"xc32_{ci}"], writes=[f"rgG{ci}"])
                                A("dve", lambda e, ci=ci: e.scalar_tensor_tensor(out=rgG[:, ci, :], in0=rgS[:, ci, :], scalar=1.0, in1=rgG[:, ci, :],
                                                                                 op0=ALU.add, op1=ALU.mult),
                                  reads=[f"rgS{ci}", f"rgG{ci}"], writes=[f"rgG{ci}"])

                                def yrg(e, ci=ci, c=c, fl=yflags[c]):
                                    if fl["first"] == "rg":
                                        return e.tensor_copy(out=ysum[:, c, :], in_=rgG[:, ci, :])
                                    return e.tensor_tensor(out=ysum[:, c, :], in0=ysum[:, c, :], in1=rgG[:, ci, :], op=ALU.add)
                                A("dve", yrg, reads=[f"rgG{ci}", f"ysum{c}"], writes=[f"ysum{c}"], tag=("rg", c, yflags[c]))
                            cur[0] = None

                        hg_lists = []
                        for hh in range(8):
                            cur[0] = []
                            hg_lists.append(cur[0])
                            si = nslot("small")
                            slab = slabs[si]
                            stok = f"slab{si}"

                            def ld_hg(e, slab=slab, l=l, hh=hh, ds=d_slab[si]):
                                return e.dma_start(out=slab[:, 0:5120].rearrange("p (k n) -> p k n", k=KC),
                                                   in_=win_s[l, :, :, 3072 + hh * 640:3072 + (hh + 1) * 640]).then_inc(ds, 16)
                            A("sp", ld_hg, writes=[stok], dma=d_slab[si])
                            sk = f"Sc_{l}_{hh}"
                            bk, bt = nb("hg")
                            A("pe", win_mm(bk, slab, 0, 640), reads=hreads + [stok], writes=[bt])
                            A("act", lambda e, bk=bk: e.activation(out=qs[:], in_=bk[:], func=ACT.Silu), reads=[bt], writes=["qs"])
                            bk, bt = nb("hg")
                            A("pe", win_mm(bk, slab, 128, 640), reads=hreads + [stok], writes=[bt])
                            A("act", lambda e, bk=bk: e.activation(out=fz[:], in_=bk[:], func=ACT.Tanh, scale=0.5), reads=[bt], writes=["fz"])
                            bk, bt = nb("hg")
                            A("pe", win_mm(bk, slab, 256, 640), reads=hreads + [stok], writes=[bt])
                            A("act", lambda e, bk=bk: e.activation(out=vF[:], in_=bk[:], func=ACT.Copy), reads=[bt], writes=["vF"])
                            bk, bt = nb("hg")
                            A("pe", win_mm(bk, slab, 384, 640), reads=hreads + [stok], writes=[bt])
                            A("act", lambda e, bk=bk: e.activation(out=sgb[:], in_=bk[:], func=ACT.Silu), reads=[bt], writes=["sgb"])
                            bk, bt = nb("hg")
                            A("pe", win_mm(bk, slab, 512, 640), reads=hreads + [stok], writes=[bt])
                            A("act", lambda e, bk=bk: e.activation(out=tmb[:], in_=bk[:], func=ACT.Tanh, scale=0.5), reads=[bt], writes=["tmb"])
                            A("dve", lambda e, l=l, hh=hh: e.tensor_scalar(out=kk[:], in0=fz[:], scalar1=kbn[:, hh, l:l + 1], scalar2=kbp[:, hh, l:l + 1],
                                                                                op0=ALU.mult, op1=ALU.add), reads=["fz"], writes=["kk"])
                            A("dve", lambda e, l=l, hh=hh: e.tensor_scalar(out=fz[:], in0=fz[:], scalar1=kbp[:, hh, l:l + 1], scalar2=kb1[:, hh, l:l + 1],
                                                                                op0=ALU.mult, op1=ALU.add), reads=["fz", "kk"], writes=["fz"])
                            fv = fz[:].rearrange("p (c i) -> p c i", i=32)
                            fmv = fm[:].rearrange("p (c i) -> p c i", i=32)
                            A("pool", lambda e: e.tensor_copy(out=fmv[:, :, 0:1], in_=fv[:, :, 0:1]), reads=["fz"], writes=["fm"])
                            A("dve", lambda e: e.tensor_tensor_scan(out=Pp[:], data0=fz[:], data1=fm[:], initial=1.0, op0=ALU.mult, op1=ALU.max),
                                   reads=["fz", "fm"], writes=["Pp"], cost=1.3)
                            A("dve", lambda e: e.reciprocal(out=Rr[:], in_=Pp[:]), reads=["Pp"], writes=["Rr"], cost=3.4)
                            A("dve", lambda e: e.tensor_tensor(out=qt[:], in0=qs[:], in1=Pp[:], op=ALU.mult), reads=["qs", "Pp"], writes=["qt"])
                            A("dve", lambda e: e.tensor_tensor(out=kt[:], in0=kk[:], in1=Rr[:], op=ALU.mult), reads=["kk", "Rr"], writes=["kt"])
                            bkT, btT = nb("hg")
                            kTp = bkT[:].bitcast(BF16)

                            def trk(e, kTp=kTp):
                                ins = None
                                for c in range(8):
                                    ins = e.transpose(kTp[0:32, c * 128:(c + 1) * 128], kt[:, c * 32:(c + 1) * 32], identbf[:])
                                return ins

                            def trk2(e, kTp=kTp):
                                ins = None
                                for c in range(8, 16):
                                    ins = e.transpose(kTp[0:32, (c - 8) * 128:(c - 7) * 128], kt[:, c * 32:(c + 1) * 32], identbf[:])
                                return ins
                            A("pe", trk, reads=["kt", "identbf"], writes=[btT], cost=0.6)
                            A("act", lambda e, kTp=kTp: e.activation(out=kT32[:, 0:8, :], in_=kTp[0:32, :].rearrange("p (c n) -> p c n", c=8), func=ACT.Copy),
                                   reads=[btT], writes=["kT32a"])
                            bkT2, btT2 = nb("hg")
                            kTp2 = bkT2[:].bitcast(BF16)
                            A("pe", lambda e, kTp2=kTp2: trk2(e, kTp2), reads=["kt", "identbf"], writes=[btT2], cost=0.6)
                            A("act", lambda e, kTp2=kTp2: e.activation(out=kT32[:, 8:16, :], in_=kTp2[0:32, :].rearrange("p (c n) -> p c n", c=8), func=ACT.Copy),
                                   reads=[btT2], writes=["kT32b"])
                            bkV, btV = nb("hg")
                            vTp = bkV[:].bitcast(BF16)

                            def trv(e, vTp=vTp, lo=0):
                                ins = None
                                for c in range(lo, lo + 8):
                                    ins = e.transpose(vTp[0:32, (c - lo) * 128:(c - lo + 1) * 128], vF[:, c * 32:(c + 1) * 32], identbf[:])
                                return ins
                            A("pe", lambda e, vTp=vTp: trv(e, vTp, 0), reads=["vF", "identbf"], writes=[btV], cost=0.6)
                            A("dve", lambda e, vTp=vTp: e.tensor_copy(out=v32[:, 0:8, :], in_=vTp[0:32, :].rearrange("p (c n) -> p c n", c=8)),
                                   reads=[btV], writes=["v32a"])
                            bkV2, btV2 = nb("hg")
                            vTp2 = bkV2[:].bitcast(BF16)
                            A("pe", lambda e, vTp2=vTp2: trv(e, vTp2, 8), reads=["vF", "identbf"], writes=[btV2], cost=0.6)
                            A("dve", lambda e, vTp2=vTp2: e.tensor_copy(out=v32[:, 8:16, :], in_=vTp2[0:32, :].rearrange("p (c n) -> p c n", c=8)),
                                   reads=[btV2], writes=["v32b"])
                            bkS, btS = nb("hg")

                            def scmm(e, bkS=bkS):
                                ins = None
                                for c in range(NCH):
                                    ins = e.matmul(bkS[0:32, c * 32:(c + 1) * 32], kt[:, c * 32:(c + 1) * 32], qt[:, c * 32:(c + 1) * 32], start=True, stop=True)
                                return ins
                            A("pe", scmm, reads=["kt", "qt"], writes=[btS], cost=1.1)
                            A("dve", lambda e, bkS=bkS: e.tensor_tensor(out=Sm32[:], in0=bkS[0:32, :].rearrange("p (c t) -> p c t", c=NCH), in1=mask32[:], op=ALU.mult),
                                   reads=[btS], writes=["Sm32"])
                            abanks = []
                            for q4 in range(4):
                                bkA, btA = nb("hg")
                                abanks.append((bkA, btA))

                                def amm(e, bkA=bkA, q4=q4):
                                    ins = None
                                    for cc in range(4):
                                        c = q4 * 4 + cc
                                        ins = e.matmul(bkA[:, cc * 128:(cc + 1) * 128], kT32[:, c, :], v32[:, c, :], start=True, stop=True)
                                    return ins
                                A("pe", amm, reads=["kT32a", "kT32b", "v32a", "v32b"], writes=[btA], cost=0.5)
                            ecol = Pp[:].rearrange("p (c i) -> p c i", i=32)
                            qtv = qt[:].rearrange("p (c i) -> p c i", i=32)
                            qt2v = qt2[:].rearrange("p (c i) -> p c i", i=32)
                            A("dve", lambda e: e.tensor_tensor(out=qt2v[:, 1:NCH, :], in0=qtv[:, 1:NCH, :],
                                                               in1=ecol[:, 0:NCH - 1, 31:32].to_broadcast([128, NCH - 1, 32]), op=ALU.mult),
                              reads=["qt", "Pp"], writes=["qt2"])
                            bkO, btO = nb("hg")
                            for c in range(NCH):
                                bkA, btA = abanks[c // 4]
                                a_ap = bkA[:, (c % 4) * 128:(c % 4 + 1) * 128]
                                if c == 0:
                                    A("dve", lambda e, a_ap=a_ap, l=l, hh=hh: e.tensor_tensor(out=Uall[:, 0, :], in0=Sc32[:, l * 8 + hh, :], in1=a_ap, op=ALU.add),
                                      reads=[btA, f"Sc32_{l}_{hh}"], writes=["U0"], cost=0.48)
                                else:
                                    A("dve", lambda e, a_ap=a_ap, c=c: e.scalar_tensor_tensor(out=Uall[:, c, :], in0=Uall[:, c - 1, :], scalar=ecol[:, c - 1, 31:32],
                                                                                             in1=a_ap, op0=ALU.mult, op1=ALU.add),
                                      reads=[btA, f"U{c - 1}", "Pp"], writes=[f"U{c}"], cost=0.48)
                                if c < NCH - 1:
                                    A("act", lambda e, c=c: e.activation(out=Sbf[:, c, :], in_=Uall[:, c, :], func=ACT.Copy), reads=[f"U{c}"], writes=[f"Ub{c}"], cost=0.25)

                                def omm(e, bkO=bkO, l=l, hh=hh, c=c):
                                    e.matmul(bkO[:, c * 32:(c + 1) * 32], v32[:, c, :], Sm32[:, c, :], start=True, stop=False)
                                    if c == 0:
                                        return e.matmul(bkO[:, 0:32], Sc16[:, l * 8 + hh, :], qt[:, 0:32], start=False, stop=True)
                                    return e.matmul(bkO[:, c * 32:(c + 1) * 32], Sbf[:, c - 1, :], qt2[:, c * 32:(c + 1) * 32], start=False, stop=True)
                                A("pe", omm, reads=["v32a", "v32b", "Sm32"] + (["qt", f"Sc16_{l}_{hh}"] if c == 0 else ["qt2", f"Ub{c - 1}"]), writes=[btO], cost=0.3)
                            A("dve", lambda e, l=l, hh=hh: e.tensor_scalar(out=Sc32[:, l * 8 + hh, :], in0=Uall[:, NCH - 1, :], scalar1=ecol[:, NCH - 1, 31:32], scalar2=None,
                                                                           op0=ALU.mult), reads=[f"U{NCH - 1}", "Pp"], writes=[f"Sc32_{l}_{hh}"])
                            A("act", lambda e, l=l, hh=hh: e.activation(out=Sc16[:, l * 8 + hh, :], in_=Sc32[:, l * 8 + hh, :], func=ACT.Copy),
                              reads=[f"Sc32_{l}_{hh}"], writes=[f"Sc16_{l}_{hh}"], cost=0.25)
                            A("act", lambda e, bkO=bkO: e.activation(out=o2[:], in_=bkO[:], func=ACT.Square), reads=[btO], writes=["o2"])
                            bkM, btM = nb("hg")
                            A("pe", lambda e, bkM=bkM: e.matmul(bkM[:], onesH[:], o2[:], start=True, stop=True), reads=["o2"], writes=[btM], cost=0.25)
                            A("act", lambda e, bkM=bkM: e.activation(out=msq[:], in_=bkM[:], func=ACT.Ln, bias=epsb[:, 0:1], scale=1.0), reads=[btM], writes=["qs"])
                            A("act", lambda e: e.activation(out=msq[:], in_=msq[:], func=ACT.Exp, scale=-0.5), reads=["qs"], writes=["qs"])
                            A("dve", lambda e, bkO=bkO, l=l: e.scalar_tensor_tensor(out=onb[:], in0=bkO[:], scalar=vecT[:, 0, VROW["hg_norm"] + l:VROW["hg_norm"] + l + 1],
                                                                                         in1=msq[:], op0=ALU.mult, op1=ALU.mult),
                                   reads=[btO, "qs"], writes=["Rr"])
                            A("dve", lambda e: e.tensor_tensor(out=onb[:], in0=onb[:], in1=sgb[:], op=ALU.mult), reads=["Rr", "sgb"], writes=["Rr"])
                            A("dve", lambda e: e.scalar_tensor_tensor(out=onb[:], in0=tmb[:], scalar=1.0, in1=onb[:], op0=ALU.add, op1=ALU.mult),
                                   reads=["Rr", "tmb"], writes=["Rr"])
                            def yhg(e, hh=hh, fl=yflags[hh]):
                                if fl["first"] == "hg":
                                    return e.tensor_copy(out=ysum[:, hh, :], in_=onb[:])
                                return e.tensor_tensor(out=ysum[:, hh, :], in0=ysum[:, hh, :], in1=onb[:], op=ALU.add)
                            A("dve", yhg, reads=["Rr", f"ysum{hh}"], writes=[f"ysum{hh}"], tag=("hg", hh, yflags[hh]))
                            cur[0] = None
                        for b in range(4):
                            flush_merged(rg_lists[b], hg_lists[2 * b] + hg_lists[2 * b + 1])

                        si = nslot()
                        slab = slabs[si]
                        stok = f"slab{si}"
                        A("sp", lambda e, slab=slab, l=l, ds=d_slab[si]: e.dma_start(
                            out=slab[:, :].rearrange("p (k n) -> p k n", k=KC), in_=wo_s[l]).then_inc(ds, 16), writes=[stok], dma=d_slab[si])
                        yreads = [f"ysum{c}" for c in range(KC)]
                        for m in range(KC):
                            bk, bt = nb()

                            def womm(e, bk=bk, slab=slab, m=m):
                                ins = None
                                for kc in range(KC):
                                    ins = e.matmul(bk[:], slab[:, kc * D + m * 128: kc * D + (m + 1) * 128], ysum[:, kc, :], start=(kc == 0), stop=(kc == KC - 1))
                                return ins
                            A("pe", womm, reads=yreads + [stok], writes=[bt])
                            A("dve", lambda e, bk=bk, m=m: e.scalar_tensor_tensor(out=xres[:, m, :], in0=bk[:], scalar=0.5, in1=xres[:, m, :], op0=ALU.mult, op1=ALU.add),
                                   reads=[bt, f"xres{m}"], writes=[f"xres{m}"])

                        rms_rstd([(xres[:, c, :], f"xres{c}") for c in range(KC)], None, EPS, onesD, rstd[:], "rstd")
                        for c in range(KC):
                            A("dve", lambda e, c=c, l=l: e.scalar_tensor_tensor(
                                out=hbuf[:, c, :], in0=xres[:, c, :], scalar=V(VROW["norm_mlp"] + l, c), in1=rstd[:],
                                op0=ALU.mult, op1=ALU.mult), reads=[f"xres{c}", "rstd"], writes=[f"h{c}"])
                        for sg in range(2):
                            uslabs = []
                            for half in range(2):
                                fg = sg * 2 + half
                                si = nslot()
                                A("sp", lambda e, slab=slabs[si], l=l, fg=fg, ds=d_slab[si]: e.dma_start(
                                    out=slab[:, :].rearrange("p (k n) -> p k n", k=KC), in_=wu_s[l, :, :, fg * 1024:(fg + 1) * 1024]).then_inc(ds, 16),
                                    writes=[f"slab{si}"], dma=d_slab[si])
                                uslabs.append((slabs[si], f"slab{si}"))
                                for f8 in range(8):
                                    f = half * 8 + f8
                                    bk, bt = nb()

                                    def upmm(e, bk=bk, slab=slabs[si], f8=f8):
                                        ins = None
                                        for kc in range(KC):
                                            ins = e.matmul(bk[:], slab[:, kc * 1024 + f8 * 128: kc * 1024 + (f8 + 1) * 128], hbuf[:, kc, :], start=(kc == 0), stop=(kc == KC - 1))
                                        return ins
                                    A("pe", upmm, reads=hreads + [f"slab{si}"], writes=[bt])
                                    r_ = rl[f % 2]
                                    A("act", lambda e, bk=bk, r_=r_: e.activation(out=r_[:], in_=bk[:], func=ACT.Relu), reads=[bt], writes=[f"rl{f % 2}"])
                                    A("pool", lambda e, r_=r_, f=f: e.tensor_tensor(out=hidden[:, f, :], in0=r_[:], in1=r_[:], op=ALU.mult),
                                           reads=[f"rl{f % 2}"], writes=[f"hid{f}", "iob"])
                            dslabs = []
                            for half in range(2):
                                fg = sg * 2 + half
                                si = nslot()
                                A("sp", lambda e, slab=slabs[si], l=l, fg=fg, ds=d_slab[si]: e.dma_start(
                                    out=slab[:, :].rearrange("p (k n) -> p k n", k=8), in_=wd_s[l, :, fg * 8:(fg + 1) * 8, :]).then_inc(ds, 16),
                                    writes=[f"slab{si}"], dma=d_slab[si])
                                dslabs.append((slabs[si], f"slab{si}"))
                            for m in range(KC):
                                bk, bt = nb()

                                def dnmm(e, bk=bk, m=m, dslabs=dslabs):
                                    ins = None
                                    for f in range(16):
                                        slab = dslabs[f // 8][0]
                                        f8 = f % 8
                                        ins = e.matmul(bk[:], slab[:, f8 * D + m * 128: f8 * D + (m + 1) * 128], hidden[:, f, :], start=(f == 0), stop=(f == 15))
                                    return ins
                                A("pe", dnmm, reads=[f"hid{f}" for f in range(16)] + [dslabs[0][1], dslabs[1][1]], writes=[bt])
                                A("dve", lambda e, bk=bk, m=m: e.tensor_tensor(out=xres[:, m, :], in0=xres[:, m, :], in1=bk[:], op=ALU.add),
                                       reads=[bt, f"xres{m}"], writes=[f"xres{m}"])

                    rms_rstd([(xres[:, c, :], f"xres{c}") for c in range(KC)], None, EPS, onesD, rstd[:], "rstd")
                    for c in range(KC):
                        A("dve", lambda e, c=c: e.scalar_tensor_tensor(
                            out=xres[:, c, :], in0=xres[:, c, :], scalar=V(VROW["norm_final"], c), in1=rstd[:],
                            op0=ALU.mult, op1=ALU.mult), reads=[f"xres{c}", "rstd"], writes=[f"xres{c}"])
                    for b in range(4):
                        for cg in range(2):
                            bk, bt = nb()

                            def tro(e, bk=bk, b=b, cg=cg):
                                ins = None
                                for cc in range(4):
                                    c = cg * 4 + cc
                                    ins = e.transpose(bk[:, cc * 128:(cc + 1) * 128], xres[:, c, b * 128:(b + 1) * 128], ident32[:])
                                return ins
                            A("pe", tro, reads=[f"xres{c}" for c in range(cg * 4, cg * 4 + 4)], writes=[bt])
                            if cg == 0:
                                A("act", lambda e, bk=bk, b=b, cg=cg: e.activation(out=iob[:, b, cg * 512:(cg + 1) * 512], in_=bk[:], func=ACT.Copy),
                                       reads=[bt], writes=[f"io_{b}_{cg}"])
                            else:
                                A("dve", lambda e, bk=bk, b=b, cg=cg: e.tensor_copy(out=iob[:, b, cg * 512:(cg + 1) * 512], in_=bk[:]),
                                       reads=[bt], writes=[f"io_{b}_{cg}"])
                    A("sp", lambda e, row0=row0: e.dma_start(out=out_d[row0:row0 + TT, :].rearrange("(b p) d -> p b d", p=128), in_=iob[:]).then_inc(d_o, 16),
                           reads=["iob"] + [f"hid{f}" for f in range(16)] + [f"io_{b}_{cg}" for b in range(4) for cg in range(2)], writes=["OUT"], dma=d_o)
            A("sp", lambda e: e.nop(), reads=["OUT"])
            T2.finish(block, sems_main(nc, es, sems))
            stats1 = T2.stats
    build_program.stats = (stats0, stats1)
    return nc


def sems_main(nc, es, sems):
    return {e: es.enter_context(nc.semaphore("m_" + e)) for e in ENGS}


_CACHE = {}


def kernel(x, lb_logits, norm_mix, w_in, conv_w, conv_b, w_r, b_r, w_i, b_i, lam, hg_norm, w_out, norm_mlp, w_up, w_down, norm_final):
    x = np.asarray(x)
    B, T, _ = x.shape
    NL = int(np.asarray(w_in).shape[0])
    ncores = 8
    assert B % ncores == 0
    NSEQ = B // ncores
    key = (NSEQ, T, NL)
    if key not in _CACHE:
        _CACHE[key] = build_program(NSEQ, T, NL)
    nc = _CACHE[key]
    f32 = lambda a: np.ascontiguousarray(np.asarray(a, dtype=np.float32))
    shared = {
        "lb_logits": f32(lb_logits), "norm_mix": f32(norm_mix), "w_in": f32(w_in), "conv_w": f32(conv_w),
        "conv_b": f32(conv_b), "w_r": f32(w_r), "b_r": f32(b_r), "w_i": f32(w_i), "b_i": f32(b_i),
        "lam": f32(lam), "hg_norm": f32(hg_norm), "w_out": f32(w_out), "norm_mlp": f32(norm_mlp),
        "w_up": f32(w_up), "w_down": f32(w_down), "norm_final": f32(norm_final).reshape(1, D),
    }
    xs = f32(x).reshape(ncores, NSEQ * T, D)
    in_maps = []
    for i in range(ncores):
        m = dict(shared)
        m["x"] = xs[i]
        in_maps.append(m)
    res = run_bass_kernel_spmd(nc, in_maps, core_ids=list(range(ncores)))
    out = np.stack([np.asarray(r["out"]) for r in res.results], axis=0)
    return out.reshape(B, T, D).astype(np.float32)
```

```python
import contextlib
import numpy as np
import concourse.bass as bass
import concourse.mybir as mybir
from concourse.bass_utils import run_bass_kernel_spmd

F32 = mybir.dt.float32
BF16 = mybir.dt.bfloat16
ACT = mybir.ActivationFunctionType
ALU = mybir.AluOpType

D = 1024
KC = 8
DIN = 8192
DFF = 4096
TT = 512
NCH = TT // 32
EPS = 1e-6
RG_C = 8.0
ENGS = ("pe", "act", "dve", "pool", "sp")


class _Op:
    __slots__ = ("eng", "emit", "deps", "needs_inc", "semval", "dsem", "dval")

    def __init__(self, eng, emit):
        self.eng = eng
        self.emit = emit
        self.deps = set()
        self.needs_inc = False
        self.semval = 0
        self.dsem = None
        self.dval = 0


class TK:
    def __init__(self):
        self.ops = []
        self.last_w = {}
        self.readers = {}
        self.dma_cnt = {}

    def add(self, eng, emit, reads=(), writes=(), dma=None, ndma=1):
        idx = len(self.ops)
        op = _Op(eng, emit)
        deps = op.deps
        lw = self.last_w
        rd = self.readers
        for t in reads:
            w = lw.get(t)
            if w is not None:
                deps.add((w, 0))
        for t in writes:
            w = lw.get(t)
            if w is not None:
                deps.add((w, 1))
            for r in rd.get(t, ()):
                deps.add((r, 2))
        for t in reads:
            rd.setdefault(t, []).append(idx)
        for t in writes:
            lw[t] = idx
            rd[t] = []
        if dma is not None:
            op.dsem = dma
            c = self.dma_cnt.get(dma.num, 0) + 16 * ndma
            self.dma_cnt[dma.num] = c
            op.dval = c
        self.ops.append(op)
        return idx

    def finish(self, block, sems):
        ops = self.ops
        real = []
        for i, op in enumerate(ops):
            rl = []
            for (d, kind) in op.deps:
                if d == i:
                    continue
                p = ops[d]
                if p.dsem is None and op.dsem is None and p.eng == op.eng:
                    if op.eng == "pe" or kind != 0:
                        continue
                rl.append(d)
                if p.dsem is None:
                    p.needs_inc = True
            real.append(rl)
        cnt = {e: 0 for e in ENGS}
        for op in ops:
            if op.dsem is None and op.needs_inc:
                cnt[op.eng] += 1
                op.semval = cnt[op.eng]
        per_eng = {e: [] for e in ENGS}
        for i, op in enumerate(ops):
            per_eng[op.eng].append(i)
        self.stats = {e: len(v) for e, v in per_eng.items()}
        self.stats["sem_max"] = dict(cnt)

        def run(eng_name, eh):
            waited = {}
            for i in per_eng[eng_name]:
                op = ops[i]
                waits = {}
                for d in real[i]:
                    p = ops[d]
                    if p.dsem is not None:
                        s, v = p.dsem, p.dval
                    else:
                        s, v = sems[p.eng], p.semval
                    if waited.get(s.num, 0) < v and waits.get(s.num, (None, 0))[1] < v:
                        waits[s.num] = (s, v)
                for sn, (s, v) in waits.items():
                    eh.wait_ge(s, v)
                    waited[sn] = v
                ins = op.emit(eh)
                if op.dsem is None and op.needs_inc:
                    ins.then_inc(sems[eng_name], 1)

        @block.tensor
        def _(e):
            run("pe", e)

        @block.scalar
        def _(e):
            run("act", e)

        @block.vector
        def _(e):
            run("dve", e)

        @block.gpsimd
        def _(e):
            run("pool", e)

        @block.sync
        def _(e):
            run("sp", e)


def _vrow(NL):
    rows = {}
    r = 0
    for nm in ("norm_mix", "norm_mlp", "conv_b", "b_r", "b_i", "lam", "lb_logits"):
        rows[nm] = r
        r += NL
    rows["conv_w"] = r
    r += 4 * NL
    rows["norm_final"] = r
    r += 1
    rows["hg_norm"] = r
    r += NL
    return rows, r


def build_program(NSEQ, T, NL):
    assert T % TT == 0
    NTILE = T // TT
    nc = bass.Bass("TRN2", target_bir_lowering=False)
    dram = {}

    def din(name, shape):
        dram[name] = nc.dram_tensor(name, list(shape), F32, kind="ExternalInput").ap()
        return dram[name]

    x_d = din("x", [NSEQ * T, D])
    lbl_d = din("lb_logits", [NL, D])
    nmix_d = din("norm_mix", [NL, D])
    win_d = din("w_in", [NL, D, DIN])
    cw_d = din("conv_w", [NL, 4, D])
    cb_d = din("conv_b", [NL, D])
    wr_d = din("w_r", [NL, 4, 256, 256])
    br_d = din("b_r", [NL, D])
    wi_d = din("w_i", [NL, 4, 256, 256])
    bi_d = din("b_i", [NL, D])
    lam_d = din("lam", [NL, D])
    hgn_d = din("hg_norm", [NL, 128])
    wo_d = din("w_out", [NL, D, D])
    nmlp_d = din("norm_mlp", [NL, D])
    wu_d = din("w_up", [NL, D, DFF])
    wd_d = din("w_down", [NL, DFF, D])
    nf_d = din("norm_final", [1, D])
    out_d = nc.dram_tensor("out", [NSEQ * T, D], F32, kind="ExternalOutput").ap()

    def dscr(name, shape):
        return nc.dram_tensor(name, list(shape), BF16, kind="Internal").ap()

    win_s = dscr("win_s", [NL, 128, KC, DIN])
    wg_s = dscr("wg_s", [NL, 128, 4, 2, 2, 256])
    dg_s = dscr("dg_s", [NL, 128, 8, 4, 128])
    wo_s = dscr("wo_s", [NL, 128, KC, D])
    wu_s = dscr("wu_s", [NL, 128, KC, DFF])
    wd_s = dscr("wd_s", [NL, 128, 32, D])

    VROW, NV = _vrow(NL)
    tk = TK()
    es = contextlib.ExitStack()
    with es:
        def sem(name):
            return es.enter_context(nc.semaphore(name))

        sems = {e: sem("s_" + e) for e in ENGS}
        block_holder = []

        def sb(name, shape, dt):
            return es.enter_context(nc.sbuf_tensor(name, list(shape), dt))

        ident32 = sb("ident32", [128, 128], F32)
        identbf = sb("identbf", [128, 128], BF16)
        onesD = sb("onesD", [128, 128], BF16)
        onesH = sb("onesH", [128, 128], BF16)
        mask32 = sb("mask32", [32, NCH, 32], F32)
        vecT = sb("vecT", [128, KC, NV], F32)
        chalf = sb("chalf", [128, KC, NL], F32)
        cfull = sb("cfull", [128, KC, NL], F32)
        nchalf = sb("nchalf", [128, KC, NL], F32)
        epsb = sb("epsb", [128, 1], F32)
        brh = sb("brh", [128, KC, NL], F32)
        bih = sb("bih", [128, KC, NL], F32)
        kbp = sb("kbp", [128, KC, NL], F32)
        kbn = sb("kbn", [128, KC, NL], F32)
        kb1 = sb("kb1", [128, KC, NL], F32)
        Sc32 = sb("Sc32", [128, NL * 8, 128], F32)
        Sc16 = sb("Sc16", [128, NL * 8, 128], BF16)
        hcar = sb("hcar", [128, NL * 8], F32)
        xkeep = sb("xkeep", [128, NL * 8, 3], BF16)

        d_small = sem("d_small")

        def V(row, c):
            return vecT[:, c, row:row + 1]

        pes = contextlib.ExitStack()
        with pes:
            def psb(name, shape, dt):
                return pes.enter_context(nc.sbuf_tensor(name, list(shape), dt))

            st32 = [psb(f"st32_{i}", [128, 8192], F32) for i in range(2)]
            st16 = [psb(f"st16_{i}", [128, 8192], BF16) for i in range(2)]
            vrow = psb("vrow", [128, D], F32)
            d_ld = [sem("d_pl0"), sem("d_pl1")]
            d_st = [sem("d_ps0"), sem("d_ps1")]
            pps = pes.enter_context(nc.psum_tensor("pps", [128, 512], F32))
            block = pes.enter_context(nc.Block())

            def mk_ident(e):
                e.memset(ident32[:], 0.0)
                return e.affine_select(out=ident32[:], in_=ident32[:], pattern=[[-1, 128]],
                                       compare_op=ALU.not_equal, fill=1.0, base=0, channel_multiplier=1)
            tk.add("pool", mk_ident, writes=["ident32"])
            tk.add("pool", lambda e: e.tensor_copy(out=identbf[:], in_=ident32[:]), reads=["ident32"], writes=["identbf"])
            tk.add("pool", lambda e: e.memset(onesD[:], 1.0 / 1024.0), writes=["onesD"])
            tk.add("pool", lambda e: e.memset(onesH[:], 1.0 / 128.0), writes=["onesH"])

            def mk_mask(e):
                e.memset(mask32[:], 1.0)
                return e.affine_select(out=mask32[:], in_=mask32[:], pattern=[[0, NCH], [1, 32]],
                                       compare_op=ALU.is_ge, fill=0.0, base=0, channel_multiplier=-1)
            tk.add("pool", mk_mask, writes=["mask32"])
            tk.add("pool", lambda e: e.memset(vrow[:], 0.0), writes=["vrow"])
            tk.add("pool", lambda e: e.memset(epsb[:], EPS), writes=["epsb"])

            def ld_small(e):
                ins = None
                for nm, ap in (("norm_mix", nmix_d), ("norm_mlp", nmlp_d), ("conv_b", cb_d), ("b_r", br_d),
                               ("b_i", bi_d), ("lam", lam_d), ("lb_logits", lbl_d)):
                    r = VROW[nm]
                    ins = e.dma_start(out=vrow[r:r + NL, :], in_=ap[:, :]).then_inc(d_small, 16)
                r = VROW["conv_w"]
                ins = e.dma_start(out=vrow[r:r + 4 * NL, :], in_=cw_d.rearrange("l j d -> (l j) d")).then_inc(d_small, 16)
                r = VROW["norm_final"]
                ins = e.dma_start(out=vrow[r:r + 1, :], in_=nf_d[:, :]).then_inc(d_small, 16)
                r = VROW["hg_norm"]
                ins = e.dma_start(out=vrow[r:r + NL, 0:128], in_=hgn_d[:, :]).then_inc(d_small, 16)
                return ins
            tk.add("sp", ld_small, reads=["vrow"], writes=["vrow"], dma=d_small, ndma=10)
            for half in range(2):
                def tr_small(e, half=half):
                    ins = None
                    for cc in range(4):
                        c = half * 4 + cc
                        ins = e.transpose(pps[:, cc * 128:cc * 128 + NV], vrow[0:NV, c * 128:(c + 1) * 128], ident32[0:NV, 0:NV])
                    return ins
                tk.add("pe", tr_small, reads=["vrow", "ident32"], writes=["pps"])
                tk.add("dve", lambda e, half=half: e.tensor_copy(
                    out=vecT[:, half * 4:(half + 1) * 4, :],
                    in_=pps[:].rearrange("p (c n) -> p c n", c=4)[:, :, 0:NV]), reads=["pps"], writes=["vecT"])

            lam_v = vecT[:, :, VROW["lam"]:VROW["lam"] + NL]
            lbl_v = vecT[:, :, VROW["lb_logits"]:VROW["lb_logits"] + NL]
            tA = psb("tA", [128, KC, NL], F32)
            tB = psb("tB", [128, KC, NL], F32)
            tC = psb("tC", [128, KC, 1], F32)

            tk.add("act", lambda e: e.activation(out=tA[:], in_=lam_v, func=ACT.Exp, scale=-1.0), reads=["vecT"], writes=["tA"])
            tk.add("dve", lambda e: e.tensor_scalar(out=tB[:], in0=tA[:], scalar1=0.2, scalar2=-0.25, op0=ALU.mult, op1=ALU.add), reads=["tA"], writes=["tB"])
            tk.add("dve", lambda e: e.tensor_tensor(out=tB[:], in0=tB[:], in1=tA[:], op=ALU.mult), reads=["tA", "tB"], writes=["tB"])
            tk.add("dve", lambda e: e.tensor_scalar(out=tB[:], in0=tB[:], scalar1=1.0 / 3.0, scalar2=None, op0=ALU.add), reads=["tB"], writes=["tB"])
            tk.add("dve", lambda e: e.tensor_tensor(out=tB[:], in0=tB[:], in1=tA[:], op=ALU.mult), reads=["tA", "tB"], writes=["tB"])
            tk.add("dve", lambda e: e.tensor_scalar(out=tB[:], in0=tB[:], scalar1=-0.5, scalar2=None, op0=ALU.add), reads=["tB"], writes=["tB"])
            tk.add("dve", lambda e: e.tensor_tensor(out=tB[:], in0=tB[:], in1=tA[:], op=ALU.mult), reads=["tA", "tB"], writes=["tB"])
            tk.add("dve", lambda e: e.tensor_scalar(out=tB[:], in0=tB[:], scalar1=1.0, scalar2=None, op0=ALU.add), reads=["tB"], writes=["tB"])
            tk.add("dve", lambda e: e.tensor_tensor(out=tB[:], in0=tB[:], in1=tA[:], op=ALU.mult), reads=["tA", "tB"], writes=["tB"])
            tk.add("dve", lambda e: e.tensor_scalar(out=cfull[:], in0=tB[:], scalar1=-RG_C, scalar2=None, op0=ALU.mult), reads=["tB"], writes=["cfull"])
            tk.add("dve", lambda e: e.tensor_scalar(out=chalf[:], in0=tB[:], scalar1=-0.5 * RG_C, scalar2=None, op0=ALU.mult), reads=["tB"], writes=["chalf"])
            tk.add("dve", lambda e: e.tensor_scalar(out=nchalf[:], in0=tB[:], scalar1=0.5 * RG_C, scalar2=None, op0=ALU.mult), reads=["tB"], writes=["nchalf"])
            tk.add("dve", lambda e: e.tensor_scalar(out=brh[:], in0=vecT[:, :, VROW["b_r"]:VROW["b_r"] + NL], scalar1=0.5, scalar2=None, op0=ALU.mult), reads=["vecT"], writes=["brh"])
            tk.add("dve", lambda e: e.tensor_scalar(out=bih[:], in0=vecT[:, :, VROW["b_i"]:VROW["b_i"] + NL], scalar1=0.5, scalar2=None, op0=ALU.mult), reads=["vecT"], writes=["bih"])
            tk.add("dve", lambda e: e.tensor_reduce(out=tC[:], in_=lbl_v, axis=mybir.AxisListType.X, op=ALU.max), reads=["vecT"], writes=["tC"])
            tk.add("dve", lambda e: e.tensor_tensor(out=tA[:], in0=lbl_v, in1=tC[:].to_broadcast([128, KC, NL]), op=ALU.subtract), reads=["vecT", "tC", "tB"], writes=["tA"])
            tk.add("act", lambda e: e.activation(out=tA[:], in_=tA[:], func=ACT.Exp), reads=["tA"], writes=["tA"])
            tk.add("dve", lambda e: e.tensor_reduce(out=tC[:], in_=tA[:], axis=mybir.AxisListType.X, op=ALU.add), reads=["tA"], writes=["tC"])
            tk.add("dve", lambda e: e.reciprocal(out=tC[:], in_=tC[:]), reads=["tC"], writes=["tC"])
            tk.add("dve", lambda e: e.tensor_tensor(out=tA[:], in0=tA[:], in1=tC[:].to_broadcast([128, KC, NL]), op=ALU.mult), reads=["tA", "tC"], writes=["tA"])
            tk.add("dve", lambda e: e.memset(tB[:], 0.0), reads=["tB"], writes=["tB"])
            for l in range(1, NL):
                tk.add("dve", lambda e, l=l: e.tensor_tensor(out=tB[:, :, l:l + 1], in0=tB[:, :, l - 1:l], in1=tA[:, :, l:l + 1], op=ALU.add), reads=["tA", "tB"], writes=["tB"])
            tk.add("dve", lambda e: e.tensor_scalar(out=tB[:], in0=tB[:], scalar1=0.0, scalar2=1.0, op0=ALU.max, op1=ALU.min), reads=["tB"], writes=["tB"])
            tk.add("dve", lambda e: e.tensor_scalar(out=kbp[:], in0=tB[:], scalar1=-0.5, scalar2=0.5, op0=ALU.mult, op1=ALU.add), reads=["tB"], writes=["kbp"])
            tk.add("dve", lambda e: e.tensor_scalar(out=kbn[:], in0=tB[:], scalar1=0.5, scalar2=-0.5, op0=ALU.mult, op1=ALU.add), reads=["tB"], writes=["kbn"])
            tk.add("dve", lambda e: e.tensor_scalar(out=kb1[:], in0=tB[:], scalar1=0.5, scalar2=0.5, op0=ALU.mult, op1=ALU.add), reads=["tB"], writes=["kb1"])

            rnd = [0]
            scr_toks = []

            def conv_round(load_fn, cast_fns, store_fn, nld=1, nst=1, xreads=()):
                i = rnd[0] % 2
                rnd[0] += 1
                s32, s16 = st32[i], st16[i]
                tk.add("sp", lambda e: load_fn(e, s32, d_ld[i]), reads=[], writes=[f"st32_{i}"], dma=d_ld[i], ndma=nld)
                for k, (eng, fn) in enumerate(cast_fns):
                    tk.add(eng, lambda e, fn=fn: fn(e, s32, s16), reads=[f"st32_{i}"] + list(xreads), writes=[f"st16_{i}_{k}"])
                tok = f"scr{len(scr_toks)}"
                scr_toks.append(tok)
                tk.add("sp", lambda e: store_fn(e, s16, d_st[i]), reads=[f"st16_{i}_{k}" for k in range(8)] + [f"st32_{i}"],
                       writes=[tok], dma=d_st[i], ndma=nst)

            def cast_generic(eng, lo, hi):
                if eng == "act":
                    return ("act", lambda e, s32, s16: e.activation(out=s16[:, lo:hi], in_=s32[:, lo:hi], func=ACT.Copy))
                return (eng, lambda e, s32, s16: e.tensor_copy(out=s16[:, lo:hi], in_=s32[:, lo:hi]))

            def split3(n):
                a = (n * 3 // 8) // 128 * 128
                b = a + (n * 3 // 8) // 128 * 128
                return [cast_generic("dve", 0, a), cast_generic("act", a, b), cast_generic("pool", b, n)]

            RG_G = (0, 1, 6)
            HG_G = (2, 3, 4, 5, 7)
            for l in range(NL):
                for kc in range(KC):
                    def ld(e, s32, ds, l=l, kc=kc):
                        return e.dma_start(out=s32[:, :], in_=win_d[l, kc * 128:(kc + 1) * 128, :]).then_inc(ds, 16)
                    casts = []
                    engs = ["dve", "act", "dve", "pool", "act", "dve", "act", "dve"]
                    for g in range(8):
                        if g in RG_G:
                            gi = RG_G.index(g)

                            def cf(e, s32, s16, g=g, gi=gi, eng=engs[g]):
                                src = s32[:, g * 1024:(g + 1) * 1024].rearrange("p (b c j) -> p b c j", b=4, c=2)
                                dst = s16[:, 0:3072].rearrange("p (b x) -> p b x", b=4)[:, :, gi * 256:(gi + 1) * 256].rearrange("p b (c j) -> p b c j", c=2)
                                if eng == "act":
                                    return e.activation(out=dst, in_=src, func=ACT.Copy)
                                return e.tensor_copy(out=dst, in_=src)
                        else:
                            gi = HG_G.index(g)

                            def cf(e, s32, s16, g=g, gi=gi, eng=engs[g]):
                                src = s32[:, g * 1024:(g + 1) * 1024].rearrange("p (h j) -> p h j", h=8)
                                dst = s16[:, 3072:8192].rearrange("p (h x) -> p h x", h=8)[:, :, gi * 128:(gi + 1) * 128]
                                if eng == "act":
                                    return e.activation(out=dst, in_=src, func=ACT.Copy)
                                return e.tensor_copy(out=dst, in_=src)
                        casts.append((engs[g], cf))

                    def stf(e, s16, ds, l=l, kc=kc):
                        return e.dma_start(out=win_s[l, :, kc, :], in_=s16[:, :]).then_inc(ds, 16)
                    conv_round(ld, casts, stf)
                def ld(e, s32, ds, l=l):
                    return e.dma_start(out=s32[:, :].rearrange("p (k n) -> p k n", k=KC),
                                       in_=wo_d[l].rearrange("(k p) n -> p k n", p=128)).then_inc(ds, 16)

                def stf(e, s16, ds, l=l):
                    return e.dma_start(out=wo_s[l], in_=s16[:, :].rearrange("p (k n) -> p k n", k=KC)).then_inc(ds, 16)
                conv_round(ld, split3(8192), stf)
                for k2 in range(KC // 2):
                    def ld(e, s32, ds, l=l, k2=k2):
                        return e.dma_start(out=s32[:, :].rearrange("p (k n) -> p k n", k=2),
                                           in_=wu_d[l, k2 * 256:(k2 + 1) * 256, :].rearrange("(k p) n -> p k n", p=128)).then_inc(ds, 16)

                    def stf(e, s16, ds, l=l, k2=k2):
                        return e.dma_start(out=wu_s[l, :, 2 * k2:2 * k2 + 2, :], in_=s16[:, :].rearrange("p (k n) -> p k n", k=2)).then_inc(ds, 16)
                    conv_round(ld, split3(8192), stf)
                for f8 in range(4):
                    def ld(e, s32, ds, l=l, f8=f8):
                        return e.dma_start(out=s32[:, :].rearrange("p (k n) -> p k n", k=8),
                                           in_=wd_d[l, f8 * 1024:(f8 + 1) * 1024, :].rearrange("(k p) n -> p k n", p=128)).then_inc(ds, 16)

                    def stf(e, s16, ds, l=l, f8=f8):
                        return e.dma_start(out=wd_s[l, :, 8 * f8:8 * f8 + 8, :], in_=s16[:, :].rearrange("p (k n) -> p k n", k=8)).then_inc(ds, 16)
                    conv_round(ld, split3(8192), stf)
                def ld(e, s32, ds, l=l):
                    e.dma_start(out=s32[:, 0:2048].rearrange("p (k n) -> p k n", k=8),
                                in_=wr_d[l].rearrange("b (k p) n -> p (b k) n", p=128)).then_inc(ds, 16)
                    return e.dma_start(out=s32[:, 2048:4096].rearrange("p (k n) -> p k n", k=8),
                                       in_=wi_d[l].rearrange("b (k p) n -> p (b k) n", p=128)).then_inc(ds, 16)

                def dgf(e, s32, s16, l=l):
                    ins = None
                    for c in range(8):
                        for j in range(4):
                            o = 4096 + (c * 4 + j) * 128
                            ins = e.activation(out=s16[:, o:o + 128], in_=identbf[:], func=ACT.Copy,
                                               scale=V(VROW["conv_w"] + 4 * l + j, c))
                    return ins

                def stf(e, s16, ds, l=l):
                    for g in range(2):
                        e.dma_start(out=wg_s[l, :, :, g, :, :],
                                    in_=s16[:, g * 2048:(g + 1) * 2048].rearrange("p (b k n) -> p b k n", b=4, k=2)).then_inc(ds, 16)
                    return e.dma_start(out=dg_s[l], in_=s16[:, 4096:8192].rearrange("p (c j n) -> p c j n", c=8, j=4)).then_inc(ds, 16)
                conv_round(ld, [cast_generic("dve", 0, 4096), ("act", dgf)], stf, nld=2, nst=3, xreads=("vecT", "identbf"))
            tk.add("sp", lambda e: e.nop(), reads=scr_toks + ["vecT", "kbp", "kbn", "kb1", "chalf", "cfull", "nchalf", "brh", "bih", "mask32", "onesD", "onesH", "identbf", "epsb"], writes=["BAR"])
            for eng in ("pe", "act", "dve", "pool"):
                tk.add(eng, lambda e: e.nop(), reads=["BAR"], writes=[f"BAR_{eng}"])
            tk.finish(block, sems)
            stats0 = tk.stats

        tk2 = TK()
        tk2.dma_cnt = dict(tk.dma_cnt)
        mes = contextlib.ExitStack()
        with mes:
            def msb(name, shape, dt):
                return mes.enter_context(nc.sbuf_tensor(name, list(shape), dt))

            xres = msb("xres", [128, KC, TT], F32)
            hbuf = msb("hbuf", [128, KC, TT], BF16)
            ysum = msb("ysum", [128, KC, TT], BF16)
            iob = msb("iob", [128, 4, D], F32)
            hidden = iob[:].rearrange("p b d -> p (b d)").bitcast(BF16)[:, 0:16 * TT].rearrange("p (f t) -> p f t", f=16)
            NSLOT = 4
            slabs = [msb(f"slab{i}", [128, 8192 if i < 2 else 5120], BF16) for i in range(NSLOT)]
            d_slab = [sem(f"d_slab{i}") for i in range(NSLOT)]
            d_x = sem("d_x")
            d_o = sem("d_o")
            xa_bf = msb("xa_bf", [128, 2, TT + 3], BF16)
            xc32 = msb("xc32", [128, 2, TT], F32)
            xcbf = msb("xcbf", [128, 2, TT], BF16)
            rgA = msb("rgA", [128, 2, TT], F32)
            rgI = msb("rgI", [128, 2, TT], F32)
            rgS = msb("rgS", [128, 2, TT], F32)
            rgG = msb("rgG", [128, 2, TT], F32)
            qs = msb("qs", [128, TT], F32)
            fz = msb("fz", [128, TT], F32)
            kk = msb("kk", [128, TT], F32)
            fm = msb("fm", [128, TT], F32)
            Pp = msb("Pp", [128, TT], F32)
            Rr = msb("Rr", [128, TT], F32)
            qt = msb("qt", [128, TT], BF16)
            kt = msb("kt", [128, TT], BF16)
            qt2 = msb("qt2", [128, TT], BF16)
            vF = msb("vF", [128, TT], BF16)
            kT32 = msb("kT32", [32, NCH, 128], BF16)
            v32 = msb("v32", [32, NCH, 128], BF16)
            Sm32 = msb("Sm32", [32, NCH, 32], BF16)
            Uall = msb("Uall", [128, NCH, 128], F32)
            Sbf = msb("Sbf", [128, NCH, 128], BF16)
            o2 = msb("o2", [128, TT], BF16)
            msq, onb = qs, Rr
            sgb = msb("sgb", [128, TT], F32)
            tmb = msb("tmb", [128, TT], F32)
            sqb = msb("sqb", [128, TT], BF16)
            rstd = msb("rstd", [128, TT], F32)
            rl = [msb(f"rl{i}", [128, TT], BF16) for i in range(2)]
            banks = [mes.enter_context(nc.psum_tensor(f"bank{i}", [128, TT], F32)) for i in range(8)]
            block = mes.enter_context(nc.Block())
            T2 = tk2

            cur = [None]

            DEFC = {"pe": 1.8, "act": 0.65, "dve": 0.6, "pool": 1.0, "sp": 5.0}

            def A(eng, emit, reads=(), writes=(), dma=None, ndma=1, tag=None, cost=None):
                if cur[0] is not None:
                    cur[0].append((eng, emit, reads, writes, dma, ndma, tag, DEFC[eng] if cost is None else cost))
                else:
                    T2.add(eng, emit, reads=reads, writes=writes, dma=dma, ndma=ndma)

            def flush_merged(L1, L2):
                n1, n2 = len(L1), len(L2)
                i = j = 0
                merged = []
                efree = {e: 0.0 for e in ENGS}
                wdone = {}
                rdone = {}

                def est(op):
                    eng, _, reads, writes, dma, _, _, cost = op
                    t = efree[eng]
                    for tk_ in reads:
                        t = max(t, wdone.get(tk_, 0.0) + 0.15)
                    for tk_ in writes:
                        t = max(t, wdone.get(tk_, 0.0), rdone.get(tk_, 0.0) + 0.1)
                    return t

                def commit(op):
                    eng, _, reads, writes, dma, _, _, cost = op
                    t0 = est(op)
                    if eng == "sp":
                        efree[eng] = t0 + 0.1
                    else:
                        efree[eng] = t0 + cost
                    t1 = t0 + cost
                    for tk_ in reads:
                        rdone[tk_] = max(rdone.get(tk_, 0.0), t1)
                    for tk_ in writes:
                        wdone[tk_] = t1
                    merged.append(op)

                while i < n1 or j < n2:
                    if j >= n2:
                        commit(L1[i]); i += 1
                    elif i >= n1:
                        commit(L2[j]); j += 1
                    else:
                        a, b = est(L1[i]), est(L2[j])
                        if a < b:
                            commit(L1[i]); i += 1
                        else:
                            commit(L2[j]); j += 1
                seen = {}
                for op in merged:
                    tag = op[6]
                    if tag is not None:
                        kind, key, flags = tag
                        if key not in seen:
                            seen[key] = kind
                            flags["first"] = kind
                for (eng, emit, reads, writes, dma, ndma, tag, cost) in merged:
                    T2.add(eng, emit, reads=reads, writes=writes, dma=dma, ndma=ndma)

            POOLS = {"all": list(range(8)), "rg": [0, 1, 2], "hg": [3, 4, 5, 6, 7]}
            bank_rr = {"all": 0, "rg": 0, "hg": 0}

            def nb(pool="all"):
                lst = POOLS[pool]
                i = lst[bank_rr[pool] % len(lst)]
                bank_rr[pool] += 1
                return banks[i], f"bank{i}"

            slot_rr = {"big": 0, "small": 0}

            def nslot(kind="big"):
                i = slot_rr[kind] % 2 + (0 if kind == "big" else 2)
                slot_rr[kind] += 1
                return i

            A("pool", lambda e: e.memset(fm[:], 0.0), writes=["fm"])
            A("pool", lambda e: e.memset(xa_bf[:], 0.0), writes=["xa_bf"])

            def rms_rstd(src_chunks, reads, eps, ones, out_rstd, wtok):
                bk, bt = nb()
                n = len(src_chunks)
                for i, (ap, tok) in enumerate(src_chunks):
                    A("act", lambda e, ap=ap: e.activation(out=sqb[:], in_=ap, func=ACT.Square), reads=[tok], writes=["sqb"])
                    A("pe", lambda e, i=i: e.matmul(bk[:], ones[:], sqb[:], start=(i == 0), stop=(i == n - 1)), reads=["sqb"], writes=[bt])
                A("act", lambda e: e.activation(out=out_rstd, in_=bk[:], func=ACT.Ln, bias=epsb[:, 0:1], scale=1.0), reads=[bt], writes=[wtok])
                A("act", lambda e: e.activation(out=out_rstd, in_=out_rstd, func=ACT.Exp, scale=-0.5), reads=[wtok], writes=[wtok])

            for sq_i in range(NSEQ):
                A("pool", lambda e: e.memset(Sc32[:], 0.0), writes=[f"Sc32_{l}_{h}" for l in range(NL) for h in range(8)])
                A("pool", lambda e: e.memset(Sc16[:], 0.0), writes=[f"Sc16_{l}_{h}" for l in range(NL) for h in range(8)])
                A("pool", lambda e: e.memset(hcar[:], 0.0), writes=[f"hcar_{l}_{c}" for l in range(NL) for c in range(8)])
                A("pool", lambda e: e.memset(xkeep[:], 0.0), writes=[f"xkeep_{l}_{c}" for l in range(NL) for c in range(8)])
                for ti in range(NTILE):
                    row0 = sq_i * T + ti * TT
                    A("sp", lambda e, row0=row0: e.dma_start(out=iob[:], in_=x_d[row0:row0 + TT, :].rearrange("(b p) d -> p b d", p=128)).then_inc(d_x, 16),
                           writes=["iob"] + [f"hid{f}" for f in range(16)] + [f"io_{b}_{cg}" for b in range(4) for cg in range(2)], dma=d_x)
                    for c in range(KC):
                        bk, bt = nb()

                        def trx(e, c=c, bk=bk):
                            ins = None
                            for b in range(4):
                                ins = e.transpose(bk[:, b * 128:(b + 1) * 128], iob[:, b, c * 128:(c + 1) * 128], ident32[:])
                            return ins
                        A("pe", trx, reads=["iob"], writes=[bt])
                        if c % 2 == 0:
                            A("act", lambda e, c=c, bk=bk: e.activation(out=xres[:, c, :], in_=bk[:], func=ACT.Copy), reads=[bt], writes=[f"xres{c}"])
                        else:
                            A("dve", lambda e, c=c, bk=bk: e.tensor_copy(out=xres[:, c, :], in_=bk[:]), reads=[bt], writes=[f"xres{c}"])

                    for l in range(NL):
                        rms_rstd([(xres[:, c, :], f"xres{c}") for c in range(KC)], None, EPS, onesD, rstd[:], "rstd")
                        for c in range(KC):
                            A("dve", lambda e, c=c, l=l: e.scalar_tensor_tensor(
                                out=hbuf[:, c, :], in0=xres[:, c, :], scalar=V(VROW["norm_mix"] + l, c), in1=rstd[:],
                                op0=ALU.mult, op1=ALU.mult), reads=[f"xres{c}", "rstd"], writes=[f"h{c}"])
                        hreads = [f"h{c}" for c in range(KC)]

                        def win_mm(bk, slab, col0, W_):
                            def f(e):
                                ins = None
                                sv = slab[:, :]
                                for kc in range(KC):
                                    ins = e.matmul(bk[:], sv[:, kc * W_ + col0: kc * W_ + col0 + 128], hbuf[:, kc, :],
                                                   start=(kc == 0), stop=(kc == KC - 1))
                                return ins
                            return f

                        rg_lists = []
                        yflags = {c: {"first": None} for c in range(KC)}
                        for b in range(4):
                            cur[0] = []
                            rg_lists.append(cur[0])
                            si = nslot()
                            slab = slabs[si]
                            stok = f"slab{si}"

                            def ld_rg(e, slab=slab, l=l, b=b, ds=d_slab[si]):
                                e.dma_start(out=slab[:, 0:6144].rearrange("p (k n) -> p k n", k=KC),
                                            in_=win_s[l, :, :, b * 768:(b + 1) * 768]).then_inc(ds, 16)
                                e.dma_start(out=slab[:, 6144:7168].rearrange("p (g k n) -> p g k n", g=2, k=2),
                                            in_=wg_s[l, :, b, :, :, :]).then_inc(ds, 16)
                                return e.dma_start(out=slab[:, 7168:8192].rearrange("p (c j n) -> p c j n", c=2, j=4),
                                                   in_=dg_s[l, :, 2 * b:2 * b + 2, :, :]).then_inc(ds, 16)
                            A("sp", ld_rg, writes=[stok], dma=d_slab[si], ndma=3)
                            chs = (2 * b, 2 * b + 1)
                            for ci, c in enumerate(chs):
                                A("pool", lambda e, ci=ci, c=c, l=l: e.tensor_copy(out=xa_bf[:, ci, 0:3], in_=xkeep[:, l * 8 + c, :]),
                                       reads=[f"xkeep_{l}_{c}"], writes=[f"xa_halo{ci}"])
                                bk, bt = nb("rg")
                                A("pe", win_mm(bk, slab, 0 * 256 + ci * 128, 768), reads=hreads + [stok], writes=[bt])
                                A("act", lambda e, ci=ci, bk=bk: e.activation(out=xa_bf[:, ci, 3:3 + TT], in_=bk[:], func=ACT.Copy),
                                       reads=[bt], writes=[f"xa_bf{ci}"])
                                bk2, bt2 = nb("rg")

                                def convmm(e, ci=ci, bk2=bk2, slab=slab):
                                    ins = None
                                    for j in range(4):
                                        o = 7168 + (ci * 4 + j) * 128
                                        ins = e.matmul(bk2[:], slab[:, o:o + 128], xa_bf[:, ci, j:j + TT], start=(j == 0), stop=(j == 3))
                                    return ins
                                A("pe", convmm, reads=[f"xa_bf{ci}", f"xa_halo{ci}", stok], writes=[bt2], cost=0.9)
                                A("pool", lambda e, ci=ci, c=c, l=l: e.tensor_copy(out=xkeep[:, l * 8 + c, :], in_=xa_bf[:, ci, TT:TT + 3]),
                                       reads=[f"xa_bf{ci}"], writes=[f"xkeep_{l}_{c}"])
                                A("act", lambda e, ci=ci, c=c, l=l, bk2=bk2: e.activation(out=xc32[:, ci, :], in_=bk2[:], func=ACT.Identity,
                                                                                            bias=V(VROW["conv_b"] + l, c), scale=1.0),
                                       reads=[bt2], writes=[f"xc32_{ci}"])
                                A("pool", lambda e, ci=ci: e.tensor_copy(out=xcbf[:, ci, :], in_=xc32[:, ci, :]), reads=[f"xc32_{ci}"], writes=[f"xcbf{ci}"])
                            for ci, c in enumerate(chs):
                                for g in range(2):
                                    bk, bt = nb("rg")

                                    def gmm(e, ci=ci, g=g, bk=bk, slab=slab):
                                        ins = None
                                        for k2 in range(2):
                                            o = 6144 + (g * 2 + k2) * 256 + ci * 128
                                            ins = e.matmul(bk[:], slab[:, o:o + 128], xcbf[:, k2, :], start=(k2 == 0), stop=(k2 == 1))
                                        return ins
                                    A("pe", gmm, reads=["xcbf0", "xcbf1", stok], writes=[bt], cost=0.45)
                                    dst = rgA if g == 0 else rgI
                                    bsrc = brh if g == 0 else bih
                                    A("act", lambda e, ci=ci, c=c, l=l, bk=bk, dst=dst, bsrc=bsrc: e.activation(
                                        out=dst[:, ci, :], in_=bk[:], func=ACT.Tanh, bias=bsrc[:, c, l:l + 1], scale=0.5),
                                        reads=[bt], writes=[("rgA" if g == 0 else "rgI") + str(ci)])
                            for ci, c in enumerate(chs):
                                A("act", lambda e, ci=ci, c=c, l=l: e.activation(out=rgS[:, ci, :], in_=rgA[:, ci, :], func=ACT.Tanh,
                                                                                     bias=nchalf[:, c, l:l + 1], scale=nchalf[:, c, l:l + 1]),
                                       reads=[f"rgA{ci}"], writes=[f"rgS{ci}"])
                                A("act", lambda e, ci=ci, c=c, l=l: e.activation(out=rgA[:, ci, :], in_=rgA[:, ci, :], func=ACT.Exp,
                                                                                     bias=chalf[:, c, l:l + 1], scale=chalf[:, c, l:l + 1]),
                                       reads=[f"rgA{ci}"], writes=[f"rgA{ci}"])
                            for ci, c in enumerate(chs):
                                A("act", lambda e, ci=ci: e.activation(out=rgG[:, ci, :], in_=rgA[:, ci, :], func=ACT.Square),
                                  reads=[f"rgA{ci}"], writes=[f"rgG{ci}"])
                                A("dve", lambda e, ci=ci: e.scalar_tensor_tensor(out=rgS[:, ci, :], in0=rgG[:, ci, :], scalar=1.0, in1=rgS[:, ci, :],
                                                                                 op0=ALU.add, op1=ALU.mult),
                                  reads=[f"rgS{ci}", f"rgG{ci}"], writes=[f"rgS{ci}"])
                                A("act", lambda e, ci=ci: e.activation(out=rgS[:, ci, :], in_=rgS[:, ci, :], func=ACT.Sqrt, scale=0.25),
                                  reads=[f"rgS{ci}"], writes=[f"rgS{ci}"])
                                A("dve", lambda e, ci=ci: e.scalar_tensor_tensor(out=rgI[:, ci, :], in0=rgI[:, ci, :], scalar=1.0, in1=xc32[:, ci, :],
                                                                                      op0=ALU.add, op1=ALU.mult),
                                       reads=[f"rgI{ci}", f"xc32_{ci}"], writes=[f"rgI{ci}"])
                                A("dve", lambda e, ci=ci: e.tensor_tensor(out=rgI[:, ci, :], in0=rgI[:, ci, :], in1=rgS[:, ci, :], op=ALU.mult),
                                       reads=[f"rgI{ci}", f"rgS{ci}"], writes=[f"rgI{ci}"])
                                A("dve", lambda e, ci=ci, c=c, l=l: e.tensor_tensor_scan(out=xc32[:, ci, :], data0=rgA[:, ci, :], data1=rgI[:, ci, :],
                                                                                             initial=hcar[:, l * 8 + c:l * 8 + c + 1],
                                                                                             op0=ALU.mult, op1=ALU.add),
                                       reads=[f"rgA{ci}", f"rgI{ci}", f"hcar_{l}_{c}"], writes=[f"xc32_{ci}"], cost=1.3)
                                A("dve", lambda e, ci=ci, c=c, l=l: e.tensor_copy(out=hcar[:, l * 8 + c:l * 8 + c + 1], in_=xc32[:, ci, TT - 1:TT]),
                                       reads=[f"xc32_{ci}"], writes=[f"hcar_{l}_{c}"])
                            for ci, c in enumerate(chs):
                                bk, bt = nb("rg")
                                A("pe", win_mm(bk, slab, 1 * 256 + ci * 128, 768), reads=hreads + [stok], writes=[bt])
                                A("act", lambda e, ci=ci, bk=bk: e.activation(out=rgG[:, ci, :], in_=bk[:], func=ACT.Gelu_apprx_tanh),
                                       reads=[bt], writes=[f"rgG{ci}"])
                                bk2, bt2 = nb("rg")
                                A("pe", win_mm(bk2, slab, 2 * 256 + ci * 128, 768), reads=hreads + [stok], writes=[bt2])
                                A("act", lambda e, ci=ci, bk2=bk2: e.activation(out=rgS[:, ci, :], in_=bk2[:], func=ACT.Tanh, scale=0.5),
                                       reads=[bt2], writes=[f"rgS{ci}"])
                                A("dve", lambda e, ci=ci: e.tensor_tensor(out=rgG[:, ci, :], in0=rgG[:, ci, :], in1=xc32[:, ci, :], op=ALU.mult),
                                       reads=[f"rgG{ci}", f"xc32_{ci}"], writes=[f"rgG{ci}"])
                                A("dve", lambda e, ci=ci: e.scalar_tensor_tensor(out=rgG[:, ci, :], in0=rgS[:, ci, :], scalar=1.0, in1=rgG[:, ci, :],
                                                                                 op0=ALU.add, op1=ALU.mult),
                                  reads=[f"rgS{ci}", f"rgG{ci}"], writes=[f"rgG{ci}"])

                                def yrg(e, ci=ci, c=c, fl=yflags[c]):
                                    if fl["first"] == "rg":
                                        return e.tensor_copy(out=ysum[:, c, :], in_=rgG[:, ci, :])
                                    return e.tensor_tensor(out=ysum[:, c, :], in0=ysum[:, c, :], in1=rgG[:, ci, :], op=ALU.add)
                                A("dve", yrg, reads=[f"rgG{ci}", f"ysum{c}"], writes=[f"ysum{c}"], tag=("rg", c, yflags[c]))
                            cur[0] = None

                        hg_lists = []
                        for hh in range(8):
                            cur[0] = []
                            hg_lists.append(cur[0])
                            si = nslot("small")
                            slab = slabs[si]
                            stok = f"slab{si}"

                            def ld_hg(e, slab=slab, l=l, hh=hh, ds=d_slab[si]):
                                return e.dma_start(out=slab[:, 0:5120].rearrange("p (k n) -> p k n", k=KC),
                                                   in_=win_s[l, :, :, 3072 + hh * 640:3072 + (hh + 1) * 640]).then_inc(ds, 16)
                            A("sp", ld_hg, writes=[stok], dma=d_slab[si])
                            sk = f"Sc_{l}_{hh}"
                            bk, bt = nb("hg")
                            A("pe", win_mm(bk, slab, 0, 640), reads=hreads + [stok], writes=[bt])
                            A("act", lambda e, bk=bk: e.activation(out=qs[:], in_=bk[:], func=ACT.Silu), reads=[bt], writes=["qs"])
                            bk, bt = nb("hg")
                            A("pe", win_mm(bk, slab, 128, 640), reads=hreads + [stok], writes=[bt])
                            A("act", lambda e, bk=bk: e.activation(out=fz[:], in_=bk[:], func=ACT.Tanh, scale=0.5), reads=[bt], writes=["fz"])
                            bk, bt = nb("hg")
                            A("pe", win_mm(bk, slab, 256, 640), reads=hreads + [stok], writes=[bt])
                            A("act", lambda e, bk=bk: e.activation(out=vF[:], in_=bk[:], func=ACT.Copy), reads=[bt], writes=["vF"])
                            bk, bt = nb("hg")
                            A("pe", win_mm(bk, slab, 384, 640), reads=hreads + [stok], writes=[bt])
                            A("act", lambda e, bk=bk: e.activation(out=sgb[:], in_=bk[:], func=ACT.Silu), reads=[bt], writes=["sgb"])
                            bk, bt = nb("hg")
                            A("pe", win_mm(bk, slab, 512, 640), reads=hreads + [stok], writes=[bt])
                            A("act", lambda e, bk=bk: e.activation(out=tmb[:], in_=bk[:], func=ACT.Tanh, scale=0.5), reads=[bt], writes=["tmb"])
                            A("dve", lambda e, l=l, hh=hh: e.tensor_scalar(out=kk[:], in0=fz[:], scalar1=kbn[:, hh, l:l + 1], scalar2=kbp[:, hh, l:l + 1],
                                                                                op0=ALU.mult, op1=ALU.add), reads=["fz"], writes=["kk"])
                            A("dve", lambda e, l=l, hh=hh: e.tensor_scalar(out=fz[:], in0=fz[:], scalar1=kbp[:, hh, l:l + 1], scalar2=kb1[:, hh, l:l + 1],
                                                                                op0=ALU.mult, op1=ALU.add), reads=["fz", "kk"], writes=["fz"])
                            fv = fz[:].rearrange("p (c i) -> p c i", i=32)
                            fmv = fm[:].rearrange("p (c i) -> p c i", i=32)
                            A("pool", lambda e: e.tensor_copy(out=fmv[:, :, 0:1], in_=fv[:, :, 0:1]), reads=["fz"], writes=["fm"])
                            A("dve", lambda e: e.tensor_tensor_scan(out=Pp[:], data0=fz[:], data1=fm[:], initial=1.0, op0=ALU.mult, op1=ALU.max),
                                   reads=["fz", "fm"], writes=["Pp"], cost=1.3)
                            A("dve", lambda e: e.reciprocal(out=Rr[:], in_=Pp[:]), reads=["Pp"], writes=["Rr"], cost=3.4)
                            A("dve", lambda e: e.tensor_tensor(out=qt[:], in0=qs[:], in1=Pp[:], op=ALU.mult), reads=["qs", "Pp"], writes=["qt"])
                            A("dve", lambda e: e.tensor_tensor(out=kt[:], in0=kk[:], in1=Rr[:], op=ALU.mult), reads=["kk", "Rr"], writes=["kt"])
                            bkT, btT = nb("hg")
                            kTp = bkT[:].bitcast(BF16)

                            def trk(e, kTp=kTp):
                                ins = None
                                for c in range(8):
                                    ins = e.transpose(kTp[0:32, c * 128:(c + 1) * 128], kt[:, c * 32:(c + 1) * 32], identbf[:])
                                return ins

                            def trk2(e, kTp=kTp):
                                ins = None
                                for c in range(8, 16):
                                    ins = e.transpose(kTp[0:32, (c - 8) * 128:(c - 7) * 128], kt[:, c * 32:(c + 1) * 32], identbf[:])
                                return ins
                            A("pe", trk, reads=["kt", "identbf"], writes=[btT], cost=0.6)
                            A("act", lambda e, kTp=kTp: e.activation(out=kT32[:, 0:8, :], in_=kTp[0:32, :].rearrange("p (c n) -> p c n", c=8), func=ACT.Copy),
                                   reads=[btT], writes=["kT32a"])
                            bkT2, btT2 = nb("hg")
                            kTp2 = bkT2[:].bitcast(BF16)
                            A("pe", lambda e, kTp2=kTp2: trk2(e, kTp2), reads=["kt", "identbf"], writes=[btT2], cost=0.6)
                            A("act", lambda e, kTp2=kTp2: e.activation(out=kT32[:, 8:16, :], in_=kTp2[0:32, :].rearrange("p (c n) -> p c n", c=8), func=ACT.Copy),
                                   reads=[btT2], writes=["kT32b"])
                            bkV, btV = nb("hg")
                            vTp = bkV[:].bitcast(BF16)

                            def trv(e, vTp=vTp, lo=0):
                                ins = None
                                for c in range(lo, lo + 8):
                                    ins = e.transpose(vTp[0:32, (c - lo) * 128:(c - lo + 1) * 128], vF[:, c * 32:(c + 1) * 32], identbf[:])
                                return ins
                            A("pe", lambda e, vTp=vTp: trv(e, vTp, 0), reads=["vF", "identbf"], writes=[btV], cost=0.6)
                            A("dve", lambda e, vTp=vTp: e.tensor_copy(out=v32[:, 0:8, :], in_=vTp[0:32, :].rearrange("p (c n) -> p c n", c=8)),
                                   reads=[btV], writes=["v32a"])
                            bkV2, btV2 = nb("hg")
                            vTp2 = bkV2[:].bitcast(BF16)
                            A("pe", lambda e, vTp2=vTp2: trv(e, vTp2, 8), reads=["vF", "identbf"], writes=[btV2], cost=0.6)
                            A("dve", lambda e, vTp2=vTp2: e.tensor_copy(out=v32[:, 8:16, :], in_=vTp2[0:32, :].rearrange("p (c n) -> p c n", c=8)),
                                   reads=[btV2], writes=["v32b"])
                            bkS, btS = nb("hg")

                            def scmm(e, bkS=bkS):
                                ins = None
                                for c in range(NCH):
                                    ins = e.matmul(bkS[0:32, c * 32:(c + 1) * 32], kt[:, c * 32:(c + 1) * 32], qt[:, c * 32:(c + 1) * 32], start=True, stop=True)
                                return ins
                            A("pe", scmm, reads=["kt", "qt"], writes=[btS], cost=1.1)
                            A("dve", lambda e, bkS=bkS: e.tensor_tensor(out=Sm32[:], in0=bkS[0:32, :].rearrange("p (c t) -> p c t", c=NCH), in1=mask32[:], op=ALU.mult),
                                   reads=[btS], writes=["Sm32"])
                            abanks = []
                            for q4 in range(4):
                                bkA, btA = nb("hg")
                                abanks.append((bkA, btA))

                                def amm(e, bkA=bkA, q4=q4):
                                    ins = None
                                    for cc in range(4):
                                        c = q4 * 4 + cc
                                        ins = e.matmul(bkA[:, cc * 128:(cc + 1) * 128], kT32[:, c, :], v32[:, c, :], start=True, stop=True)
                                    return ins
                                A("pe", amm, reads=["kT32a", "kT32b", "v32a", "v32b"], writes=[btA], cost=0.5)
                            ecol = Pp[:].rearrange("p (c i) -> p c i", i=32)
                            qtv = qt[:].rearrange("p (c i) -> p c i", i=32)
                            qt2v = qt2[:].rearrange("p (c i) -> p c i", i=32)
                            A("dve", lambda e: e.tensor_tensor(out=qt2v[:, 1:NCH, :], in0=qtv[:, 1:NCH, :],
                                                               in1=ecol[:, 0:NCH - 1, 31:32].to_broadcast([128, NCH - 1, 32]), op=ALU.mult),
                              reads=["qt", "Pp"], writes=["qt2"])
                            bkO, btO = nb("hg")
                            for c in range(NCH):
                                bkA, btA = abanks[c // 4]
                                a_ap = bkA[:, (c % 4) * 128:(c % 4 + 1) * 128]
                                if c == 0:
                                    A("dve", lambda e, a_ap=a_ap, l=l, hh=hh: e.tensor_tensor(out=Uall[:, 0, :], in0=Sc32[:, l * 8 + hh, :], in1=a_ap, op=ALU.add),
                                      reads=[btA, f"Sc32_{l}_{hh}"], writes=["U0"], cost=0.48)
                                else:
                                    A("dve", lambda e, a_ap=a_ap, c=c: e.scalar_tensor_tensor(out=Uall[:, c, :], in0=Uall[:, c - 1, :], scalar=ecol[:, c - 1, 31:32],
                                                                                             in1=a_ap, op0=ALU.mult, op1=ALU.add),
                                      reads=[btA, f"U{c - 1}", "Pp"], writes=[f"U{c}"], cost=0.48)
                                if c < NCH - 1:
                                    A("act", lambda e, c=c: e.activation(out=Sbf[:, c, :], in_=Uall[:, c, :], func=ACT.Copy), reads=[f"U{c}"], writes=[f"Ub{c}"], cost=0.25)

                                def omm(e, bkO=bkO, l=l, hh=hh, c=c):
                                    e.matmul(bkO[:, c * 32:(c + 1) * 32], v32[:, c, :], Sm32[:, c, :], start=True, stop=False)
                                    if c == 0:
                                        return e.matmul(bkO[:, 0:32], Sc16[:, l * 8 + hh, :], qt[:, 0:32], start=False, stop=True)
                                    return e.matmul(bkO[:, c * 32:(c + 1) * 32], Sbf[:, c - 1, :], qt2[:, c * 32:(c + 1) * 32], start=False, stop=True)
                                A("pe", omm, reads=["v32a", "v32b", "Sm32"] + (["qt", f"Sc16_{l}_{hh}"] if c == 0 else ["qt2", f"Ub{c - 1}"]), writes=[btO], cost=0.3)
                            A("dve", lambda e, l=l, hh=hh: e.tensor_scalar(out=Sc32[:, l * 8 + hh, :], in0=Uall[:, NCH - 1, :], scalar1=ecol[:, NCH - 1, 31:32], scalar2=None,
                                                                           op0=ALU.mult), reads=[f"U{NCH - 1}", "Pp"], writes=[f"Sc32_{l}_{hh}"])
                            A("act", lambda e, l=l, hh=hh: e.activation(out=Sc16[:, l * 8 + hh, :], in_=Sc32[:, l * 8 + hh, :], func=ACT.Copy),
                              reads=[f"Sc32_{l}_{hh}"], writes=[f"Sc16_{l}_{hh}"], cost=0.25)
                            A("act", lambda e, bkO=bkO: e.activation(out=o2[:], in_=bkO[:], func=ACT.Square), reads=[btO], writes=["o2"])
                            bkM, btM = nb("hg")
                            A("pe", lambda e, bkM=bkM: e.matmul(bkM[:], onesH[:], o2[:], start=True, stop=True), reads=["o2"], writes=[btM], cost=0.25)
                            A("act", lambda e, bkM=bkM: e.activation(out=msq[:], in_=bkM[:], func=ACT.Ln, bias=epsb[:, 0:1], scale=1.0), reads=[btM], writes=["qs"])
                            A("act", lambda e: e.activation(out=msq[:], in_=msq[:], func=ACT.Exp, scale=-0.5), reads=["qs"], writes=["qs"])
                            A("dve", lambda e, bkO=bkO, l=l: e.scalar_tensor_tensor(out=onb[:], in0=bkO[:], scalar=vecT[:, 0, VROW["hg_norm"] + l:VROW["hg_norm"] + l + 1],
                                                                                         in1=msq[:], op0=ALU.mult, op1=ALU.mult),
                                   reads=[btO, "qs"], writes=["Rr"])
                            A("dve", lambda e: e.tensor_tensor(out=onb[:], in0=onb[:], in1=sgb[:], op=ALU.mult), reads=["Rr", "sgb"], writes=["Rr"])
                            A("dve", lambda e: e.scalar_tensor_tensor(out=onb[:], in0=tmb[:], scalar=1.0, in1=onb[:], op0=ALU.add, op1=ALU.mult),
                                   reads=["Rr", "tmb"], writes=["Rr"])
                            def yhg(e, hh=hh, fl=yflags[hh]):
                                if fl["first"] == "hg":
                                    return e.tensor_copy(out=ysum[:, hh, :], in_=onb[:])
                                return e.tensor_tensor(out=ysum[:, hh, :], in0=ysum[:, hh, :], in1=onb[:], op=ALU.add)
                            A("dve", yhg, reads=["Rr", f"ysum{hh}"], writes=[f"ysum{hh}"], tag=("hg", hh, yflags[hh]))
                            cur[0] = None
                        for b in range(4):
                            flush_merged(rg_lists[b], hg_lists[2 * b] + hg_lists[2 * b + 1])

                        si = nslot()
                        slab = slabs[si]
                        stok = f"slab{si}"
                        A("sp", lambda e, slab=slab, l=l, ds=d_slab[si]: e.dma_start(
                            out=slab[:, :].rearrange("p (k n) -> p k n", k=KC), in_=wo_s[l]).then_inc(ds, 16), writes=[stok], dma=d_slab[si])
                        yreads = [f"ysum{c}" for c in range(KC)]
                        for m in range(KC):
                            bk, bt = nb()

                            def womm(e, bk=bk, slab=slab, m=m):
                                ins = None
                                for kc in range(KC):
                                    ins = e.matmul(bk[:], slab[:, kc * D + m * 128: kc * D + (m + 1) * 128], ysum[:, kc, :], start=(kc == 0), stop=(kc == KC - 1))
                                return ins
                            A("pe", womm, reads=yreads + [stok], writes=[bt])
                            A("dve", lambda e, bk=bk, m=m: e.scalar_tensor_tensor(out=xres[:, m, :], in0=bk[:], scalar=0.5, in1=xres[:, m, :], op0=ALU.mult, op1=ALU.add),
                                   reads=[bt, f"xres{m}"], writes=[f"xres{m}"])

                        rms_rstd([(xres[:, c, :], f"xres{c}") for c in range(KC)], None, EPS, onesD, rstd[:], "rstd")
                        for c in range(KC):
                            A("dve", lambda e, c=c, l=l: e.scalar_tensor_tensor(
                                out=hbuf[:, c, :], in0=xres[:, c, :], scalar=V(VROW["norm_mlp"] + l, c), in1=rstd[:],
                                op0=ALU.mult, op1=ALU.mult), reads=[f"xres{c}", "rstd"], writes=[f"h{c}"])
                        for sg in range(2):
                            uslabs = []
                            for half in range(2):
                                fg = sg * 2 + half
                                si = nslot()
                                A("sp", lambda e, slab=slabs[si], l=l, fg=fg, ds=d_slab[si]: e.dma_start(
                                    out=slab[:, :].rearrange("p (k n) -> p k n", k=KC), in_=wu_s[l, :, :, fg * 1024:(fg + 1) * 1024]).then_inc(ds, 16),
                                    writes=[f"slab{si}"], dma=d_slab[si])
                                uslabs.append((slabs[si], f"slab{si}"))
                                for f8 in range(8):
                                    f = half * 8 + f8
                                    bk, bt = nb()

                                    def upmm(e, bk=bk, slab=slabs[si], f8=f8):
                                        ins = None
                                        for kc in range(KC):
                                            ins = e.matmul(bk[:], slab[:, kc * 1024 + f8 * 128: kc * 1024 + (f8 + 1) * 128], hbuf[:, kc, :], start=(kc == 0), stop=(kc == KC - 1))
                                        return ins
                                    A("pe", upmm, reads=hreads + [f"slab{si}"], writes=[bt])
                                    r_ = rl[f % 2]
                                    A("act", lambda e, bk=bk, r_=r_: e.activation(out=r_[:], in_=bk[:], func=ACT.Relu), reads=[bt], writes=[f"rl{f % 2}"])
                                    A("pool", lambda e, r_=r_, f=f: e.tensor_tensor(out=hidden[:, f, :], in0=r_[:], in1=r_[:], op=ALU.mult),
                                           reads=[f"rl{f % 2}"], writes=[f"hid{f}", "iob"])
                            dslabs = []
                            for mh in range(2):
                                si = nslot()
                                A("sp", lambda e, slab=slabs[si], l=l, sg=sg, mh=mh, ds=d_slab[si]: e.dma_start(
                                    out=slab[:, :].rearrange("p (k n) -> p k n", k=16), in_=wd_s[l, :, sg * 16:(sg + 1) * 16, mh * 512:(mh + 1) * 512]).then_inc(ds, 16),
                                    writes=[f"slab{si}"], dma=d_slab[si])
                                dslabs.append((slabs[si], f"slab{si}"))
                            for m in range(KC):
                                bk, bt = nb()

                                def dnmm(e, bk=bk, m=m, dslabs=dslabs):
                                    ins = None
                                    slab = dslabs[m // 4][0]
                                    mm_ = m % 4
                                    for f in range(16):
                                        ins = e.matmul(bk[:], slab[:, f * 512 + mm_ * 128: f * 512 + (mm_ + 1) * 128], hidden[:, f, :], start=(f == 0), stop=(f == 15))
                                    return ins
                                A("pe", dnmm, reads=[f"hid{f}" for f in range(16)] + [dslabs[m // 4][1]], writes=[bt])
                                A("dve", lambda e, bk=bk, m=m: e.tensor_tensor(out=xres[:, m, :], in0=xres[:, m, :], in1=bk[:], op=ALU.add),
                                       reads=[bt, f"xres{m}"], writes=[f"xres{m}"])

                    rms_rstd([(xres[:, c, :], f"xres{c}") for c in range(KC)], None, EPS, onesD, rstd[:], "rstd")
                    for c in range(KC):
                        A("dve", lambda e, c=c: e.scalar_tensor_tensor(
                            out=xres[:, c, :], in0=xres[:, c, :], scalar=V(VROW["norm_final"], c), in1=rstd[:],
                            op0=ALU.mult, op1=ALU.mult), reads=[f"xres{c}", "rstd"], writes=[f"xres{c}"])
                    for b in range(4):
                        for cg in range(2):
                            bk, bt = nb()

                            def tro(e, bk=bk, b=b, cg=cg):
                                ins = None
                                for cc in range(4):
                                    c = cg * 4 + cc
                                    ins = e.transpose(bk[:, cc * 128:(cc + 1) * 128], xres[:, c, b * 128:(b + 1) * 128], ident32[:])
                                return ins
                            A("pe", tro, reads=[f"xres{c}" for c in range(cg * 4, cg * 4 + 4)], writes=[bt])
                            if cg == 0:
                                A("act", lambda e, bk=bk, b=b, cg=cg: e.activation(out=iob[:, b, cg * 512:(cg + 1) * 512], in_=bk[:], func=ACT.Copy),
                                       reads=[bt], writes=[f"io_{b}_{cg}"])
                            else:
                                A("dve", lambda e, bk=bk, b=b, cg=cg: e.tensor_copy(out=iob[:, b, cg * 512:(cg + 1) * 512], in_=bk[:]),
                                       reads=[bt], writes=[f"io_{b}_{cg}"])
                    A("sp", lambda e, row0=row0: e.dma_start(out=out_d[row0:row0 + TT, :].rearrange("(b p) d -> p b d", p=128), in_=iob[:]).then_inc(d_o, 16),
                           reads=["iob"] + [f"hid{f}" for f in range(16)] + [f"io_{b}_{cg}" for b in range(4) for cg in range(2)], writes=["OUT"], dma=d_o)
            A("sp", lambda e: e.nop(), reads=["OUT"])
            T2.finish(block, sems_main(nc, es, sems))
            stats1 = T2.stats
    build_program.stats = (stats0, stats1)
    return nc


def sems_main(nc, es, sems):
    return {e: es.enter_context(nc.semaphore("m_" + e)) for e in ENGS}


_CACHE = {}


def kernel(x, lb_logits, norm_mix, w_in, conv_w, conv_b, w_r, b_r, w_i, b_i, lam, hg_norm, w_out, norm_mlp, w_up, w_down, norm_final):
    x = np.asarray(x)
    B, T, _ = x.shape
    NL = int(np.asarray(w_in).shape[0])
    ncores = 8
    assert B % ncores == 0
    NSEQ = B // ncores
    key = (NSEQ, T, NL)
    if key not in _CACHE:
        _CACHE[key] = build_program(NSEQ, T, NL)
    nc = _CACHE[key]
    f32 = lambda a: np.ascontiguousarray(np.asarray(a, dtype=np.float32))
    shared = {
        "lb_logits": f32(lb_logits), "norm_mix": f32(norm_mix), "w_in": f32(w_in), "conv_w": f32(conv_w),
        "conv_b": f32(conv_b), "w_r": f32(w_r), "b_r": f32(b_r), "w_i": f32(w_i), "b_i": f32(b_i),
        "lam": f32(lam), "hg_norm": f32(hg_norm), "w_out": f32(w_out), "norm_mlp": f32(norm_mlp),
        "w_up": f32(w_up), "w_down": f32(w_down), "norm_final": f32(norm_final).reshape(1, D),
    }
    xs = f32(x).reshape(ncores, NSEQ * T, D)
    in_maps = []
    for i in range(ncores):
        m = dict(shared)
        m["x"] = xs[i]
        in_maps.append(m)
    res = run_bass_kernel_spmd(nc, in_maps, core_ids=list(range(ncores)))
    out = np.stack([np.asarray(r["out"]) for r in res.results], axis=0)
    return out.reshape(B, T, D).astype(np.float32)
```
